# Optimizing a Trainium2 kernel written in Bass

```python
import jax, jax.numpy as jnp
from jax import lax
import numpy as np

D_MODEL = 2048
BATCH = 8
SEQ = 2048
DEPTH = 2

CTX_LEN = 256
GRID_W = 64
D_MIX = D_MODEL
HEAD_DIM = 128
ATTN_W = D_MIX // 2
N_HEADS = ATTN_W // HEAD_DIM
N_KV_HEADS = 2
GQA_GROUP = N_HEADS // N_KV_HEADS
KV_W = N_KV_HEADS * HEAD_DIM
POOL_W = D_MIX // 4
POOL_WINDOWS = (2, 4, 8, 16)
POOL_GROUPS = len(POOL_WINDOWS)
POOL_GC = POOL_W // POOL_GROUPS
FOURIER_W = D_MIX - ATTN_W - POOL_W
FOURIER_GROUPS = 4
FOURIER_GC = FOURIER_W // FOURIER_GROUPS
OFF_Q = 0
OFF_K = OFF_Q + ATTN_W
OFF_V = OFF_K + KV_W
OFF_POOL = OFF_V + KV_W
OFF_FOURIER = OFF_POOL + POOL_W
OFF_GATE = OFF_FOURIER + FOURIER_W
IN_W = OFF_GATE + D_MIX
Q_BLOCK = 128
ROPE_THETA = 10000.0
AXIS_ROT = HEAD_DIM // 2
EPS = 1e-6

kernel_name = "hybrid_gqa_pool_fourier_prefix_dit"


def rmsnorm(x, g):
    xf = x.astype(jnp.float32)
    y = xf * lax.rsqrt(jnp.mean(xf * xf, axis=-1, keepdims=True) + EPS)
    return (y * g.astype(jnp.float32)).astype(x.dtype)


def axial_rope_tables(n):
    rows_count = n // GRID_W
    row = jnp.broadcast_to(jnp.arange(rows_count, dtype=jnp.float32)[:, None], (rows_count, GRID_W)).reshape(-1)
    col = jnp.broadcast_to(jnp.arange(GRID_W, dtype=jnp.float32)[None, :], (rows_count, GRID_W)).reshape(-1)
    inv = ROPE_THETA ** (-jnp.arange(0, AXIS_ROT, 2, dtype=jnp.float32) / AXIS_ROT)
    ang = jnp.stack([row[:, None] * inv, col[:, None] * inv], axis=1)
    return jnp.cos(ang), jnp.sin(ang)


def apply_axial_rope(x, cos, sin):
    B, N, H, _ = x.shape
    xr = x.astype(jnp.float32).reshape(B, N, H, 2, 2, AXIS_ROT // 2)
    x1, x2 = xr[..., 0, :], xr[..., 1, :]
    c = cos[None, :, None]
    s = sin[None, :, None]
    out = jnp.stack([x1 * c - x2 * s, x2 * c + x1 * s], axis=-2)
    return out.reshape(B, N, H, HEAD_DIM).astype(x.dtype)


def attend(qg, keys, vals):
    s = jnp.einsum('bqkgd,bskd->bkgqs', qg, keys, preferred_element_type=jnp.float32) * (HEAD_DIM ** -0.5)
    p = jax.nn.softmax(s, axis=-1).astype(vals.dtype)
    return jnp.einsum('bkgqs,bskd->bqkgd', p, vals)


def latent_attention(q, k, v, kc, vc):
    B, N = q.shape[:2]
    keys = jnp.concatenate([kc, k], axis=1)
    vals = jnp.concatenate([vc, v], axis=1)
    nb = N // Q_BLOCK
    qb = q.reshape(B, nb, Q_BLOCK, N_KV_HEADS, GQA_GROUP, HEAD_DIM).transpose(1, 0, 2, 3, 4, 5)
    o = lax.map(lambda qi: attend(qi, keys, vals), qb)
    return o.transpose(1, 0, 2, 3, 4, 5).reshape(B, N, ATTN_W)


def context_attention(qc, kc, vc):
    B, C = qc.shape[:2]
    qg = qc.reshape(B, C, N_KV_HEADS, GQA_GROUP, HEAD_DIM)
    return attend(qg, kc, vc).reshape(B, C, ATTN_W)


def multiscale_pool(u, pool_w, pool_scale):
    B, N, _ = u.shape
    uf = u.astype(jnp.float32).reshape(B, N, POOL_GROUPS, POOL_GC)
    cs = jnp.concatenate([jnp.zeros((B, 1, POOL_GROUPS, POOL_GC), jnp.float32), jnp.cumsum(uf, axis=1)], axis=1)
    t = jnp.arange(N, dtype=jnp.int32)
    outs = []
    for gi, win in enumerate(POOL_WINDOWS):
        lo = jnp.clip(t - win // 2, 0, N - 1)
        hi = jnp.clip(t + (win - win // 2) - 1, 0, N - 1)
        cnt = (hi - lo + 1).astype(jnp.float32)
        csg = cs[:, :, gi]
        win_sum = jnp.take(csg, hi + 1, axis=1) - jnp.take(csg, lo, axis=1)
        outs.append(win_sum / cnt[None, :, None] - uf[:, :, gi])
    pooled = jnp.stack(outs, axis=2).astype(u.dtype)
    y = jnp.einsum('bngc,gcd->bngd', pooled, pool_w).reshape(B, N, POOL_W)
    return y * pool_scale


def fourier_mix(u, fourier_w):
    B, N, _ = u.shape
    uf = u.astype(jnp.float32).reshape(B, N, FOURIER_GROUPS, FOURIER_GC)
    f = jnp.fft.fft2(uf, axes=(1, 3), norm='ortho').real.astype(u.dtype)
    return jnp.einsum('bngc,gcd->bngd', f, fourier_w).reshape(B, N, FOURIER_W)


def split_proj(p):
    return (p[..., OFF_Q:OFF_K], p[..., OFF_K:OFF_V], p[..., OFF_V:OFF_POOL],
            p[..., OFF_POOL:OFF_FOURIER], p[..., OFF_FOURIER:OFF_GATE], p[..., OFF_GATE:])


def merge_branches(att, u_pool, u_four, g, pool_w, pool_scale, fourier_w, w_out):
    mixed = jnp.concatenate([att, multiscale_pool(u_pool, pool_w, pool_scale), fourier_mix(u_four, fourier_w)], axis=-1)
    return (mixed * jax.nn.silu(g)) @ w_out


def setup_inputs(seed: int = 0) -> dict:
    key = jax.random.key(seed)
    ks = jax.random.split(key, 20)
    f32 = jnp.float32
    nrm = lambda k, shape: jax.random.normal(k, shape, f32)
    return {
        "x": nrm(ks[0], (BATCH, SEQ, D_MODEL)),
        "c": nrm(ks[1], (BATCH, D_MODEL)),
        "ctx": nrm(ks[2], (BATCH, CTX_LEN, D_MODEL)),
        "c_ctx": nrm(ks[3], (D_MODEL,)),
        "ada_w": nrm(ks[4], (DEPTH, D_MODEL, 3 * D_MODEL)) * (0.5 * D_MODEL ** -0.5),
        "ada_b": nrm(ks[5], (DEPTH, 3 * D_MODEL)) * 0.01,
        "norm_g": 1.0 + 0.02 * nrm(ks[6], (DEPTH, D_MODEL)),
        "w_in": nrm(ks[7], (DEPTH, D_MODEL, IN_W)) * (D_MODEL ** -0.5),
        "q_norm_g": 1.0 + 0.02 * nrm(ks[8], (DEPTH, HEAD_DIM)),
        "k_norm_g": 1.0 + 0.02 * nrm(ks[9], (DEPTH, HEAD_DIM)),
        "pool_w": nrm(ks[10], (DEPTH, POOL_GROUPS, POOL_GC, POOL_GC)) * (POOL_GC ** -0.5),
        "pool_scale": 1.0 + 0.02 * nrm(ks[11], (DEPTH, POOL_W)),
        "fourier_w": nrm(ks[12], (DEPTH, FOURIER_GROUPS, FOURIER_GC, FOURIER_GC)) * (FOURIER_GC ** -0.5),
        "w_out": nrm(ks[13], (DEPTH, D_MIX, D_MODEL)) * (D_MIX ** -0.5),
        "final_norm_g": 1.0 + 0.02 * nrm(ks[14], (D_MODEL,)),
    }


def reference(x, c, ctx, c_ctx, ada_w, ada_b, norm_g, w_in, q_norm_g, k_norm_g,
              pool_w, pool_scale, fourier_w, w_out, final_norm_g):
    B, N, _ = x.shape
    C = ctx.shape[1]
    cos, sin = axial_rope_tables(N)
    xc = ctx
    for l in range(DEPTH):
        last = l == DEPTH - 1
        shift, scale, gate = jnp.split(jax.nn.silu(c) @ ada_w[l] + ada_b[l], 3, axis=-1)
        shift_c, scale_c, gate_c = jnp.split(jax.nn.silu(c_ctx) @ ada_w[l] + ada_b[l], 3, axis=-1)
        h = rmsnorm(x, norm_g[l]) * (1.0 + scale[:, None]) + shift[:, None]
        hc = rmsnorm(xc, norm_g[l]) * (1.0 + scale_c) + shift_c

        if last:
            pkv = hc @ w_in[l][:, OFF_K:OFF_POOL]
            kc, vc = pkv[..., :KV_W], pkv[..., KV_W:]
        else:
            qc, kc, vc, upc, ufc, gc = split_proj(hc @ w_in[l])
        kc = rmsnorm(kc.reshape(B, C, N_KV_HEADS, HEAD_DIM), k_norm_g[l])
        vc = vc.reshape(B, C, N_KV_HEADS, HEAD_DIM)

        q, k, v, up, uf, g = split_proj(h @ w_in[l])
        q = apply_axial_rope(rmsnorm(q.reshape(B, N, N_HEADS, HEAD_DIM), q_norm_g[l]), cos, sin)
        k = apply_axial_rope(rmsnorm(k.reshape(B, N, N_KV_HEADS, HEAD_DIM), k_norm_g[l]), cos, sin)
        v = v.reshape(B, N, N_KV_HEADS, HEAD_DIM)
        att = latent_attention(q, k, v, kc, vc)
        out = merge_branches(att, up, uf, g, pool_w[l], pool_scale[l], fourier_w[l], w_out[l])
        x_new = x + gate[:, None] * out

        if not last:
            qc = rmsnorm(qc.reshape(B, C, N_HEADS, HEAD_DIM), q_norm_g[l])
            attc = context_attention(qc, kc, vc)
            outc = merge_branches(attc, upc, ufc, gc, pool_w[l], pool_scale[l], fourier_w[l], w_out[l])
            xc = xc + gate_c * outc
        x = x_new
    return rmsnorm(x, final_norm_g)
```

```python
import math
from contextlib import ExitStack
import numpy as np
import ml_dtypes
import concourse.bass as bass
import concourse.mybir as mybir
from concourse.bass_utils import run_bass_kernel_spmd

F32 = mybir.dt.float32
BF16 = mybir.dt.bfloat16
AF = mybir.ActivationFunctionType
ALU = mybir.AluOpType

D = 2048
NLAT = 2048
NCTX = 256
NTOK = NLAT + NCTX
DEPTH = 2
INW = 4608
OFF_Q, OFF_K, OFF_V, OFF_POOL, OFF_FOUR, OFF_GATE = 0, 1024, 1280, 1536, 2048, 2560
EPS = 1e-6
import os
KVAR = int(os.environ.get('KVAR', '0'))
GR = 512
SB_BYTES = 207 * 1024


class Sched:
    ENGS = ("pe", "act", "dve", "pool", "sp")

    def __init__(self):
        self.streams = {e: [] for e in self.ENGS}
        self.lastw = {}
        self.readers = {}
        self.dma_slots = {}

    @staticmethod
    def keys(ap):
        if isinstance(ap, tuple):
            return [ap]
        if type(ap.tensor).__name__.startswith("DRam"):
            return []
        es = mybir.dt.size(ap.dtype)
        dims = list(ap.ap)[1:]
        lo = ap.offset * es
        ext = 1
        for st, cnt in dims:
            ext += abs(st) * (cnt - 1)
        hi = lo + ext * es
        nm = ap.tensor.name
        if nm.startswith("pb"):
            return [(nm, 0)]
        return [(nm, g) for g in range(lo // GR, (hi - 1) // GR + 1)]

    def op(self, eng, fn, R=(), W=(), dma_slot=None):
        idx = len(self.streams[eng])
        me = (eng, idx)
        deps = set()
        rk = [k for a in R for k in self.keys(a)]
        wk = [k for a in W for k in self.keys(a)]
        for k in rk:
            w = self.lastw.get(k)
            if w is not None:
                deps.add(w)
            if eng != "pe" and k[0].startswith("pb"):
                for re_, r in self.readers.get(k, {}).items():
                    if re_ != eng:
                        deps.add(r)
        for k in wk:
            w = self.lastw.get(k)
            if w is not None:
                deps.add(w)
            for r in self.readers.get(k, {}).values():
                deps.add(r)
        deps.discard(me)
        dma_need = {}
        for (de, di) in deps:
            sl = self.streams[de][di]["dma_slot"]
            if sl is not None:
                dma_need["dma_" + sl] = 16 * self.dma_slots[sl]
        ins = dict(eng=eng, idx=idx, fn=fn, deps=deps, dma_slot=dma_slot, dma_val=None, signal=False, ticket=None,
                   dma_need=dma_need)
        if dma_slot is not None:
            c = self.dma_slots.get(dma_slot, 0) + 1
            self.dma_slots[dma_slot] = c
            ins["dma_val"] = 16 * c
        self.streams[eng].append(ins)
        for k in wk:
            self.lastw[k] = me
            self.readers[k] = {}
        for k in rk:
            self.readers.setdefault(k, {})[eng if dma_slot is None else me] = me
        return me

    def emit(self, nc, stack):
        for e in self.ENGS:
            for ins in self.streams[e]:
                for (de, di) in ins["deps"]:
                    d = self.streams[de][di]
                    if d["dma_slot"] is None and (de != e or e != "pe"):
                        d["signal"] = True
        sems = {}
        for e in self.ENGS:
            sems[e] = stack.enter_context(nc.semaphore("s_" + e))
            t = 0
            for ins in self.streams[e]:
                if ins["signal"]:
                    t += 1
                    ins["ticket"] = t
        for s in self.dma_slots:
            sems["dma_" + s] = stack.enter_context(nc.semaphore("d_" + s))
        block = stack.enter_context(nc.Block())
        streams = self.streams

        def run(engname, eng):
            waited = {}
            for ins in streams[engname]:
                need = dict(ins["dma_need"])
                for (de, di) in ins["deps"]:
                    d = streams[de][di]
                    if d["dma_slot"] is not None:
                        continue
                    elif de != engname or engname != "pe":
                        key, val = de, d["ticket"]
                    else:
                        continue
                    if val > need.get(key, 0):
                        need[key] = val
                for key, val in need.items():
                    if waited.get(key, 0) < val:
                        eng.wait_ge(sems[key], val)
                        waited[key] = val
                bi = ins["fn"](eng)
                if ins["dma_slot"] is not None:
                    bi.then_inc(sems["dma_" + ins["dma_slot"]], 16)
                elif ins["signal"]:
                    bi.then_inc(sems[engname], 1)

        @block.tensor
        def _(e):
            run("pe", e)

        @block.scalar
        def _(e):
            run("act", e)

        @block.vector
        def _(e):
            run("dve", e)

        @block.gpsimd
        def _(e):
            run("pool", e)

        @block.sync
        def _(e):
            run("sp", e)


def _consts():
    bf = ml_dtypes.bfloat16
    c = {}
    n = np.arange(NLAT, dtype=np.int64)
    ang = 2.0 * np.pi * ((n[:, None] * n[None, :]) % NLAT).astype(np.float64) / NLAT
    c["dftC"] = np.cos(ang).astype(bf)
    c["dftS"] = np.sin(ang).astype(bf)
    n2 = np.arange(NCTX, dtype=np.int64)
    ang2 = 2.0 * np.pi * ((n2[:, None] * n2[None, :]) % NCTX).astype(np.float64) / NCTX
    d256 = np.stack([np.cos(ang2), np.sin(ang2)], 0)
    d256 = d256.reshape(2, 2, 128, NCTX).transpose(2, 0, 1, 3)
    c["dft256"] = np.ascontiguousarray(d256).astype(bf)
    m = np.arange(128, dtype=np.int64)
    angc = 2.0 * np.pi * ((m[:, None] * m[None, :]) % 128).astype(np.float64) / 128
    s_lat = 1.0 / math.sqrt(NLAT * 128.0)
    s_ctx = 1.0 / math.sqrt(NCTX * 128.0)
    ccs = np.stack([np.cos(angc) * s_lat, -np.sin(angc) * s_lat, np.cos(angc) * s_ctx, -np.sin(angc) * s_ctx], 1)
    c["ccs"] = np.ascontiguousarray(ccs).astype(bf)
    N3 = 384
    band = np.zeros((128, 20, 128), np.float64)
    t = np.arange(N3)
    for gi, win in enumerate((2, 4, 8, 16)):
        lo = np.clip(t - win // 2, 0, N3 - 1)
        hi = np.clip(t + (win - win // 2) - 1, 0, N3 - 1)
        cnt = (hi - lo + 1).astype(np.float64)
        B = np.zeros((N3, N3), np.float64)
        for tt in range(N3):
            B[lo[tt]:hi[tt] + 1, tt] = 1.0 / cnt[tt]
        B -= np.eye(N3)
        band[:, gi * 5 + 0, :] = B[128:256, 128:256]
        band[:, gi * 5 + 1, :] = B[0:128, 0:128]
        band[:, gi * 5 + 2, :] = B[256:384, 256:384]
        band[:, gi * 5 + 3, :] = B[0:128, 128:256]
        band[:, gi * 5 + 4, :] = B[128:256, 0:128]
    c["band"] = band.astype(bf)
    tok = np.arange(NLAT)
    row = (tok // 64).astype(np.float32)
    col = (tok % 64).astype(np.float32)
    inv = (np.float32(10000.0) ** (-np.arange(0, 64, 2, dtype=np.float32) / np.float32(64))).astype(np.float32)
    angr = np.stack([row[:, None] * inv[None, :], col[:, None] * inv[None, :]], 1).astype(np.float32)
    c["ropeC"] = np.cos(angr).reshape(NLAT, 64).astype(np.float32)
    c["ropeS"] = np.sin(angr).reshape(NLAT, 64).astype(np.float32)
    c["identF"] = np.eye(128, dtype=np.float32)
    c["identB"] = np.eye(128, dtype=np.float32).astype(bf)
    c["onesB"] = np.ones((128, 128), np.float32).astype(bf)
    return c


_CONST_CACHE = {}


def get_consts():
    if not _CONST_CACHE:
        _CONST_CACHE.update(_consts())
    return _CONST_CACHE


class _Stop(Exception):
    pass


def build_program(debug=False, stop_after=None, wseq=None):
    nc = bass.Bass("TRN2", target_bir_lowering=False)
    S = Sched()

    marks = []

    def mark(name):
        marks.append((name, len(S.streams['pe'])))
        if stop_after is not None and name == stop_after:
            raise _Stop()
    stack = ExitStack()

    def din(name, shape, dt=F32):
        return nc.dram_tensor(name, list(shape), dt, kind="ExternalInput").ap()

    x_in = din("x", [NLAT, D])
    ctx_in = din("ctx", [NCTX, D])
    cvec = din("cvec", [128, 32])
    ada_w = din("ada_w", [DEPTH, D, 3 * D])
    ada_b2 = din("ada_b2", [DEPTH, 2, 3 * D])
    ngcol = din("ngcol", [128, 32])
    fngb = din("fngb", [128, D])
    w_in = din("w_in", [DEPTH, D, INW])
    w_out = din("w_out", [DEPTH, D, D])
    gains = din("gains", [DEPTH, 128, 256])
    pool_w = din("pool_w", [DEPTH, 4, 128, 128])
    pscol = din("pscol", [128, 8])
    four_w = din("fourier_w", [DEPTH, 4, 128, 128])
    dftC = din("dftC", [NLAT, NLAT], BF16)
    dftS = din("dftS", [NLAT, NLAT], BF16)
    dft256_d = din("dft256", [128, 2, 2, NCTX], BF16)
    ccs_d = din("ccs", [128, 4, 128], BF16)
    band_d = din("band", [128, 20, 128], BF16)
    ropeC = din("ropeC", [NLAT, 64])
    ropeS = din("ropeS", [NLAT, 64])
    identF_d = din("identF", [128, 128])
    identB_d = din("identB", [128, 128], BF16)
    onesB_d = din("onesB", [128, 128], BF16)
    out_d = nc.dram_tensor("out", [NLAT, D], F32, kind="ExternalOutput").ap()
    x1_d = nc.dram_tensor("x1s", [NLAT, D], F32, kind="ExternalOutput" if debug else "Internal").ap()
    xc1_d = nc.dram_tensor("xc1s", [NCTX, D], F32, kind="ExternalOutput" if debug else "Internal").ap()
    wc_d = nc.dram_tensor("wcache", [DEPTH, 11, 128, 16 * 512], BF16).ap()
    modrow_h = nc.dram_tensor("modrow", [DEPTH, 2, 3 * D], F32)
    modrow_d = modrow_h.ap()

    SB = stack.enter_context(nc.sbuf_tensor("SB", [128, SB_BYTES // 4], F32))
    cur = [0]

    def alloc(nbytes, align=512):
        o = (cur[0] + align - 1) // align * align
        cur[0] = o + nbytes
        assert cur[0] <= SB_BYTES, ("SBUF overflow", cur[0])
        return o

    def vf(off, n):
        return SB[:, off // 4: off // 4 + n]

    def vb(off, n):
        return SB[:, off // 4: off // 4 + (n + 1) // 2].bitcast(BF16)[:, 0:n]

    def r3(v, b):
        return v.rearrange("p (a b) -> p a b", b=b)

    o_hT = alloc(16 * NTOK * 2)
    hT = r3(vb(o_hT, 16 * NTOK), NTOK)
    o_KT = alloc(2 * NTOK * 2)
    KT = r3(vb(o_KT, 2 * NTOK), NTOK)
    o_V = alloc(18 * 256 * 2)
    Vv = r3(vb(o_V, 18 * 256), 256)
    o_ReT = alloc(4 * NTOK * 2)
    ReT = r3(vb(o_ReT, 4 * NTOK), NTOK)
    o_wb = [alloc(16 * 512 * 2), alloc(16 * 512 * 2)]
    wbuf = [r3(vb(o, 16 * 512), 512) for o in o_wb]
    identF = vf(alloc(512), 128)
    identB = vb(alloc(256, 256), 128)
    onesB = vb(alloc(256, 256), 128)
    band = r3(vb(alloc(20 * 256), 20 * 128), 128)
    dft256 = vb(alloc(2048), 1024).rearrange("p (m j k) -> p m j k", m=2, j=2)
    ccs = r3(vb(alloc(1024), 512), 128)
    poolw = r3(vb(alloc(1024), 512), 128)
    fourw = r3(vb(alloc(1024), 512), 128)
    gains_s = vf(alloc(1024), 256)
    o_small = alloc(2048)
    sc3 = r3(vb(o_small, 32), 2)
    cv = vf(o_small + 128, 32)
    cvt = vf(o_small + 256, 32)
    ngc = vf(o_small + 384, 32)
    psc = vf(o_small + 512, 8)
    mcols = vf(o_small + 576, 64).rearrange("p (w k j) -> p w k j", w=2, k=2)
    mhalf = vf(o_small + 832, 16)
    ssb = vf(alloc(64, 512), 16)
    rsb = vf(alloc(64, 512), 16)
    ssq = vf(alloc(64, 512), 16)
    rfin = vf(alloc(64, 512), 4)
    o_gp = [alloc(2048), alloc(2048)]
    gpiece = [vf(o, 512) for o in o_gp]
    o_rope = alloc(1024)
    ropec = [vf(o_rope, 64), vf(o_rope + 512, 64)]
    ropes = [vf(o_rope + 256, 64), vf(o_rope + 768, 64)]
    o_R = alloc(0, 1024)
    R_BYTES = SB_BYTES - o_R

    class Arena:
        def __init__(self):
            self.c = 0

        def a(self, nbytes, align=512):
            o = (self.c + align - 1) // align * align
            self.c = o + nbytes
            assert self.c <= R_BYTES, ("scratch overflow", self.c, R_BYTES)
            return o_R + o

    banks = [stack.enter_context(nc.psum_tensor("pb%d" % i, [128, 512], F32)) for i in range(8)]

    def bankb(i):
        return banks[i][:, :].bitcast(BF16)

    rr = {}

    def nxt(cls, lst):
        i = rr.get(cls, 0)
        rr[cls] = i + 1
        return lst[i % len(lst)]

    def dma(eng, out, in_, slot, R=None, W=None):
        S.op(eng, lambda e, o=out, i=in_: e.dma_start(out=o, in_=i), R=R if R is not None else [in_],
             W=W if W is not None else [out], dma_slot=slot + "_" + eng)

    def mm(out, lhsT, rhs, start, stop, extraR=()):
        S.op("pe", lambda e, o=out, l=lhsT, r=rhs, s=start, t=stop: e.matmul(o, l, r, start=s, stop=t),
             R=[lhsT, rhs] + list(extraR), W=[out])

    def tr(out, in_, ident):
        S.op("pe", lambda e, o=out, i=in_, d=ident: e.transpose(o, i, d), R=[in_, ident], W=[out])

    def act(out, in_, func, bias=None, scale=None, accum=None, extraR=()):
        kw = {}
        Rl = [in_] + list(extraR)
        Wl = [out]
        if bias is not None:
            kw["bias"] = bias
            if not isinstance(bias, float):
                Rl.append(bias)
        if scale is not None:
            kw["scale"] = scale
            if not isinstance(scale, float):
                Rl.append(scale)
        if accum is not None:
            kw["accum_out"] = accum
            Wl.append(accum)
        S.op("act", lambda e, o=out, i=in_, f=func, k=kw: e.activation(o, i, f, **k), R=Rl, W=Wl)

    def ts(eng, out, in0, s1, s2, op0, op1=None):
        Rl = [in0] + [s for s in (s1, s2) if s is not None and not isinstance(s, float)]
        if op1 is None:
            S.op(eng, lambda e, o=out, i=in0, a=s1, p=op0: e.tensor_scalar(o, i, a, None, p), R=Rl, W=[out])
        else:
            S.op(eng, lambda e, o=out, i=in0, a=s1, b=s2, p=op0, q=op1: e.tensor_scalar(o, i, a, b, p, q), R=Rl, W=[out])

    def tt(eng, out, in0, in1, op):
        S.op(eng, lambda e, o=out, a=in0, b=in1, p=op: e.tensor_tensor(o, a, b, p), R=[in0, in1], W=[out])

    def stt(out, in0, scalar, in1, op0, op1):
        Rl = [in0, in1] + ([scalar] if not isinstance(scalar, float) else [])
        S.op("dve", lambda e, o=out, a=in0, s=scalar, b=in1, p=op0, q=op1: e.scalar_tensor_tensor(o, a, s, b, p, q),
             R=Rl, W=[out])

    def cp(eng, out, in_):
        if eng == "act":
            S.op("act", lambda e, o=out, i=in_: e.activation(o, i, AF.Copy), R=[in_], W=[out])
        else:
            S.op(eng, lambda e, o=out, i=in_: e.tensor_copy(o, i), R=[in_], W=[out])

    def recip(out, in_):
        S.op("dve", lambda e, o=out, i=in_: e.reciprocal(o, i), R=[in_], W=[out])

    def rstd_from_ss(ss, rs, n, inv_n):
        ts("pool", rs, ss, float(inv_n), float(EPS), ALU.mult, ALU.add)
        tt("pool", rs, rs, mhalf[:, 0:n], ALU.pow)

    def ap4(v, off_el, dims):
        ps = list(v.ap)[0][0]
        return bass.AP(v.tensor, v.offset + off_el, [[ps, 128]] + [list(d) for d in dims])

    wcount = [0]

    wrec = []
    wseen = set()

    def issue_w(k, ent):
        src3, eng, ckey = ent
        buf = wbuf[k % 2]
        if ckey is not None and ckey in wseen:
            lyr, idx = ckey
            dma("sp", buf.rearrange("p a b -> p (a b)"), wc_d[lyr, idx], "w%d" % (k % 2), R=[("D", "wc", lyr, idx)])
            return
        dma(eng, buf[:, :, :], src3, "w%d" % (k % 2))
        if ckey is not None:
            lyr, idx = ckey
            wseen.add(ckey)
            dma("sp", wc_d[lyr, idx], buf.rearrange("p a b -> p (a b)"), "wst", W=[("D", "wc", lyr, idx)])

    def load_w(src3, eng="pool", ckey=None):
        k = wcount[0]
        wcount[0] += 1
        wrec.append((src3, eng, ckey))
        if wseq is None:
            dma(eng, wbuf[k % 2][:, :, :], src3, "w%d" % (k % 2))
        else:
            if k == 0:
                issue_w(0, wseq[0])
            if k + 1 < len(wseq):
                issue_w(k + 1, wseq[k + 1])
        return wbuf[k % 2]

    def wsrc(wd, l, c0, ncol=512):
        return wd[l, :, c0:c0 + ncol].rearrange("(j p) n -> p j n", p=128)

    for dst, src in ((identF, identF_d), (identB, identB_d), (onesB, onesB_d),
                     (band, band_d), (dft256, dft256_d), (ccs, ccs_d),
                     (cv, cvec), (ngc, ngcol), (psc, pscol)):
        dma("sp", dst, src, "const")
    S.op("pool", lambda e: e.memset(mhalf, -0.5), W=[mhalf])
    act(cvt, cv, AF.Tanh, scale=0.5)
    stt(cvt, cvt, 1.0, cv, ALU.add, ALU.mult)
    ts("dve", sc3.rearrange("p a b -> p (a b)"), cvt, 0.5, None, ALU.mult)

    xsrc = {("lat", 0): x_in, ("ctx", 0): ctx_in, ("lat", 1): x1_d, ("ctx", 1): xc1_d}
    xdst = {("lat", 0): x1_d, ("ctx", 0): xc1_d, ("lat", 1): out_d}

    def xkey(tensor, tile, cb):
        return ("D", tensor.tensor.name, tile, cb)

    def xkeys(tensor, tile):
        return [xkey(tensor, tile, cb) for cb in range(4)]

    deferred = []
    try:
      for l in range(DEPTH):
          last = l == DEPTH - 1
          dma("pool", poolw, pool_w[l].rearrange("g c d -> c g d"), "lw")
          dma("pool", fourw, four_w[l].rearrange("g c d -> c g d"), "lw")
          dma("sp", gains_s, gains[l], "lw2")

          ar = Arena()
          o_mp = [ar.a(2048), ar.a(2048)]
          o_ab = [ar.a(2048), ar.a(2048)]
          colsP = banks[1][:, 0:96].rearrange("p (c w) -> p c w", w=2)
          for nb in range(12):
              wv = load_w(wsrc(ada_w, l, nb * 512))
              abp = vf(o_ab[nb % 2], 512)[0:2, :]
              mp = vf(o_mp[nb % 2], 512)[0:2, :]
              dma("sp", abp, ada_b2[l, :, nb * 512:(nb + 1) * 512], "adab%d" % (nb % 2))
              acc = banks[0][0:2, :]
              for j in range(16):
                  mm(acc, sc3[:, j, :], wv[:, j, :], j == 0, j == 15)
              tt("dve", mp, acc, abp, ALU.add)
              dma("sp", modrow_d[l, :, nb * 512:(nb + 1) * 512], mp, "modrow",
                  W=[("D", "modrow", l, nb)])
              for q in range(4):
                  tr(colsP[:, nb * 4 + q, :], mp[:, q * 128:(q + 1) * 128], identF[0:2, 0:2])
          for which in range(2):
              cp("dve", mcols[:, which, 1, :], colsP[:, 0:16, which])
              stt(mcols[:, which, 0, :], colsP[:, 16:32, which], 1.0, ngc[:, l * 16:(l + 1) * 16], ALU.add, ALU.mult)

          mark('ada%d' % l)
          ar = Arena()
          o_xt = [ar.a(8192), ar.a(8192)]
          o_junk = ar.a(4096)
          junk = vb(o_junk, 2048)
          seqs = [("ctx", t) for t in range(2)] + [("lat", t) for t in range(16)]
          def p1_a(ti):
              which, t = seqs[ti]
              src = xsrc[(which, l)]
              xt = vf(o_xt[ti % 2], 2048)
              dma("sp", xt, src[t * 128:(t + 1) * 128, :], "xt%d" % (ti % 2),
                  R=xkeys(src, t) if l > 0 else [])
              ss = ssb[:, 12 + ti % 2:13 + ti % 2]
              act(junk, xt, AF.Square, accum=ss)
              rs = rsb[:, 12 + ti % 2:13 + ti % 2]
              rstd_from_ss(ss, rs, 1, 1.0 / D)
              ts("dve", xt, xt, rs, None, ALU.mult)

          def p1_b(ti):
              which, t = seqs[ti]
              wi = 0 if which == "lat" else 1
              xt = vf(o_xt[ti % 2], 2048)
              gt = ti
              pbs = [0, 1, 2, 3] if ti % 2 == 0 else [4, 5, 6, 7]
              for jb in range(4):
                  for j in range(jb * 4, jb * 4 + 4):
                      pb = banks[pbs[jb]][:, (j % 4) * 128:(j % 4 + 1) * 128]
                      tr(pb, xt[:, j * 128:(j + 1) * 128], identF)
                  for j in range(jb * 4, jb * 4 + 4):
                      pb = banks[pbs[jb]][:, (j % 4) * 128:(j % 4 + 1) * 128]
                      dst = hT[:, j, gt * 128:(gt + 1) * 128]
                      if jb % 2 == 0:
                          act(dst, pb, AF.Identity, bias=mcols[:, wi, 1, j:j + 1], scale=mcols[:, wi, 0, j:j + 1])
                      else:
                          ts("dve", dst, pb, mcols[:, wi, 0, j:j + 1], mcols[:, wi, 1, j:j + 1], ALU.mult, ALU.add)

          p1_a(0)
          for ti in range(len(seqs)):
              if ti + 1 < len(seqs):
                  p1_a(ti + 1)
              p1_b(ti)

          mark('p1_%d' % l)
          ar = Arena()
          o_kx = ar.a(1024)
          o_ta = ar.a(512)
          o_tb = ar.a(512)
          o_kr = [ar.a(512), ar.a(512)]
          junk = vb(ar.a(512), 256)
          wv = load_w(wsrc(w_in, l, OFF_K))
          def kv_mm(gt):
              pb = banks[gt % 2]
              for j in range(16):
                  mm(pb[:, :], hT[:, j, gt * 128:(gt + 1) * 128], wv[:, j, :], j == 0, j == 15)

          kv_mm(0)
          for gt in range(18):
              is_lat = gt >= 2
              pb = banks[gt % 2]
              if gt + 1 < 18:
                  kv_mm(gt + 1)
              so = 2 * (gt % 2)
              ss = ssb[:, so:so + 2]
              rs = rsb[:, so:so + 2]
              for h in range(2):
                  act(junk[:, 0:128], pb[:, h * 128:(h + 1) * 128], AF.Square, accum=ssb[:, so + h:so + h + 1])
              rstd_from_ss(ss, rs, 2, 1.0 / 128)
              kr = vb(o_kr[gt % 2], 256)
              kx = vf(o_kx, 256)
              if is_lat:
                  t = gt - 2
                  rc, rsn = ropec[gt % 2], ropes[gt % 2]
                  dma("sp", rc, ropeC[t * 128:(t + 1) * 128, :], "rope%d" % (gt % 2))
                  dma("sp", rsn, ropeS[t * 128:(t + 1) * 128, :], "rope%d" % (gt % 2))
              for h in range(2):
                  stt(kx[:, h * 128:(h + 1) * 128] if is_lat else kr[:, h * 128:(h + 1) * 128],
                      pb[:, h * 128:(h + 1) * 128], rsb[:, so + h:so + h + 1], gains_s[:, 128:256], ALU.mult, ALU.mult)
              cp("act", Vv[:, gt, :], pb[:, 256:512])
              if is_lat:
                  rope(S, tt, ap4, kx, kr, vf(o_ta, 128), vf(o_tb, 128), rc, rsn, 2)
              kb = nxt("B", [2, 3])
              pbt = bankb(kb)
              for h in range(2):
                  tr(pbt[:, h * 128:(h + 1) * 128], kr[:, h * 128:(h + 1) * 128], identB)
              for h in range(2):
                  cp("dve", KT[:, h, gt * 128:(gt + 1) * 128], pbt[:, h * 128:(h + 1) * 128])

          mark('g1_%d' % l)
          ar = Arena()
          o_tm = ar.a(18 * 512 * 2)
          tm = r3(vb(o_tm, 18 * 512), 512)
          o_pT2 = ar.a(2 * 4 * 512 * 2)
          pT2 = vb(o_pT2, 4096).rearrange("p (m g k) -> p m g k", m=2, g=4)
          wv = load_w(wsrc(w_in, l, OFF_FOUR))
          ftiles = list(range(18)) if not last else list(range(2, 18))
          for gt in ftiles:
              pb = banks[nxt("A", [0, 1])]
              for j in range(16):
                  mm(pb[:, :], hT[:, j, gt * 128:(gt + 1) * 128], wv[:, j, :], j == 0, j == 15)
              cp("act" if gt % 2 == 0 else "dve", tm[:, gt, :], pb[:, :])
          if not last:
              for mat in range(2):
                  for g in range(4):
                      pb = banks[nxt("A", [0, 1])]
                      for j in range(2):
                          mm(pb[:, 0:256], tm[:, j, g * 128:(g + 1) * 128], dft256[:, mat, j, :], j == 0, j == 1)
                      cp("act" if g % 2 == 0 else "dve", pT2[:, mat, g, 0:256], pb[:, 0:256])
              for g in range(4):
                  pb = banks[nxt("C", [4, 5])]
                  mm(pb[:, 0:256], ccs[:, 2, :], pT2[:, 0, g, 0:256], True, False)
                  mm(pb[:, 0:256], ccs[:, 3, :], pT2[:, 1, g, 0:256], False, True)
                  cp("act" if g % 2 == 0 else "dve", ReT[:, g, 0:256], pb[:, 0:256])
          for kb in range(4):
              for mat, dsrc in enumerate((dftC, dftS)):
                  dv = load_w(dsrc[:, kb * 512:(kb + 1) * 512].rearrange("(j p) k -> p j k", p=128), eng="pool")
                  for g in range(4):
                      pb = banks[nxt("A", [0, 1, 2, 3])]
                      for j in range(16):
                          mm(pb[:, :], tm[:, 2 + j, g * 128:(g + 1) * 128], dv[:, j, :], j == 0, j == 15)
                      cp("act" if g % 2 == 0 else "dve", pT2[:, mat, g, :], pb[:, :])
              for g in range(4):
                  pb = banks[nxt("C", [4, 5])]
                  mm(pb[:, :], ccs[:, 0, :], pT2[:, 0, g, :], True, False)
                  mm(pb[:, :], ccs[:, 1, :], pT2[:, 1, g, :], False, True)
                  cp("act" if g % 2 == 0 else "dve", ReT[:, g, 256 + kb * 512:256 + (kb + 1) * 512], pb[:, :])

          mark('g3_%d' % l)
          groups = ([] if last else [("ctx", 0, NCTX, 0)]) + [("lat", 256 + g * 512, 512, g * 4) for g in range(4)]
          for (which, tok0, ntok, T0) in groups:
              wi = 0 if which == "lat" else 1
              is_lat = which == "lat"
              NT = 16 if is_lat else 2
              ntile = ntok // 128
              ktiles = list(range(18)) if is_lat else [0, 1]
              src_x = xsrc[(which, l)]
              dst_x = xdst[(which, l)]
              ar = Arena()
              o_q = ar.a(6 * 1024, 1024)
              QT = r3(vb(o_q, 4 * ntok), ntok)
              qx = vf(o_q + 4096, 512)
              up_tm = r3(vb(o_q, 6 * 512), 512)
              mgT = r3(vb(ar.a(16 * ntok * 2), 16 * ntok), ntok)
              PT = [vb(ar.a(ntok * 2), ntok) for _ in range(3)]
              tmpAf = [vf(ar.a(2048), 512) for _ in range(2)]
              tmpA = [v[:, 0:ntok] for v in tmpAf]
              tmpO = [vf(ar.a(2048), 512)[:, 0:ntok] for _ in range(2)]
              qr = [vb(ar.a(1024), 512) for _ in range(2)]
              ta = vf(ar.a(1024), 256)
              tb = vf(ar.a(1024), 256)
              plT = [vb(ar.a(ntok * 2), ntok)] * 2
              xp = [vf(ar.a(2048), 512) for _ in range(2)]

              def gate_chunk(fc, wg, branch, branch_in_psum):
                  gb = banks[nxt("B", [2, 3])]
                  hcol = (fc % 4) * 128
                  for j in range(16):
                      mm(gb[:, 0:ntok], wg[:, j, hcol:hcol + 128], hT[:, j, tok0:tok0 + ntok], j == 0, j == 15)
                  th = tmpA[fc % 2]
                  act(th, gb[:, 0:ntok], AF.Tanh, scale=0.5)
                  stt(th, th, 1.0, gb[:, 0:ntok], ALU.add, ALU.mult)
                  stt(mgT[:, fc, :], th, 0.5, branch, ALU.mult, ALU.mult)

              for hb in range(2):
                  wq = load_w(wsrc(w_in, l, OFF_Q + hb * 512), ckey=(l, hb))
                  def q_mm(tti):
                      pb = banks[tti % 2]
                      c0 = tok0 + tti * 128
                      for j in range(16):
                          mm(pb[:, :], hT[:, j, c0:c0 + 128], wq[:, j, :], j == 0, j == 15)

                  q_mm(0)
                  for tti in range(ntile):
                      pb = banks[tti % 2]
                      if tti + 1 < ntile:
                          q_mm(tti + 1)
                      so = 4 + 4 * (tti % 2)
                      for h in range(4):
                          act(PT[2][:, 0:128], pb[:, h * 128:(h + 1) * 128], AF.Square, accum=ssb[:, so + h:so + h + 1])
                      mark('q1')
                      rstd_from_ss(ssb[:, so:so + 4], rsb[:, so:so + 4], 4, 1.0 / 128)
                      mark('q2')
                      qrv = qr[tti % 2]
                      if is_lat:
                          t = T0 + tti
                          rc, rsn = ropec[tti % 2], ropes[tti % 2]
                          dma("sp", rc, ropeC[t * 128:(t + 1) * 128, :], "rope%d" % (tti % 2))
                          dma("sp", rsn, ropeS[t * 128:(t + 1) * 128, :], "rope%d" % (tti % 2))
                      for h in range(4):
                          stt(qx[:, h * 128:(h + 1) * 128] if is_lat else qrv[:, h * 128:(h + 1) * 128],
                              pb[:, h * 128:(h + 1) * 128], rsb[:, so + h:so + h + 1], gains_s[:, 0:128], ALU.mult, ALU.mult)
                      if is_lat:
                          rope(S, tt, ap4, qx, qrv, ta, tb, rc, rsn, 4)
                      mark('q3')
                      pbt = bankb(nxt("B", [2, 3]))
                      for h in range(4):
                          tr(pbt[:, h * 128:(h + 1) * 128], qrv[:, h * 128:(h + 1) * 128], identB)
                      mark('q4')
                      cp("dve" if tti % 2 == 0 else "act", QT[:, :, tti * 128:(tti + 1) * 128],
                         pbt[:, 0:512].rearrange("p (h d) -> p h d", h=4))
                      mark('q5')
                  mark('qdone')
                  wg = load_w(wsrc(w_in, l, OFF_GATE + hb * 512), ckey=(l, 2 + hb))
                  for h in range(4):
                      fc = hb * 4 + h
                      Ob, Lb = banks[6], banks[7]
                      nk = len(ktiles)

                      def s_mm(ki):
                          kt = ktiles[ki]
                          sb = banks[(4, 5, 0, 1)[ki % 4]]
                          mm(sb[:, 0:ntok], KT[:, hb, kt * 128:(kt + 1) * 128], QT[:, h, :], True, True)
                          act(PT[ki % 3], sb[:, 0:ntok], AF.Exp, scale=float(128.0 ** -0.5))

                      def pv_mm(ki):
                          kt = ktiles[ki]
                          mm(Ob[:, 0:ntok], Vv[:, kt, hb * 128:(hb + 1) * 128], PT[ki % 3], ki == 0, ki == nk - 1)
                          mm(Lb[:, 0:ntok], onesB, PT[ki % 3], ki == 0, ki == nk - 1)

                      s_mm(0)
                      if nk > 1:
                          s_mm(1)
                      for ki in range(nk):
                          if ki + 2 < nk:
                              s_mm(ki + 2)
                          pv_mm(ki)
                      mark('attA')
                      tO = tmpO[fc % 2]
                      recip(tO, Lb[:, 0:ntok])
                      tt("dve", tO, Ob[:, 0:ntok], tO, ALU.mult)
                      mark('attB')
                      gate_chunk(fc, wg, tO, False)
                      mark('attC')

              mark('att_%d_%s_%d' % (l, which, T0))
              while deferred:
                  deferred.pop(0)()
              wp = load_w(wsrc(w_in, l, OFF_POOL), ckey=(l, 6))
              Tlo = max(T0 - 1, 0)
              Thi = min(T0 + ntile, NT - 1)
              seq_tok0 = 256 if is_lat else 0
              for T in range(Tlo, Thi + 1):
                  pb = banks[nxt("A", [0, 1])]
                  c0 = seq_tok0 + T * 128
                  for j in range(16):
                      mm(pb[:, :], hT[:, j, c0:c0 + 128], wp[:, j, :], j == 0, j == 15)
                  cp("act" if T % 2 == 0 else "dve", up_tm[:, T - Tlo, :], pb[:, :])
              wg = load_w(wsrc(w_in, l, OFF_GATE + 2 * 512), ckey=(l, 4))
              for g in range(4):
                  fc = 8 + g
                  pb = banks[nxt("C", [4, 5])]
                  for tti in range(ntile):
                      T = T0 + tti
                      terms = []
                      if T > 0:
                          terms.append((T - 1, 3))
                      terms.append((T, 1 if T == 0 else (2 if T == NT - 1 else 0)))
                      if T < NT - 1:
                          terms.append((T + 1, 4))
                      for i, (Tn, kind) in enumerate(terms):
                          mm(pb[:, tti * 128:(tti + 1) * 128], up_tm[:, Tn - Tlo, g * 128:(g + 1) * 128],
                             band[:, g * 5 + kind, :], i == 0, i == len(terms) - 1)
                  pl = plT[g % 2]
                  cp("act", pl, pb[:, 0:ntok])
                  yb = banks[nxt("D", [6, 7])]
                  mm(yb[:, 0:ntok], poolw[:, g, :], pl, True, True)
                  tO = tmpO[fc % 2]
                  ts("dve", tO, yb[:, 0:ntok], psc[:, l * 4 + g:l * 4 + g + 1], None, ALU.mult)
                  gate_chunk(fc, wg, tO, False)

              wg = load_w(wsrc(w_in, l, OFF_GATE + 3 * 512), ckey=(l, 5))
              for g in range(4):
                  fc = 12 + g
                  yb = banks[nxt("D", [6, 7])]
                  mm(yb[:, 0:ntok], fourw[:, g, :], ReT[:, g, tok0:tok0 + ntok], True, True)
                  gate_chunk(fc, wg, yb[:, 0:ntok], True)

              mark('four_%d_%s_%d' % (l, which, T0))
              gate_c0 = 2 * D
              for cb in range(4):
                  wo = load_w(wsrc(w_out, l, cb * 512), ckey=(l, 7 + cb))
                  gp = gpiece[cb % 2]
                  dma("sp", gp, modrow_d[l, wi:wi + 1, gate_c0 + cb * 512:gate_c0 + (cb + 1) * 512].partition_broadcast(128).rearrange("p a n -> p (a n)"),
                      "gp%d" % (cb % 2), R=[("D", "modrow", l, 8 + cb)])
                  for tti in range(ntile):
                      T = T0 + tti
                      xpv = xp[nxt("xp", [0, 1])]
                      slot = "xp%d" % ((rr["xp"] - 1) % 2)
                      dma("sp", xpv, src_x[T * 128:(T + 1) * 128, cb * 512:(cb + 1) * 512], slot,
                          R=[xkey(src_x, T, cb)] if l > 0 else [])
                      pb = banks[nxt("W", [0, 1, 4, 5])]
                      for fc in range(16):
                          mm(pb[:, :], mgT[:, fc, tti * 128:(tti + 1) * 128], wo[:, fc, :], fc == 0, fc == 15)
                      tmpx = tmpAf[tti % 2]
                      tt("dve", tmpx, pb[:, :], gp, ALU.mult)
                      tt("dve", xpv, tmpx, xpv, ALU.add)
                      if last:
                          act(tmpAf[(tti + 1) % 2], xpv, AF.Square, accum=ssq[:, tti * 4 + cb:tti * 4 + cb + 1])
                      dma("sp", dst_x[T * 128:(T + 1) * 128, cb * 512:(cb + 1) * 512], xpv, "xo%d" % ((rr["xp"] - 1) % 2),
                          W=[xkey(dst_x, T, cb)])
              mark('wout_%d_%s_%d' % (l, which, T0))
              if last:
                  for tti in range(ntile):
                      S.op("dve", lambda e, o=rfin[:, tti:tti + 1], i=ssq[:, tti * 4:(tti + 1) * 4]:
                           e.tensor_reduce(o, i, mybir.AxisListType.X, ALU.add),
                           R=[ssq[:, tti * 4:(tti + 1) * 4]], W=[rfin[:, tti:tti + 1]])
                      rstd_from_ss(rfin[:, tti:tti + 1], rfin[:, tti:tti + 1], 1, 1.0 / D)

                  def final_pass(T0=T0, ntile=ntile, xp=xp):
                      for cb in range(4):
                          fgv = gpiece[cb % 2]
                          dma("sp", fgv, fngb[:, cb * 512:(cb + 1) * 512], "gp%d" % (cb % 2))
                          for tti in range(ntile):
                              T = T0 + tti
                              xpv = xp[nxt("xp", [0, 1])]
                              sl = (rr["xp"] - 1) % 2
                              dma("sp", xpv, out_d[T * 128:(T + 1) * 128, cb * 512:(cb + 1) * 512], "xp%d" % sl,
                                  R=[xkey(out_d, T, cb)])
                              stt(xpv, xpv, rfin[:, tti:tti + 1], fgv, ALU.mult, ALU.mult)
                              dma("sp", out_d[T * 128:(T + 1) * 128, cb * 512:(cb + 1) * 512], xpv, "xo%d" % sl,
                                  W=[xkey(out_d, T, cb)])

                  deferred.append(final_pass)
      while deferred:
          deferred.pop(0)()

    except _Stop:
        pass

    allout = [xkey(out_d, T, cb) for T in range(16) for cb in range(4)]
    if debug:
        allout += [xkey(x1_d, T, cb) for T in range(16) for cb in range(4)]
        allout += [xkey(xc1_d, T, cb) for T in range(2) for cb in range(4)]
    S.op("sp", lambda e: e.nop(), R=allout)
    S.emit(nc, stack)
    stack.close()
    nc._wrec = wrec
    nc._marks = marks
    return nc


def rope(S, tt, ap4, xin, xout, ta, tb, rc, rsn, H):
    dims = [[128, H], [64, 2], [1, 32]]
    x1 = ap4(xin, 0, dims)
    x2 = ap4(xin, 32, dims)
    o1 = ap4(xout, 0, dims)
    o2 = ap4(xout, 32, dims)
    tdims = [[64, H], [32, 2], [1, 32]]
    a = ap4(ta, 0, tdims)
    b = ap4(tb, 0, tdims)
    cdims = [[0, H], [32, 2], [1, 32]]
    c = ap4(rc, 0, cdims)
    s = ap4(rsn, 0, cdims)
    tt("dve", a, x1, c, ALU.mult)
    tt("dve", b, x2, s, ALU.mult)
    tt("dve", o1, a, b, ALU.subtract)
    tt("dve", a, x2, c, ALU.mult)
    tt("dve", b, x1, s, ALU.mult)
    tt("dve", o2, a, b, ALU.add)


def build_two_pass(debug=False, stop_after=None):
    rec = build_program(debug=debug, stop_after=stop_after)._wrec
    return build_program(debug=debug, stop_after=stop_after, wseq=rec)


_NC_CACHE = {}


def make_in_maps(x, c, ctx, c_ctx, ada_w, ada_b, norm_g, w_in, q_norm_g, k_norm_g,
                 pool_w, pool_scale, fourier_w, w_out, final_norm_g, cores):
    f = lambda a: np.ascontiguousarray(np.asarray(a, dtype=np.float32))
    x, c, ctx, c_ctx = f(x), f(c), f(ctx), f(c_ctx)
    ada_w, ada_b, norm_g, w_in = f(ada_w), f(ada_b), f(norm_g), f(w_in)
    q_norm_g, k_norm_g, pool_w, pool_scale = f(q_norm_g), f(k_norm_g), f(pool_w), f(pool_scale)
    fourier_w, w_out, final_norm_g = f(fourier_w), f(w_out), f(final_norm_g)
    consts = get_consts()
    shared = dict(consts)
    shared["ada_w"] = ada_w
    shared["ada_b2"] = np.ascontiguousarray(np.repeat(ada_b[:, None, :], 2, axis=1))
    shared["ngcol"] = np.ascontiguousarray(norm_g.reshape(DEPTH, 16, 128).transpose(2, 0, 1).reshape(128, 32))
    shared["fngb"] = np.ascontiguousarray(np.broadcast_to(final_norm_g[None, :], (128, D)))
    shared["w_in"] = w_in
    shared["w_out"] = w_out
    g = np.concatenate([q_norm_g, k_norm_g], axis=1)
    shared["gains"] = np.ascontiguousarray(np.broadcast_to(g[:, None, :], (DEPTH, 128, 256)))
    shared["pool_w"] = pool_w
    shared["pscol"] = np.ascontiguousarray(pool_scale.reshape(DEPTH, 4, 128).transpose(2, 0, 1).reshape(128, 8))
    shared["fourier_w"] = fourier_w
    cc = c_ctx.reshape(16, 128).T
    maps = []
    for b in cores:
        m = dict(shared)
        m["x"] = x[b]
        m["ctx"] = ctx[b]
        cb = c[b].reshape(16, 128).T
        m["cvec"] = np.ascontiguousarray(np.stack([cb, cc], axis=2).reshape(128, 32))
        maps.append(m)
    return maps


def kernel(x, c, ctx, c_ctx, ada_w, ada_b, norm_g, w_in, q_norm_g, k_norm_g,
           pool_w, pool_scale, fourier_w, w_out, final_norm_g):
    if "nc" not in _NC_CACHE:
        _NC_CACHE["nc"] = build_two_pass(debug=False)
    nc = _NC_CACHE["nc"]
    maps = make_in_maps(x, c, ctx, c_ctx, ada_w, ada_b, norm_g, w_in, q_norm_g, k_norm_g,
                        pool_w, pool_scale, fourier_w, w_out, final_norm_g, list(range(8)))
    res = run_bass_kernel_spmd(nc, maps, core_ids=list(range(8)))
    out = np.stack([np.asarray(r["out"], dtype=np.float32) for r in res.results], axis=0)
    return out
```

```python
import math
from contextlib import ExitStack
import numpy as np
import ml_dtypes
import concourse.bass as bass
import concourse.mybir as mybir
from concourse.bass_utils import run_bass_kernel_spmd

F32 = mybir.dt.float32
BF16 = mybir.dt.bfloat16
AF = mybir.ActivationFunctionType
ALU = mybir.AluOpType

D = 2048
NLAT = 2048
NCTX = 256
NTOK = NLAT + NCTX
DEPTH = 2
INW = 4608
OFF_Q, OFF_K, OFF_V, OFF_POOL, OFF_FOUR, OFF_GATE = 0, 1024, 1280, 1536, 2048, 2560
EPS = 1e-6
import os
KVAR = int(os.environ.get('KVAR', '0'))
GR = 512
SB_BYTES = 207 * 1024


class Sched:
    ENGS = ("pe", "act", "dve", "pool", "sp")

    def __init__(self):
        self.streams = {e: [] for e in self.ENGS}
        self.lastw = {}
        self.readers = {}
        self.dma_slots = {}

    @staticmethod
    def keys(ap):
        if isinstance(ap, tuple):
            return [ap]
        if type(ap.tensor).__name__.startswith("DRam"):
            return []
        es = mybir.dt.size(ap.dtype)
        dims = list(ap.ap)[1:]
        lo = ap.offset * es
        ext = 1
        for st, cnt in dims:
            ext += abs(st) * (cnt - 1)
        hi = lo + ext * es
        nm = ap.tensor.name
        if nm.startswith("pb"):
            return [(nm, 0)]
        return [(nm, g) for g in range(lo // GR, (hi - 1) // GR + 1)]

    def op(self, eng, fn, R=(), W=(), dma_slot=None):
        idx = len(self.streams[eng])
        me = (eng, idx)
        deps = set()
        rk = [k for a in R for k in self.keys(a)]
        wk = [k for a in W for k in self.keys(a)]
        for k in rk:
            w = self.lastw.get(k)
            if w is not None:
                deps.add(w)
            if eng != "pe" and k[0].startswith("pb"):
                for re_, r in self.readers.get(k, {}).items():
                    if re_ != eng:
                        deps.add(r)
        for k in wk:
            w = self.lastw.get(k)
            if w is not None:
                deps.add(w)
            for r in self.readers.get(k, {}).values():
                deps.add(r)
        deps.discard(me)
        dma_need = {}
        for (de, di) in deps:
            sl = self.streams[de][di]["dma_slot"]
            if sl is not None:
                dma_need["dma_" + sl] = 16 * self.dma_slots[sl]
        ins = dict(eng=eng, idx=idx, fn=fn, deps=deps, dma_slot=dma_slot, dma_val=None, signal=False, ticket=None,
                   dma_need=dma_need)
        if dma_slot is not None:
            c = self.dma_slots.get(dma_slot, 0) + 1
            self.dma_slots[dma_slot] = c
            ins["dma_val"] = 16 * c
        self.streams[eng].append(ins)
        for k in wk:
            self.lastw[k] = me
            self.readers[k] = {}
        for k in rk:
            self.readers.setdefault(k, {})[eng if dma_slot is None else me] = me
        return me

    def emit(self, nc, stack):
        for e in self.ENGS:
            for ins in self.streams[e]:
                for (de, di) in ins["deps"]:
                    d = self.streams[de][di]
                    if d["dma_slot"] is None and (de != e or e != "pe"):
                        d["signal"] = True
        sems = {}
        for e in self.ENGS:
            sems[e] = stack.enter_context(nc.semaphore("s_" + e))
            t = 0
            for ins in self.streams[e]:
                if ins["signal"]:
                    t += 1
                    ins["ticket"] = t
        for s in self.dma_slots:
            sems["dma_" + s] = stack.enter_context(nc.semaphore("d_" + s))
        block = stack.enter_context(nc.Block())
        streams = self.streams

        def run(engname, eng):
            waited = {}
            for ins in streams[engname]:
                need = dict(ins["dma_need"])
                for (de, di) in ins["deps"]:
                    d = streams[de][di]
                    if d["dma_slot"] is not None:
                        continue
                    elif de != engname or engname != "pe":
                        key, val = de, d["ticket"]
                    else:
                        continue
                    if val > need.get(key, 0):
                        need[key] = val
                for key, val in need.items():
                    if waited.get(key, 0) < val:
                        eng.wait_ge(sems[key], val)
                        waited[key] = val
                bi = ins["fn"](eng)
                if ins["dma_slot"] is not None:
                    bi.then_inc(sems["dma_" + ins["dma_slot"]], 16)
                elif ins["signal"]:
                    bi.then_inc(sems[engname], 1)

        @block.tensor
        def _(e):
            run("pe", e)

        @block.scalar
        def _(e):
            run("act", e)

        @block.vector
        def _(e):
            run("dve", e)

        @block.gpsimd
        def _(e):
            run("pool", e)

        @block.sync
        def _(e):
            run("sp", e)


def _consts():
    bf = ml_dtypes.bfloat16
    c = {}
    n = np.arange(NLAT, dtype=np.int64)
    ang = 2.0 * np.pi * ((n[:, None] * n[None, :]) % NLAT).astype(np.float64) / NLAT
    c["dftC"] = np.cos(ang).astype(bf)
    c["dftS"] = np.sin(ang).astype(bf)
    n2 = np.arange(NCTX, dtype=np.int64)
    ang2 = 2.0 * np.pi * ((n2[:, None] * n2[None, :]) % NCTX).astype(np.float64) / NCTX
    d256 = np.stack([np.cos(ang2), np.sin(ang2)], 0)
    d256 = d256.reshape(2, 2, 128, NCTX).transpose(2, 0, 1, 3)
    c["dft256"] = np.ascontiguousarray(d256).astype(bf)
    m = np.arange(128, dtype=np.int64)
    angc = 2.0 * np.pi * ((m[:, None] * m[None, :]) % 128).astype(np.float64) / 128
    s_lat = 1.0 / math.sqrt(NLAT * 128.0)
    s_ctx = 1.0 / math.sqrt(NCTX * 128.0)
    ccs = np.stack([np.cos(angc) * s_lat, -np.sin(angc) * s_lat, np.cos(angc) * s_ctx, -np.sin(angc) * s_ctx], 1)
    c["ccs"] = np.ascontiguousarray(ccs).astype(bf)
    N3 = 384
    band = np.zeros((128, 20, 128), np.float64)
    t = np.arange(N3)
    for gi, win in enumerate((2, 4, 8, 16)):
        lo = np.clip(t - win // 2, 0, N3 - 1)
        hi = np.clip(t + (win - win // 2) - 1, 0, N3 - 1)
        cnt = (hi - lo + 1).astype(np.float64)
        B = np.zeros((N3, N3), np.float64)
        for tt in range(N3):
            B[lo[tt]:hi[tt] + 1, tt] = 1.0 / cnt[tt]
        B -= np.eye(N3)
        band[:, gi * 5 + 0, :] = B[128:256, 128:256]
        band[:, gi * 5 + 1, :] = B[0:128, 0:128]
        band[:, gi * 5 + 2, :] = B[256:384, 256:384]
        band[:, gi * 5 + 3, :] = B[0:128, 128:256]
        band[:, gi * 5 + 4, :] = B[128:256, 0:128]
    c["band"] = band.astype(bf)
    tok = np.arange(NLAT)
    row = (tok // 64).astype(np.float32)
    col = (tok % 64).astype(np.float32)
    inv = (np.float32(10000.0) ** (-np.arange(0, 64, 2, dtype=np.float32) / np.float32(64))).astype(np.float32)
    angr = np.stack([row[:, None] * inv[None, :], col[:, None] * inv[None, :]], 1).astype(np.float32)
    c["ropeC"] = np.cos(angr).reshape(NLAT, 64).astype(np.float32)
    c["ropeS"] = np.sin(angr).reshape(NLAT, 64).astype(np.float32)
    c["identF"] = np.eye(128, dtype=np.float32)
    c["identB"] = np.eye(128, dtype=np.float32).astype(bf)
    c["onesB"] = np.ones((128, 128), np.float32).astype(bf)
    return c


_CONST_CACHE = {}


def get_consts():
    if not _CONST_CACHE:
        _CONST_CACHE.update(_consts())
    return _CONST_CACHE


class _Stop(Exception):
    pass


def build_program(debug=False, stop_after=None, wseq=None):
    nc = bass.Bass("TRN2", target_bir_lowering=False)
    S = Sched()

    marks = []

    def mark(name):
        marks.append((name, len(S.streams['pe'])))
        if stop_after is not None and name == stop_after:
            raise _Stop()
    stack = ExitStack()

    def din(name, shape, dt=F32):
        return nc.dram_tensor(name, list(shape), dt, kind="ExternalInput").ap()

    x_in = din("x", [NLAT, D])
    ctx_in = din("ctx", [NCTX, D])
    cvec = din("cvec", [128, 32])
    ada_w = din("ada_w", [DEPTH, D, 3 * D])
    ada_b2 = din("ada_b2", [DEPTH, 2, 3 * D])
    ngcol = din("ngcol", [128, 32])
    fngb = din("fngb", [128, D])
    w_in = din("w_in", [DEPTH, D, INW])
    w_out = din("w_out", [DEPTH, D, D])
    gains = din("gains", [DEPTH, 128, 256])
    pool_w = din("pool_w", [DEPTH, 4, 128, 128])
    pscol = din("pscol", [128, 8])
    four_w = din("fourier_w", [DEPTH, 4, 128, 128])
    dftC = din("dftC", [NLAT, NLAT], BF16)
    dftS = din("dftS", [NLAT, NLAT], BF16)
    dft256_d = din("dft256", [128, 2, 2, NCTX], BF16)
    ccs_d = din("ccs", [128, 4, 128], BF16)
    band_d = din("band", [128, 20, 128], BF16)
    ropeC = din("ropeC", [NLAT, 64])
    ropeS = din("ropeS", [NLAT, 64])
    identF_d = din("identF", [128, 128])
    identB_d = din("identB", [128, 128], BF16)
    onesB_d = din("onesB", [128, 128], BF16)
    out_d = nc.dram_tensor("out", [NLAT, D], F32, kind="ExternalOutput").ap()
    x1_d = nc.dram_tensor("x1s", [NLAT, D], F32, kind="ExternalOutput" if debug else "Internal").ap()
    xc1_d = nc.dram_tensor("xc1s", [NCTX, D], F32, kind="ExternalOutput" if debug else "Internal").ap()
    wc_d = nc.dram_tensor("wcache", [DEPTH, 11, 128, 16 * 512], BF16).ap()
    modrow_h = nc.dram_tensor("modrow", [DEPTH, 2, 3 * D], F32)
    modrow_d = modrow_h.ap()

    SB = stack.enter_context(nc.sbuf_tensor("SB", [128, SB_BYTES // 4], F32))
    cur = [0]

    def alloc(nbytes, align=512):
        o = (cur[0] + align - 1) // align * align
        cur[0] = o + nbytes
        assert cur[0] <= SB_BYTES, ("SBUF overflow", cur[0])
        return o

    def vf(off, n):
        return SB[:, off // 4: off // 4 + n]

    def vb(off, n):
        return SB[:, off // 4: off // 4 + (n + 1) // 2].bitcast(BF16)[:, 0:n]

    def r3(v, b):
        return v.rearrange("p (a b) -> p a b", b=b)

    o_hT = alloc(16 * NTOK * 2)
    hT = r3(vb(o_hT, 16 * NTOK), NTOK)
    o_KT = alloc(2 * NTOK * 2)
    KT = r3(vb(o_KT, 2 * NTOK), NTOK)
    o_V = alloc(18 * 256 * 2)
    Vv = r3(vb(o_V, 18 * 256), 256)
    o_ReT = alloc(4 * NTOK * 2)
    ReT = r3(vb(o_ReT, 4 * NTOK), NTOK)
    o_wb = [alloc(16 * 512 * 2), alloc(16 * 512 * 2)]
    wbuf = [r3(vb(o, 16 * 512), 512) for o in o_wb]
    identF = vf(alloc(512), 128)
    identB = vb(alloc(256, 256), 128)
    onesB = vb(alloc(256, 256), 128)
    band = r3(vb(alloc(20 * 256), 20 * 128), 128)
    dft256 = vb(alloc(2048), 1024).rearrange("p (m j k) -> p m j k", m=2, j=2)
    ccs = r3(vb(alloc(1024), 512), 128)
    poolw = r3(vb(alloc(1024), 512), 128)
    fourw = r3(vb(alloc(1024), 512), 128)
    gains_s = vf(alloc(1024), 256)
    o_small = alloc(2048)
    sc3 = r3(vb(o_small, 32), 2)
    cv = vf(o_small + 128, 32)
    cvt = vf(o_small + 256, 32)
    ngc = vf(o_small + 384, 32)
    psc = vf(o_small + 512, 8)
    mcols = vf(o_small + 576, 64).rearrange("p (w k j) -> p w k j", w=2, k=2)
    mhalf = vf(o_small + 832, 16)
    ssb = vf(alloc(64, 512), 16)
    rsb = vf(alloc(64, 512), 16)
    ssq = vf(alloc(64, 512), 16)
    rfin = vf(alloc(64, 512), 4)
    o_gp = [alloc(2048), alloc(2048)]
    gpiece = [vf(o, 512) for o in o_gp]
    o_rope = alloc(1024)
    ropec = [vf(o_rope, 64), vf(o_rope + 512, 64)]
    ropes = [vf(o_rope + 256, 64), vf(o_rope + 768, 64)]
    o_R = alloc(0, 1024)
    R_BYTES = SB_BYTES - o_R

    class Arena:
        def __init__(self):
            self.c = 0

        def a(self, nbytes, align=512):
            o = (self.c + align - 1) // align * align
            self.c = o + nbytes
            assert self.c <= R_BYTES, ("scratch overflow", self.c, R_BYTES)
            return o_R + o

    banks = [stack.enter_context(nc.psum_tensor("pb%d" % i, [128, 512], F32)) for i in range(8)]

    def bankb(i):
        return banks[i][:, :].bitcast(BF16)

    rr = {}

    def nxt(cls, lst):
        i = rr.get(cls, 0)
        rr[cls] = i + 1
        return lst[i % len(lst)]

    def dma(eng, out, in_, slot, R=None, W=None):
        S.op(eng, lambda e, o=out, i=in_: e.dma_start(out=o, in_=i), R=R if R is not None else [in_],
             W=W if W is not None else [out], dma_slot=slot + "_" + eng)

    def mm(out, lhsT, rhs, start, stop, extraR=()):
        S.op("pe", lambda e, o=out, l=lhsT, r=rhs, s=start, t=stop: e.matmul(o, l, r, start=s, stop=t),
             R=[lhsT, rhs] + list(extraR), W=[out])

    def tr(out, in_, ident):
        S.op("pe", lambda e, o=out, i=in_, d=ident: e.transpose(o, i, d), R=[in_, ident], W=[out])

    def act(out, in_, func, bias=None, scale=None, accum=None, extraR=()):
        kw = {}
        Rl = [in_] + list(extraR)
        Wl = [out]
        if bias is not None:
            kw["bias"] = bias
            if not isinstance(bias, float):
                Rl.append(bias)
        if scale is not None:
            kw["scale"] = scale
            if not isinstance(scale, float):
                Rl.append(scale)
        if accum is not None:
            kw["accum_out"] = accum
            Wl.append(accum)
        S.op("act", lambda e, o=out, i=in_, f=func, k=kw: e.activation(o, i, f, **k), R=Rl, W=Wl)

    def ts(eng, out, in0, s1, s2, op0, op1=None):
        Rl = [in0] + [s for s in (s1, s2) if s is not None and not isinstance(s, float)]
        if op1 is None:
            S.op(eng, lambda e, o=out, i=in0, a=s1, p=op0: e.tensor_scalar(o, i, a, None, p), R=Rl, W=[out])
        else:
            S.op(eng, lambda e, o=out, i=in0, a=s1, b=s2, p=op0, q=op1: e.tensor_scalar(o, i, a, b, p, q), R=Rl, W=[out])

    def tt(eng, out, in0, in1, op):
        S.op(eng, lambda e, o=out, a=in0, b=in1, p=op: e.tensor_tensor(o, a, b, p), R=[in0, in1], W=[out])

    def stt(out, in0, scalar, in1, op0, op1):
        Rl = [in0, in1] + ([scalar] if not isinstance(scalar, float) else [])
        S.op("dve", lambda e, o=out, a=in0, s=scalar, b=in1, p=op0, q=op1: e.scalar_tensor_tensor(o, a, s, b, p, q),
             R=Rl, W=[out])

    def cp(eng, out, in_):
        if eng == "act":
            S.op("act", lambda e, o=out, i=in_: e.activation(o, i, AF.Copy), R=[in_], W=[out])
        else:
            S.op(eng, lambda e, o=out, i=in_: e.tensor_copy(o, i), R=[in_], W=[out])

    def recip(out, in_):
        S.op("dve", lambda e, o=out, i=in_: e.reciprocal(o, i), R=[in_], W=[out])

    def rstd_from_ss(ss, rs, n, inv_n):
        ts("pool", rs, ss, float(inv_n), float(EPS), ALU.mult, ALU.add)
        tt("pool", rs, rs, mhalf[:, 0:n], ALU.pow)

    def ap4(v, off_el, dims):
        ps = list(v.ap)[0][0]
        return bass.AP(v.tensor, v.offset + off_el, [[ps, 128]] + [list(d) for d in dims])

    wcount = [0]

    wrec = []
    wseen = set()

    wpending = []

    def issue_w(k, ent):
        src3, eng, ckey = ent
        buf = wbuf[k % 2]
        if ckey is not None and ckey in wseen:
            lyr, idx = ckey
            dma("pool", buf.rearrange("p a b -> p (a b)"), wc_d[lyr, idx], "w%d" % (k % 2), R=[("D", "wc", lyr, idx)])
            return
        dma(eng, buf[:, :, :], src3, "w%d" % (k % 2))
        if ckey is not None:
            wseen.add(ckey)
            wpending.append((k, ckey))

    def flush_w_stores(upto):
        while wpending and wpending[0][0] <= upto:
            k, (lyr, idx) = wpending.pop(0)
            dma("sp", wc_d[lyr, idx], wbuf[k % 2].rearrange("p a b -> p (a b)"), "wst", W=[("D", "wc", lyr, idx)])

    def load_w(src3, eng="pool", ckey=None):
        k = wcount[0]
        wcount[0] += 1
        wrec.append((src3, eng, ckey))
        if wseq is None:
            dma(eng, wbuf[k % 2][:, :, :], src3, "w%d" % (k % 2))
        else:
            if k == 0:
                issue_w(0, wseq[0])
            flush_w_stores(k)
            if k + 1 < len(wseq):
                issue_w(k + 1, wseq[k + 1])
        return wbuf[k % 2]

    def wsrc(wd, l, c0, ncol=512):
        return wd[l, :, c0:c0 + ncol].rearrange("(j p) n -> p j n", p=128)

    for dst, src in ((identF, identF_d), (identB, identB_d), (onesB, onesB_d),
                     (band, band_d), (dft256, dft256_d), (ccs, ccs_d),
                     (cv, cvec), (ngc, ngcol), (psc, pscol)):
        dma("sp", dst, src, "const")
    S.op("pool", lambda e: e.memset(mhalf, -0.5), W=[mhalf])
    act(cvt, cv, AF.Tanh, scale=0.5)
    stt(cvt, cvt, 1.0, cv, ALU.add, ALU.mult)
    ts("dve", sc3.rearrange("p a b -> p (a b)"), cvt, 0.5, None, ALU.mult)

    xsrc = {("lat", 0): x_in, ("ctx", 0): ctx_in, ("lat", 1): x1_d, ("ctx", 1): xc1_d}
    xdst = {("lat", 0): x1_d, ("ctx", 0): xc1_d, ("lat", 1): out_d}

    def xkey(tensor, tile, cb):
        return ("D", tensor.tensor.name, tile, cb)

    def xkeys(tensor, tile):
        return [xkey(tensor, tile, cb) for cb in range(4)]

    deferred = []
    try:
      for l in range(DEPTH):
          last = l == DEPTH - 1
          dma("pool", poolw, pool_w[l].rearrange("g c d -> c g d"), "lw")
          dma("pool", fourw, four_w[l].rearrange("g c d -> c g d"), "lw")
          dma("sp", gains_s, gains[l], "lw2")

          ar = Arena()
          o_mp = [ar.a(2048), ar.a(2048)]
          o_ab = [ar.a(2048), ar.a(2048)]
          colsP = banks[1][:, 0:96].rearrange("p (c w) -> p c w", w=2)
          for nb in range(12):
              wv = load_w(wsrc(ada_w, l, nb * 512))
              abp = vf(o_ab[nb % 2], 512)[0:2, :]
              mp = vf(o_mp[nb % 2], 512)[0:2, :]
              dma("sp", abp, ada_b2[l, :, nb * 512:(nb + 1) * 512], "adab%d" % (nb % 2))
              acc = banks[0][0:2, :]
              for j in range(16):
                  mm(acc, sc3[:, j, :], wv[:, j, :], j == 0, j == 15)
              tt("dve", mp, acc, abp, ALU.add)
              dma("sp", modrow_d[l, :, nb * 512:(nb + 1) * 512], mp, "modrow",
                  W=[("D", "modrow", l, nb)])
              for q in range(4):
                  tr(colsP[:, nb * 4 + q, :], mp[:, q * 128:(q + 1) * 128], identF[0:2, 0:2])
          for which in range(2):
              cp("dve", mcols[:, which, 1, :], colsP[:, 0:16, which])
              stt(mcols[:, which, 0, :], colsP[:, 16:32, which], 1.0, ngc[:, l * 16:(l + 1) * 16], ALU.add, ALU.mult)

          mark('ada%d' % l)
          ar = Arena()
          o_xt = [ar.a(8192), ar.a(8192)]
          o_junk = ar.a(4096)
          junk = vb(o_junk, 2048)
          seqs = [("ctx", t) for t in range(2)] + [("lat", t) for t in range(16)]
          def p1_a(ti):
              which, t = seqs[ti]
              src = xsrc[(which, l)]
              xt = vf(o_xt[ti % 2], 2048)
              dma("sp", xt, src[t * 128:(t + 1) * 128, :], "xt%d" % (ti % 2),
                  R=xkeys(src, t) if l > 0 else [])
              ss = ssb[:, 12 + ti % 2:13 + ti % 2]
              act(junk, xt, AF.Square, accum=ss)
              rs = rsb[:, 12 + ti % 2:13 + ti % 2]
              rstd_from_ss(ss, rs, 1, 1.0 / D)
              ts("dve", xt, xt, rs, None, ALU.mult)

          def p1_b(ti):
              which, t = seqs[ti]
              wi = 0 if which == "lat" else 1
              xt = vf(o_xt[ti % 2], 2048)
              gt = ti
              pbs = [0, 1, 2, 3] if ti % 2 == 0 else [4, 5, 6, 7]
              for jb in range(4):
                  for j in range(jb * 4, jb * 4 + 4):
                      pb = banks[pbs[jb]][:, (j % 4) * 128:(j % 4 + 1) * 128]
                      tr(pb, xt[:, j * 128:(j + 1) * 128], identF)
                  for j in range(jb * 4, jb * 4 + 4):
                      pb = banks[pbs[jb]][:, (j % 4) * 128:(j % 4 + 1) * 128]
                      dst = hT[:, j, gt * 128:(gt + 1) * 128]
                      if jb % 2 == 0:
                          act(dst, pb, AF.Identity, bias=mcols[:, wi, 1, j:j + 1], scale=mcols[:, wi, 0, j:j + 1])
                      else:
                          ts("dve", dst, pb, mcols[:, wi, 0, j:j + 1], mcols[:, wi, 1, j:j + 1], ALU.mult, ALU.add)

          p1_a(0)
          for ti in range(len(seqs)):
              if ti + 1 < len(seqs):
                  p1_a(ti + 1)
              p1_b(ti)

          mark('p1_%d' % l)
          ar = Arena()
          o_kx = ar.a(1024)
          o_ta = ar.a(512)
          o_tb = ar.a(512)
          o_kr = [ar.a(512), ar.a(512)]
          junk = vb(ar.a(512), 256)
          wv = load_w(wsrc(w_in, l, OFF_K))
          def kv_mm(gt):
              pb = banks[gt % 2]
              for j in range(16):
                  mm(pb[:, :], hT[:, j, gt * 128:(gt + 1) * 128], wv[:, j, :], j == 0, j == 15)

          kv_mm(0)
          for gt in range(18):
              is_lat = gt >= 2
              pb = banks[gt % 2]
              if gt + 1 < 18:
                  kv_mm(gt + 1)
              so = 2 * (gt % 2)
              ss = ssb[:, so:so + 2]
              rs = rsb[:, so:so + 2]
              for h in range(2):
                  act(junk[:, 0:128], pb[:, h * 128:(h + 1) * 128], AF.Square, accum=ssb[:, so + h:so + h + 1])
              rstd_from_ss(ss, rs, 2, 1.0 / 128)
              kr = vb(o_kr[gt % 2], 256)
              kx = vf(o_kx, 256)
              if is_lat:
                  t = gt - 2
                  rc, rsn = ropec[gt % 2], ropes[gt % 2]
                  dma("sp", rc, ropeC[t * 128:(t + 1) * 128, :], "rope%d" % (gt % 2))
                  dma("sp", rsn, ropeS[t * 128:(t + 1) * 128, :], "rope%d" % (gt % 2))
              for h in range(2):
                  stt(kx[:, h * 128:(h + 1) * 128] if is_lat else kr[:, h * 128:(h + 1) * 128],
                      pb[:, h * 128:(h + 1) * 128], rsb[:, so + h:so + h + 1], gains_s[:, 128:256], ALU.mult, ALU.mult)
              cp("act", Vv[:, gt, :], pb[:, 256:512])
              if is_lat:
                  rope(S, tt, ap4, kx, kr, vf(o_ta, 128), vf(o_tb, 128), rc, rsn, 2)
              kb = nxt("B", [2, 3])
              pbt = bankb(kb)
              for h in range(2):
                  tr(pbt[:, h * 128:(h + 1) * 128], kr[:, h * 128:(h + 1) * 128], identB)
              for h in range(2):
                  cp("dve", KT[:, h, gt * 128:(gt + 1) * 128], pbt[:, h * 128:(h + 1) * 128])

          mark('g1_%d' % l)
          ar = Arena()
          o_tm = ar.a(18 * 512 * 2)
          tm = r3(vb(o_tm, 18 * 512), 512)
          o_pT2 = ar.a(2 * 4 * 512 * 2)
          pT2 = vb(o_pT2, 4096).rearrange("p (m g k) -> p m g k", m=2, g=4)
          wv = load_w(wsrc(w_in, l, OFF_FOUR))
          ftiles = list(range(18)) if not last else list(range(2, 18))
          for gt in ftiles:
              pb = banks[nxt("A", [0, 1])]
              for j in range(16):
                  mm(pb[:, :], hT[:, j, gt * 128:(gt + 1) * 128], wv[:, j, :], j == 0, j == 15)
              cp("act" if gt % 2 == 0 else "dve", tm[:, gt, :], pb[:, :])
          if not last:
              for mat in range(2):
                  for g in range(4):
                      pb = banks[nxt("A", [0, 1])]
                      for j in range(2):
                          mm(pb[:, 0:256], tm[:, j, g * 128:(g + 1) * 128], dft256[:, mat, j, :], j == 0, j == 1)
                      cp("act" if g % 2 == 0 else "dve", pT2[:, mat, g, 0:256], pb[:, 0:256])
              for g in range(4):
                  pb = banks[nxt("C", [4, 5])]
                  mm(pb[:, 0:256], ccs[:, 2, :], pT2[:, 0, g, 0:256], True, False)
                  mm(pb[:, 0:256], ccs[:, 3, :], pT2[:, 1, g, 0:256], False, True)
                  cp("act" if g % 2 == 0 else "dve", ReT[:, g, 0:256], pb[:, 0:256])
          for kb in range(4):
              for mat, dsrc in enumerate((dftC, dftS)):
                  dv = load_w(dsrc[:, kb * 512:(kb + 1) * 512].rearrange("(j p) k -> p j k", p=128), eng="pool")
                  for g in range(4):
                      pb = banks[nxt("A", [0, 1, 2, 3])]
                      for j in range(16):
                          mm(pb[:, :], tm[:, 2 + j, g * 128:(g + 1) * 128], dv[:, j, :], j == 0, j == 15)
                      cp("act" if g % 2 == 0 else "dve", pT2[:, mat, g, :], pb[:, :])
              for g in range(4):
                  pb = banks[nxt("C", [4, 5])]
                  mm(pb[:, :], ccs[:, 0, :], pT2[:, 0, g, :], True, False)
                  mm(pb[:, :], ccs[:, 1, :], pT2[:, 1, g, :], False, True)
                  cp("act" if g % 2 == 0 else "dve", ReT[:, g, 256 + kb * 512:256 + (kb + 1) * 512], pb[:, :])

          mark('g3_%d' % l)
          groups = ([] if last else [("ctx", 0, NCTX, 0)]) + [("lat", 256 + g * 512, 512, g * 4) for g in range(4)]
          for (which, tok0, ntok, T0) in groups:
              wi = 0 if which == "lat" else 1
              is_lat = which == "lat"
              NT = 16 if is_lat else 2
              ntile = ntok // 128
              ktiles = list(range(18)) if is_lat else [0, 1]
              src_x = xsrc[(which, l)]
              dst_x = xdst[(which, l)]
              ar = Arena()
              o_q = ar.a(6 * 1024, 1024)
              QT = r3(vb(o_q, 4 * ntok), ntok)
              qx = vf(o_q + 4096, 512)
              up_tm = r3(vb(o_q, 6 * 512), 512)
              mgT = r3(vb(ar.a(16 * ntok * 2), 16 * ntok), ntok)
              PT = [vb(ar.a(ntok * 2), ntok) for _ in range(3)]
              tmpAf = [vf(ar.a(2048), 512) for _ in range(2)]
              tmpA = [v[:, 0:ntok] for v in tmpAf]
              tmpO = [vf(ar.a(2048), 512)[:, 0:ntok] for _ in range(2)]
              qr = [vb(ar.a(1024), 512) for _ in range(2)]
              ta = vf(ar.a(1024), 256)
              tb = vf(ar.a(1024), 256)
              plT = [vb(ar.a(ntok * 2), ntok)] * 2
              xp = [vf(ar.a(2048), 512) for _ in range(2)]

              def gate_chunk(fc, wg, branch, branch_in_psum):
                  gb = banks[nxt("B", [2, 3])]
                  hcol = (fc % 4) * 128
                  for j in range(16):
                      mm(gb[:, 0:ntok], wg[:, j, hcol:hcol + 128], hT[:, j, tok0:tok0 + ntok], j == 0, j == 15)
                  th = tmpA[fc % 2]
                  act(th, gb[:, 0:ntok], AF.Tanh, scale=0.5)
                  stt(th, th, 1.0, gb[:, 0:ntok], ALU.add, ALU.mult)
                  stt(mgT[:, fc, :], th, 0.5, branch, ALU.mult, ALU.mult)

              for hb in range(2):
                  wq = load_w(wsrc(w_in, l, OFF_Q + hb * 512), ckey=(l, hb))
                  def q_mm(tti):
                      pb = banks[tti % 2]
                      c0 = tok0 + tti * 128
                      for j in range(16):
                          mm(pb[:, :], hT[:, j, c0:c0 + 128], wq[:, j, :], j == 0, j == 15)

                  q_mm(0)
                  for tti in range(ntile):
                      pb = banks[tti % 2]
                      if tti + 1 < ntile:
                          q_mm(tti + 1)
                      so = 4 + 4 * (tti % 2)
                      for h in range(4):
                          act(PT[2][:, 0:128], pb[:, h * 128:(h + 1) * 128], AF.Square, accum=ssb[:, so + h:so + h + 1])
                      mark('q1')
                      rstd_from_ss(ssb[:, so:so + 4], rsb[:, so:so + 4], 4, 1.0 / 128)
                      mark('q2')
                      qrv = qr[tti % 2]
                      if is_lat:
                          t = T0 + tti
                          rc, rsn = ropec[tti % 2], ropes[tti % 2]
                          dma("sp", rc, ropeC[t * 128:(t + 1) * 128, :], "rope%d" % (tti % 2))
                          dma("sp", rsn, ropeS[t * 128:(t + 1) * 128, :], "rope%d" % (tti % 2))
                      for h in range(4):
                          stt(qx[:, h * 128:(h + 1) * 128] if is_lat else qrv[:, h * 128:(h + 1) * 128],
                              pb[:, h * 128:(h + 1) * 128], rsb[:, so + h:so + h + 1], gains_s[:, 0:128], ALU.mult, ALU.mult)
                      if is_lat:
                          rope(S, tt, ap4, qx, qrv, ta, tb, rc, rsn, 4)
                      mark('q3')
                      pbt = bankb(nxt("B", [2, 3]))
                      for h in range(4):
                          tr(pbt[:, h * 128:(h + 1) * 128], qrv[:, h * 128:(h + 1) * 128], identB)
                      mark('q4')
                      cp("dve" if tti % 2 == 0 else "act", QT[:, :, tti * 128:(tti + 1) * 128],
                         pbt[:, 0:512].rearrange("p (h d) -> p h d", h=4))
                      mark('q5')
                  mark('qdone')
                  wg = load_w(wsrc(w_in, l, OFF_GATE + hb * 512), ckey=(l, 2 + hb))
                  for h in range(4):
                      fc = hb * 4 + h
                      Ob, Lb = banks[6], banks[7]
                      nk = len(ktiles)

                      def s_mm(ki):
                          kt = ktiles[ki]
                          sb = banks[(4, 5, 0, 1)[ki % 4]]
                          mm(sb[:, 0:ntok], KT[:, hb, kt * 128:(kt + 1) * 128], QT[:, h, :], True, True)
                          act(PT[ki % 3], sb[:, 0:ntok], AF.Exp, scale=float(128.0 ** -0.5))

                      def pv_mm(ki):
                          kt = ktiles[ki]
                          mm(Ob[:, 0:ntok], Vv[:, kt, hb * 128:(hb + 1) * 128], PT[ki % 3], ki == 0, ki == nk - 1)
                          mm(Lb[:, 0:ntok], onesB, PT[ki % 3], ki == 0, ki == nk - 1)

                      s_mm(0)
                      if nk > 1:
                          s_mm(1)
                      for ki in range(nk):
                          if ki + 2 < nk:
                              s_mm(ki + 2)
                          pv_mm(ki)
                      mark('attA')
                      tO = tmpO[fc % 2]
                      recip(tO, Lb[:, 0:ntok])
                      tt("dve", tO, Ob[:, 0:ntok], tO, ALU.mult)
                      mark('attB')
                      gate_chunk(fc, wg, tO, False)
                      mark('attC')

              mark('att_%d_%s_%d' % (l, which, T0))
              while deferred:
                  deferred.pop(0)()
              wp = load_w(wsrc(w_in, l, OFF_POOL), ckey=(l, 6))
              Tlo = max(T0 - 1, 0)
              Thi = min(T0 + ntile, NT - 1)
              seq_tok0 = 256 if is_lat else 0
              for T in range(Tlo, Thi + 1):
                  pb = banks[nxt("A", [0, 1])]
                  c0 = seq_tok0 + T * 128
                  for j in range(16):
                      mm(pb[:, :], hT[:, j, c0:c0 + 128], wp[:, j, :], j == 0, j == 15)
                  cp("act" if T % 2 == 0 else "dve", up_tm[:, T - Tlo, :], pb[:, :])
              wg = load_w(wsrc(w_in, l, OFF_GATE + 2 * 512), ckey=(l, 4))
              for g in range(4):
                  fc = 8 + g
                  pb = banks[nxt("C", [4, 5])]
                  for tti in range(ntile):
                      T = T0 + tti
                      terms = []
                      if T > 0:
                          terms.append((T - 1, 3))
                      terms.append((T, 1 if T == 0 else (2 if T == NT - 1 else 0)))
                      if T < NT - 1:
                          terms.append((T + 1, 4))
                      for i, (Tn, kind) in enumerate(terms):
                          mm(pb[:, tti * 128:(tti + 1) * 128], up_tm[:, Tn - Tlo, g * 128:(g + 1) * 128],
                             band[:, g * 5 + kind, :], i == 0, i == len(terms) - 1)
                  pl = plT[g % 2]
                  cp("act", pl, pb[:, 0:ntok])
                  yb = banks[nxt("D", [6, 7])]
                  mm(yb[:, 0:ntok], poolw[:, g, :], pl, True, True)
                  tO = tmpO[fc % 2]
                  ts("dve", tO, yb[:, 0:ntok], psc[:, l * 4 + g:l * 4 + g + 1], None, ALU.mult)
                  gate_chunk(fc, wg, tO, False)

              wg = load_w(wsrc(w_in, l, OFF_GATE + 3 * 512), ckey=(l, 5))
              for g in range(4):
                  fc = 12 + g
                  yb = banks[nxt("D", [6, 7])]
                  mm(yb[:, 0:ntok], fourw[:, g, :], ReT[:, g, tok0:tok0 + ntok], True, True)
                  gate_chunk(fc, wg, yb[:, 0:ntok], True)

              mark('four_%d_%s_%d' % (l, which, T0))
              gate_c0 = 2 * D
              for cb in range(4):
                  wo = load_w(wsrc(w_out, l, cb * 512), ckey=(l, 7 + cb))
                  gp = gpiece[cb % 2]
                  dma("sp", gp, modrow_d[l, wi:wi + 1, gate_c0 + cb * 512:gate_c0 + (cb + 1) * 512].partition_broadcast(128).rearrange("p a n -> p (a n)"),
                      "gp%d" % (cb % 2), R=[("D", "modrow", l, 8 + cb)])
                  for tti in range(ntile):
                      T = T0 + tti
                      xpv = xp[nxt("xp", [0, 1])]
                      slot = "xp%d" % ((rr["xp"] - 1) % 2)
                      dma("sp", xpv, src_x[T * 128:(T + 1) * 128, cb * 512:(cb + 1) * 512], slot,
                          R=[xkey(src_x, T, cb)] if l > 0 else [])
                      pb = banks[nxt("W", [0, 1, 4, 5])]
                      for fc in range(16):
                          mm(pb[:, :], mgT[:, fc, tti * 128:(tti + 1) * 128], wo[:, fc, :], fc == 0, fc == 15)
                      tmpx = tmpAf[tti % 2]
                      tt("dve", tmpx, pb[:, :], gp, ALU.mult)
                      tt("dve", xpv, tmpx, xpv, ALU.add)
                      if last:
                          act(tmpAf[(tti + 1) % 2], xpv, AF.Square, accum=ssq[:, tti * 4 + cb:tti * 4 + cb + 1])
                      dma("sp", dst_x[T * 128:(T + 1) * 128, cb * 512:(cb + 1) * 512], xpv, "xo%d" % ((rr["xp"] - 1) % 2),
                          W=[xkey(dst_x, T, cb)])
              mark('wout_%d_%s_%d' % (l, which, T0))
              if last:
                  for tti in range(ntile):
                      S.op("dve", lambda e, o=rfin[:, tti:tti + 1], i=ssq[:, tti * 4:(tti + 1) * 4]:
                           e.tensor_reduce(o, i, mybir.AxisListType.X, ALU.add),
                           R=[ssq[:, tti * 4:(tti + 1) * 4]], W=[rfin[:, tti:tti + 1]])
                      rstd_from_ss(rfin[:, tti:tti + 1], rfin[:, tti:tti + 1], 1, 1.0 / D)

                  def final_pass(T0=T0, ntile=ntile, xp=xp):
                      for cb in range(4):
                          fgv = gpiece[cb % 2]
                          dma("sp", fgv, fngb[:, cb * 512:(cb + 1) * 512], "gp%d" % (cb % 2))
                          for tti in range(ntile):
                              T = T0 + tti
                              xpv = xp[nxt("xp", [0, 1])]
                              sl = (rr["xp"] - 1) % 2
                              dma("sp", xpv, out_d[T * 128:(T + 1) * 128, cb * 512:(cb + 1) * 512], "xp%d" % sl,
                                  R=[xkey(out_d, T, cb)])
                              stt(xpv, xpv, rfin[:, tti:tti + 1], fgv, ALU.mult, ALU.mult)
                              dma("sp", out_d[T * 128:(T + 1) * 128, cb * 512:(cb + 1) * 512], xpv, "xo%d" % sl,
                                  W=[xkey(out_d, T, cb)])

                  deferred.append(final_pass)
      while deferred:
          deferred.pop(0)()

    except _Stop:
        pass

    allout = [xkey(out_d, T, cb) for T in range(16) for cb in range(4)]
    if debug:
        allout += [xkey(x1_d, T, cb) for T in range(16) for cb in range(4)]
        allout += [xkey(xc1_d, T, cb) for T in range(2) for cb in range(4)]
    S.op("sp", lambda e: e.nop(), R=allout)
    S.emit(nc, stack)
    stack.close()
    nc._wrec = wrec
    nc._marks = marks
    return nc


def rope(S, tt, ap4, xin, xout, ta, tb, rc, rsn, H):
    dims = [[128, H], [64, 2], [1, 32]]
    x1 = ap4(xin, 0, dims)
    x2 = ap4(xin, 32, dims)
    o1 = ap4(xout, 0, dims)
    o2 = ap4(xout, 32, dims)
    tdims = [[64, H], [32, 2], [1, 32]]
    a = ap4(ta, 0, tdims)
    b = ap4(tb, 0, tdims)
    cdims = [[0, H], [32, 2], [1, 32]]
    c = ap4(rc, 0, cdims)
    s = ap4(rsn, 0, cdims)
    tt("dve", a, x1, c, ALU.mult)
    tt("dve", b, x2, s, ALU.mult)
    tt("dve", o1, a, b, ALU.subtract)
    tt("dve", a, x2, c, ALU.mult)
    tt("dve", b, x1, s, ALU.mult)
    tt("dve", o2, a, b, ALU.add)


def build_two_pass(debug=False, stop_after=None):
    rec = build_program(debug=debug, stop_after=stop_after)._wrec
    return build_program(debug=debug, stop_after=stop_after, wseq=rec)


_NC_CACHE = {}


def make_in_maps(x, c, ctx, c_ctx, ada_w, ada_b, norm_g, w_in, q_norm_g, k_norm_g,
                 pool_w, pool_scale, fourier_w, w_out, final_norm_g, cores):
    f = lambda a: np.ascontiguousarray(np.asarray(a, dtype=np.float32))
    x, c, ctx, c_ctx = f(x), f(c), f(ctx), f(c_ctx)
    ada_w, ada_b, norm_g, w_in = f(ada_w), f(ada_b), f(norm_g), f(w_in)
    q_norm_g, k_norm_g, pool_w, pool_scale = f(q_norm_g), f(k_norm_g), f(pool_w), f(pool_scale)
    fourier_w, w_out, final_norm_g = f(fourier_w), f(w_out), f(final_norm_g)
    consts = get_consts()
    shared = dict(consts)
    shared["ada_w"] = ada_w
    shared["ada_b2"] = np.ascontiguousarray(np.repeat(ada_b[:, None, :], 2, axis=1))
    shared["ngcol"] = np.ascontiguousarray(norm_g.reshape(DEPTH, 16, 128).transpose(2, 0, 1).reshape(128, 32))
    shared["fngb"] = np.ascontiguousarray(np.broadcast_to(final_norm_g[None, :], (128, D)))
    shared["w_in"] = w_in
    shared["w_out"] = w_out
    g = np.concatenate([q_norm_g, k_norm_g], axis=1)
    shared["gains"] = np.ascontiguousarray(np.broadcast_to(g[:, None, :], (DEPTH, 128, 256)))
    shared["pool_w"] = pool_w
    shared["pscol"] = np.ascontiguousarray(pool_scale.reshape(DEPTH, 4, 128).transpose(2, 0, 1).reshape(128, 8))
    shared["fourier_w"] = fourier_w
    cc = c_ctx.reshape(16, 128).T
    maps = []
    for b in cores:
        m = dict(shared)
        m["x"] = x[b]
        m["ctx"] = ctx[b]
        cb = c[b].reshape(16, 128).T
        m["cvec"] = np.ascontiguousarray(np.stack([cb, cc], axis=2).reshape(128, 32))
        maps.append(m)
    return maps


def kernel(x, c, ctx, c_ctx, ada_w, ada_b, norm_g, w_in, q_norm_g, k_norm_g,
           pool_w, pool_scale, fourier_w, w_out, final_norm_g):
    if "nc" not in _NC_CACHE:
        _NC_CACHE["nc"] = build_two_pass(debug=False)
    nc = _NC_CACHE["nc"]
    maps = make_in_maps(x, c, ctx, c_ctx, ada_w, ada_b, norm_g, w_in, q_norm_g, k_norm_g,
                        pool_w, pool_scale, fourier_w, w_out, final_norm_g, list(range(8)))
    res = run_bass_kernel_spmd(nc, maps, core_ids=list(range(8)))
    out = np.stack([np.asarray(r["out"], dtype=np.float32) for r in res.results], axis=0)
    return out
```

```python
import math
from contextlib import ExitStack
import numpy as np
import ml_dtypes
import concourse.bass as bass
import concourse.mybir as mybir
from concourse.bass_utils import run_bass_kernel_spmd

F32 = mybir.dt.float32
BF16 = mybir.dt.bfloat16
AF = mybir.ActivationFunctionType
ALU = mybir.AluOpType

D = 2048
NLAT = 2048
NCTX = 256
NTOK = NLAT + NCTX
DEPTH = 2
INW = 4608
OFF_Q, OFF_K, OFF_V, OFF_POOL, OFF_FOUR, OFF_GATE = 0, 1024, 1280, 1536, 2048, 2560
EPS = 1e-6
import os
KVAR = int(os.environ.get('KVAR', '0'))
GR = 512
SB_BYTES = 207 * 1024


class Sched:
    ENGS = ("pe", "act", "dve", "pool", "sp")

    def __init__(self):
        self.streams = {e: [] for e in self.ENGS}
        self.lastw = {}
        self.readers = {}
        self.dma_slots = {}

    @staticmethod
    def keys(ap):
        if isinstance(ap, tuple):
            return [ap]
        if type(ap.tensor).__name__.startswith("DRam"):
            return []
        es = mybir.dt.size(ap.dtype)
        dims = list(ap.ap)[1:]
        lo = ap.offset * es
        ext = 1
        for st, cnt in dims:
            ext += abs(st) * (cnt - 1)
        hi = lo + ext * es
        nm = ap.tensor.name
        if nm.startswith("pb"):
            return [(nm, 0)]
        return [(nm, g) for g in range(lo // GR, (hi - 1) // GR + 1)]

    def op(self, eng, fn, R=(), W=(), dma_slot=None):
        idx = len(self.streams[eng])
        me = (eng, idx)
        deps = set()
        rk = [k for a in R for k in self.keys(a)]
        wk = [k for a in W for k in self.keys(a)]
        for k in rk:
            w = self.lastw.get(k)
            if w is not None:
                deps.add(w)
            if eng != "pe" and k[0].startswith("pb"):
                for re_, r in self.readers.get(k, {}).items():
                    if re_ != eng:
                        deps.add(r)
        for k in wk:
            w = self.lastw.get(k)
            if w is not None:
                deps.add(w)
            for r in self.readers.get(k, {}).values():
                deps.add(r)
        deps.discard(me)
        dma_need = {}
        for (de, di) in deps:
            sl = self.streams[de][di]["dma_slot"]
            if sl is not None:
                dma_need["dma_" + sl] = 16 * self.dma_slots[sl]
        ins = dict(eng=eng, idx=idx, fn=fn, deps=deps, dma_slot=dma_slot, dma_val=None, signal=False, ticket=None,
                   dma_need=dma_need)
        if dma_slot is not None:
            c = self.dma_slots.get(dma_slot, 0) + 1
            self.dma_slots[dma_slot] = c
            ins["dma_val"] = 16 * c
        self.streams[eng].append(ins)
        for k in wk:
            self.lastw[k] = me
            self.readers[k] = {}
        for k in rk:
            self.readers.setdefault(k, {})[eng if dma_slot is None else me] = me
        return me

    def emit(self, nc, stack):
        for e in self.ENGS:
            for ins in self.streams[e]:
                for (de, di) in ins["deps"]:
                    d = self.streams[de][di]
                    if d["dma_slot"] is None and (de != e or e != "pe"):
                        d["signal"] = True
        sems = {}
        for e in self.ENGS:
            sems[e] = stack.enter_context(nc.semaphore("s_" + e))
            t = 0
            for ins in self.streams[e]:
                if ins["signal"]:
                    t += 1
                    ins["ticket"] = t
        for s in self.dma_slots:
            sems["dma_" + s] = stack.enter_context(nc.semaphore("d_" + s))
        block = stack.enter_context(nc.Block())
        streams = self.streams

        def run(engname, eng):
            waited = {}
            for ins in streams[engname]:
                need = dict(ins["dma_need"])
                for (de, di) in ins["deps"]:
                    d = streams[de][di]
                    if d["dma_slot"] is not None:
                        continue
                    elif de != engname or engname != "pe":
                        key, val = de, d["ticket"]
                    else:
                        continue
                    if val > need.get(key, 0):
                        need[key] = val
                for key, val in need.items():
                    if waited.get(key, 0) < val:
                        eng.wait_ge(sems[key], val)
                        waited[key] = val
                bi = ins["fn"](eng)
                if ins["dma_slot"] is not None:
                    bi.then_inc(sems["dma_" + ins["dma_slot"]], 16)
                elif ins["signal"]:
                    bi.then_inc(sems[engname], 1)

        @block.tensor
        def _(e):
            run("pe", e)

        @block.scalar
        def _(e):
            run("act", e)

        @block.vector
        def _(e):
            run("dve", e)

        @block.gpsimd
        def _(e):
            run("pool", e)

        @block.sync
        def _(e):
            run("sp", e)


def _consts():
    bf = ml_dtypes.bfloat16
    c = {}
    n = np.arange(NLAT, dtype=np.int64)
    ang = 2.0 * np.pi * ((n[:, None] * n[None, :]) % NLAT).astype(np.float64) / NLAT
    c["dftC"] = np.cos(ang).astype(bf)
    c["dftS"] = np.sin(ang).astype(bf)
    n2 = np.arange(NCTX, dtype=np.int64)
    ang2 = 2.0 * np.pi * ((n2[:, None] * n2[None, :]) % NCTX).astype(np.float64) / NCTX
    d256 = np.stack([np.cos(ang2), np.sin(ang2)], 0)
    d256 = d256.reshape(2, 2, 128, NCTX).transpose(2, 0, 1, 3)
    c["dft256"] = np.ascontiguousarray(d256).astype(bf)
    m = np.arange(128, dtype=np.int64)
    angc = 2.0 * np.pi * ((m[:, None] * m[None, :]) % 128).astype(np.float64) / 128
    s_lat = 1.0 / math.sqrt(NLAT * 128.0)
    s_ctx = 1.0 / math.sqrt(NCTX * 128.0)
    ccs = np.stack([np.cos(angc) * s_lat, -np.sin(angc) * s_lat, np.cos(angc) * s_ctx, -np.sin(angc) * s_ctx], 1)
    c["ccs"] = np.ascontiguousarray(ccs).astype(bf)
    N3 = 384
    band = np.zeros((128, 20, 128), np.float64)
    t = np.arange(N3)
    for gi, win in enumerate((2, 4, 8, 16)):
        lo = np.clip(t - win // 2, 0, N3 - 1)
        hi = np.clip(t + (win - win // 2) - 1, 0, N3 - 1)
        cnt = (hi - lo + 1).astype(np.float64)
        B = np.zeros((N3, N3), np.float64)
        for tt in range(N3):
            B[lo[tt]:hi[tt] + 1, tt] = 1.0 / cnt[tt]
        B -= np.eye(N3)
        band[:, gi * 5 + 0, :] = B[128:256, 128:256]
        band[:, gi * 5 + 1, :] = B[0:128, 0:128]
        band[:, gi * 5 + 2, :] = B[256:384, 256:384]
        band[:, gi * 5 + 3, :] = B[0:128, 128:256]
        band[:, gi * 5 + 4, :] = B[128:256, 0:128]
    c["band"] = band.astype(bf)
    tok = np.arange(NLAT)
    row = (tok // 64).astype(np.float32)
    col = (tok % 64).astype(np.float32)
    inv = (np.float32(10000.0) ** (-np.arange(0, 64, 2, dtype=np.float32) / np.float32(64))).astype(np.float32)
    angr = np.stack([row[:, None] * inv[None, :], col[:, None] * inv[None, :]], 1).astype(np.float32)
    c["ropeC"] = np.cos(angr).reshape(NLAT, 64).astype(np.float32)
    c["ropeS"] = np.sin(angr).reshape(NLAT, 64).astype(np.float32)
    c["identF"] = np.eye(128, dtype=np.float32)
    c["identB"] = np.eye(128, dtype=np.float32).astype(bf)
    c["onesB"] = np.ones((128, 128), np.float32).astype(bf)
    return c


_CONST_CACHE = {}


def get_consts():
    if not _CONST_CACHE:
        _CONST_CACHE.update(_consts())
    return _CONST_CACHE


class _Stop(Exception):
    pass


def build_program(debug=False, stop_after=None, wseq=None):
    nc = bass.Bass("TRN2", target_bir_lowering=False)
    S = Sched()

    marks = []

    def mark(name):
        marks.append((name, len(S.streams['pe'])))
        if stop_after is not None and name == stop_after:
            raise _Stop()
    stack = ExitStack()

    def din(name, shape, dt=F32):
        return nc.dram_tensor(name, list(shape), dt, kind="ExternalInput").ap()

    x_in = din("x", [NLAT, D])
    ctx_in = din("ctx", [NCTX, D])
    cvec = din("cvec", [128, 32])
    ada_w = din("ada_w", [DEPTH, D, 3 * D])
    ada_b2 = din("ada_b2", [DEPTH, 2, 3 * D])
    ngcol = din("ngcol", [128, 32])
    fngb = din("fngb", [128, D])
    w_in = din("w_in", [DEPTH, D, INW])
    w_out = din("w_out", [DEPTH, D, D])
    gains = din("gains", [DEPTH, 128, 256])
    pool_w = din("pool_w", [DEPTH, 4, 128, 128])
    pscol = din("pscol", [128, 8])
    four_w = din("fourier_w", [DEPTH, 4, 128, 128])
    dftC = din("dftC", [NLAT, NLAT], BF16)
    dftS = din("dftS", [NLAT, NLAT], BF16)
    dft256_d = din("dft256", [128, 2, 2, NCTX], BF16)
    ccs_d = din("ccs", [128, 4, 128], BF16)
    band_d = din("band", [128, 20, 128], BF16)
    ropeC = din("ropeC", [NLAT, 64])
    ropeS = din("ropeS", [NLAT, 64])
    identF_d = din("identF", [128, 128])
    identB_d = din("identB", [128, 128], BF16)
    onesB_d = din("onesB", [128, 128], BF16)
    out_d = nc.dram_tensor("out", [NLAT, D], F32, kind="ExternalOutput").ap()
    x1_d = nc.dram_tensor("x1s", [NLAT, D], F32, kind="ExternalOutput" if debug else "Internal").ap()
    xc1_d = nc.dram_tensor("xc1s", [NCTX, D], F32, kind="ExternalOutput" if debug else "Internal").ap()
    wc_d = nc.dram_tensor("wcache", [DEPTH, 11, 128, 16 * 512], BF16).ap()
    modrow_h = nc.dram_tensor("modrow", [DEPTH, 2, 3 * D], F32)
    modrow_d = modrow_h.ap()

    SB = stack.enter_context(nc.sbuf_tensor("SB", [128, SB_BYTES // 4], F32))
    cur = [0]

    def alloc(nbytes, align=512):
        o = (cur[0] + align - 1) // align * align
        cur[0] = o + nbytes
        assert cur[0] <= SB_BYTES, ("SBUF overflow", cur[0])
        return o

    def vf(off, n):
        return SB[:, off // 4: off // 4 + n]

    def vb(off, n):
        return SB[:, off // 4: off // 4 + (n + 1) // 2].bitcast(BF16)[:, 0:n]

    def r3(v, b):
        return v.rearrange("p (a b) -> p a b", b=b)

    o_hT = alloc(16 * NTOK * 2)
    hT = r3(vb(o_hT, 16 * NTOK), NTOK)
    o_KT = alloc(2 * NTOK * 2)
    KT = r3(vb(o_KT, 2 * NTOK), NTOK)
    o_V = alloc(18 * 256 * 2)
    Vv = r3(vb(o_V, 18 * 256), 256)
    o_ReT = alloc(4 * NTOK * 2)
    ReT = r3(vb(o_ReT, 4 * NTOK), NTOK)
    o_wb = [alloc(16 * 512 * 2), alloc(16 * 512 * 2)]
    wbuf = [r3(vb(o, 16 * 512), 512) for o in o_wb]
    identF = vf(alloc(512), 128)
    identB = vb(alloc(256, 256), 128)
    onesB = vb(alloc(256, 256), 128)
    band = r3(vb(alloc(20 * 256), 20 * 128), 128)
    dft256 = vb(alloc(2048), 1024).rearrange("p (m j k) -> p m j k", m=2, j=2)
    ccs = r3(vb(alloc(1024), 512), 128)
    poolw = r3(vb(alloc(1024), 512), 128)
    fourw = r3(vb(alloc(1024), 512), 128)
    gains_s = vf(alloc(1024), 256)
    o_small = alloc(2048)
    sc3 = r3(vb(o_small, 32), 2)
    cv = vf(o_small + 128, 32)
    cvt = vf(o_small + 256, 32)
    ngc = vf(o_small + 384, 32)
    psc = vf(o_small + 512, 8)
    mcols = vf(o_small + 576, 64).rearrange("p (w k j) -> p w k j", w=2, k=2)
    mhalf = vf(o_small + 832, 16)
    ssb = vf(alloc(64, 512), 16)
    rsb = vf(alloc(64, 512), 16)
    ssq = vf(alloc(128, 512), 32)
    rfin = vf(alloc(64, 512), 4)
    o_gp = [alloc(2048), alloc(2048)]
    gpiece = [vf(o, 512) for o in o_gp]
    halo_prev = vb(alloc(1024), 512)
    o_rope = alloc(1024)
    ropec = [vf(o_rope, 64), vf(o_rope + 512, 64)]
    ropes = [vf(o_rope + 256, 64), vf(o_rope + 768, 64)]
    o_R = alloc(0, 1024)
    R_BYTES = SB_BYTES - o_R

    class Arena:
        def __init__(self):
            self.c = 0

        def a(self, nbytes, align=512):
            o = (self.c + align - 1) // align * align
            self.c = o + nbytes
            assert self.c <= R_BYTES, ("scratch overflow", self.c, R_BYTES)
            return o_R + o

    banks = [stack.enter_context(nc.psum_tensor("pb%d" % i, [128, 512], F32)) for i in range(8)]

    def bankb(i):
        return banks[i][:, :].bitcast(BF16)

    rr = {}

    def nxt(cls, lst):
        i = rr.get(cls, 0)
        rr[cls] = i + 1
        return lst[i % len(lst)]

    def dma(eng, out, in_, slot, R=None, W=None):
        S.op(eng, lambda e, o=out, i=in_: e.dma_start(out=o, in_=i), R=R if R is not None else [in_],
             W=W if W is not None else [out], dma_slot=slot + "_" + eng)

    def mm(out, lhsT, rhs, start, stop, extraR=()):
        S.op("pe", lambda e, o=out, l=lhsT, r=rhs, s=start, t=stop: e.matmul(o, l, r, start=s, stop=t),
             R=[lhsT, rhs] + list(extraR), W=[out])

    def tr(out, in_, ident):
        S.op("pe", lambda e, o=out, i=in_, d=ident: e.transpose(o, i, d), R=[in_, ident], W=[out])

    def act(out, in_, func, bias=None, scale=None, accum=None, extraR=()):
        kw = {}
        Rl = [in_] + list(extraR)
        Wl = [out]
        if bias is not None:
            kw["bias"] = bias
            if not isinstance(bias, float):
                Rl.append(bias)
        if scale is not None:
            kw["scale"] = scale
            if not isinstance(scale, float):
                Rl.append(scale)
        if accum is not None:
            kw["accum_out"] = accum
            Wl.append(accum)
        S.op("act", lambda e, o=out, i=in_, f=func, k=kw: e.activation(o, i, f, **k), R=Rl, W=Wl)

    def ts(eng, out, in0, s1, s2, op0, op1=None):
        Rl = [in0] + [s for s in (s1, s2) if s is not None and not isinstance(s, float)]
        if op1 is None:
            S.op(eng, lambda e, o=out, i=in0, a=s1, p=op0: e.tensor_scalar(o, i, a, None, p), R=Rl, W=[out])
        else:
            S.op(eng, lambda e, o=out, i=in0, a=s1, b=s2, p=op0, q=op1: e.tensor_scalar(o, i, a, b, p, q), R=Rl, W=[out])

    def tt(eng, out, in0, in1, op):
        S.op(eng, lambda e, o=out, a=in0, b=in1, p=op: e.tensor_tensor(o, a, b, p), R=[in0, in1], W=[out])

    def stt(out, in0, scalar, in1, op0, op1):
        Rl = [in0, in1] + ([scalar] if not isinstance(scalar, float) else [])
        S.op("dve", lambda e, o=out, a=in0, s=scalar, b=in1, p=op0, q=op1: e.scalar_tensor_tensor(o, a, s, b, p, q),
             R=Rl, W=[out])

    def cp(eng, out, in_):
        if eng == "act":
            S.op("act", lambda e, o=out, i=in_: e.activation(o, i, AF.Copy), R=[in_], W=[out])
        else:
            S.op(eng, lambda e, o=out, i=in_: e.tensor_copy(o, i), R=[in_], W=[out])

    def recip(out, in_):
        S.op("dve", lambda e, o=out, i=in_: e.reciprocal(o, i), R=[in_], W=[out])

    def rstd_from_ss(ss, rs, n, inv_n):
        ts("pool", rs, ss, float(inv_n), float(EPS), ALU.mult, ALU.add)
        tt("pool", rs, rs, mhalf[:, 0:n], ALU.pow)

    def ap4(v, off_el, dims):
        ps = list(v.ap)[0][0]
        return bass.AP(v.tensor, v.offset + off_el, [[ps, 128]] + [list(d) for d in dims])

    wcount = [0]

    wrec = []
    wseen = set()

    wpending = []

    def issue_w(k, ent):
        src3, eng, ckey = ent
        buf = wbuf[k % 2]
        if ckey is not None and ckey in wseen:
            lyr, idx = ckey
            dma("pool", buf.rearrange("p a b -> p (a b)"), wc_d[lyr, idx], "w%d" % (k % 2), R=[("D", "wc", lyr, idx)])
            return
        dma(eng, buf[:, :, :], src3, "w%d" % (k % 2))
        if ckey is not None:
            wseen.add(ckey)
            wpending.append((k, ckey))

    def flush_w_stores(upto):
        while wpending and wpending[0][0] <= upto:
            k, (lyr, idx) = wpending.pop(0)
            dma("sp", wc_d[lyr, idx], wbuf[k % 2].rearrange("p a b -> p (a b)"), "wst", W=[("D", "wc", lyr, idx)])

    def load_w(src3, eng="pool", ckey=None):
        k = wcount[0]
        wcount[0] += 1
        wrec.append((src3, eng, ckey))
        if wseq is None:
            dma(eng, wbuf[k % 2][:, :, :], src3, "w%d" % (k % 2))
        else:
            if k == 0:
                issue_w(0, wseq[0])
            flush_w_stores(k)
            if k + 1 < len(wseq):
                issue_w(k + 1, wseq[k + 1])
        return wbuf[k % 2]

    def wsrc(wd, l, c0, ncol=512):
        return wd[l, :, c0:c0 + ncol].rearrange("(j p) n -> p j n", p=128)

    for dst, src in ((identF, identF_d), (identB, identB_d), (onesB, onesB_d),
                     (band, band_d), (dft256, dft256_d), (ccs, ccs_d),
                     (cv, cvec), (ngc, ngcol), (psc, pscol)):
        dma("sp", dst, src, "const")
    S.op("pool", lambda e: e.memset(mhalf, -0.5), W=[mhalf])
    act(cvt, cv, AF.Tanh, scale=0.5)
    stt(cvt, cvt, 1.0, cv, ALU.add, ALU.mult)
    ts("dve", sc3.rearrange("p a b -> p (a b)"), cvt, 0.5, None, ALU.mult)

    xsrc = {("lat", 0): x_in, ("ctx", 0): ctx_in, ("lat", 1): x1_d, ("ctx", 1): xc1_d}
    xdst = {("lat", 0): x1_d, ("ctx", 0): xc1_d, ("lat", 1): out_d}

    def xkey(tensor, tile, cb):
        return ("D", tensor.tensor.name, tile, cb)

    def xkeys(tensor, tile):
        return [xkey(tensor, tile, cb) for cb in range(4)]

    deferred = []
    try:
      for l in range(DEPTH):
          last = l == DEPTH - 1
          dma("pool", poolw, pool_w[l].rearrange("g c d -> c g d"), "lw")
          dma("pool", fourw, four_w[l].rearrange("g c d -> c g d"), "lw")
          dma("sp", gains_s, gains[l], "lw2")

          ar = Arena()
          o_mp = [ar.a(2048), ar.a(2048)]
          o_ab = [ar.a(2048), ar.a(2048)]
          colsP = banks[1][:, 0:96].rearrange("p (c w) -> p c w", w=2)
          for nb in range(12):
              wv = load_w(wsrc(ada_w, l, nb * 512))
              abp = vf(o_ab[nb % 2], 512)[0:2, :]
              mp = vf(o_mp[nb % 2], 512)[0:2, :]
              dma("sp", abp, ada_b2[l, :, nb * 512:(nb + 1) * 512], "adab%d" % (nb % 2))
              acc = banks[0][0:2, :]
              for j in range(16):
                  mm(acc, sc3[:, j, :], wv[:, j, :], j == 0, j == 15)
              tt("dve", mp, acc, abp, ALU.add)
              dma("sp", modrow_d[l, :, nb * 512:(nb + 1) * 512], mp, "modrow",
                  W=[("D", "modrow", l, nb)])
              for q in range(4):
                  tr(colsP[:, nb * 4 + q, :], mp[:, q * 128:(q + 1) * 128], identF[0:2, 0:2])
          for which in range(2):
              cp("dve", mcols[:, which, 1, :], colsP[:, 0:16, which])
              stt(mcols[:, which, 0, :], colsP[:, 16:32, which], 1.0, ngc[:, l * 16:(l + 1) * 16], ALU.add, ALU.mult)

          mark('ada%d' % l)
          ar = Arena()
          o_xt = [ar.a(8192), ar.a(8192)]
          o_junk = ar.a(4096)
          junk = vb(o_junk, 2048)
          seqs = [("ctx", t) for t in range(2)] + [("lat", t) for t in range(16)]
          def p1_a(ti):
              which, t = seqs[ti]
              src = xsrc[(which, l)]
              xt = vf(o_xt[ti % 2], 2048)
              dma("sp", xt, src[t * 128:(t + 1) * 128, :], "xt%d" % (ti % 2),
                  R=xkeys(src, t) if l > 0 else [])
              ss = ssb[:, 12 + ti % 2:13 + ti % 2]
              act(junk, xt, AF.Square, accum=ss)
              rs = rsb[:, 12 + ti % 2:13 + ti % 2]
              rstd_from_ss(ss, rs, 1, 1.0 / D)
              ts("dve", xt, xt, rs, None, ALU.mult)

          def p1_b(ti):
              which, t = seqs[ti]
              wi = 0 if which == "lat" else 1
              xt = vf(o_xt[ti % 2], 2048)
              gt = ti
              pbs = [0, 1, 2, 3] if ti % 2 == 0 else [4, 5, 6, 7]
              for jb in range(4):
                  for j in range(jb * 4, jb * 4 + 4):
                      pb = banks[pbs[jb]][:, (j % 4) * 128:(j % 4 + 1) * 128]
                      tr(pb, xt[:, j * 128:(j + 1) * 128], identF)
                  for j in range(jb * 4, jb * 4 + 4):
                      pb = banks[pbs[jb]][:, (j % 4) * 128:(j % 4 + 1) * 128]
                      dst = hT[:, j, gt * 128:(gt + 1) * 128]
                      if jb % 2 == 0:
                          act(dst, pb, AF.Identity, bias=mcols[:, wi, 1, j:j + 1], scale=mcols[:, wi, 0, j:j + 1])
                      else:
                          ts("dve", dst, pb, mcols[:, wi, 0, j:j + 1], mcols[:, wi, 1, j:j + 1], ALU.mult, ALU.add)

          p1_a(0)
          for ti in range(len(seqs)):
              if ti + 1 < len(seqs):
                  p1_a(ti + 1)
              p1_b(ti)

          mark('p1_%d' % l)
          ar = Arena()
          o_kx = ar.a(1024)
          o_ta = ar.a(512)
          o_tb = ar.a(512)
          o_kr = [ar.a(512), ar.a(512)]
          junk = vb(ar.a(512), 256)
          wv = load_w(wsrc(w_in, l, OFF_K))
          def kv_mm(gt):
              pb = banks[gt % 2]
              for j in range(16):
                  mm(pb[:, :], hT[:, j, gt * 128:(gt + 1) * 128], wv[:, j, :], j == 0, j == 15)

          kv_mm(0)
          for gt in range(18):
              is_lat = gt >= 2
              pb = banks[gt % 2]
              if gt + 1 < 18:
                  kv_mm(gt + 1)
              so = 2 * (gt % 2)
              ss = ssb[:, so:so + 2]
              rs = rsb[:, so:so + 2]
              for h in range(2):
                  act(junk[:, 0:128], pb[:, h * 128:(h + 1) * 128], AF.Square, accum=ssb[:, so + h:so + h + 1])
              rstd_from_ss(ss, rs, 2, 1.0 / 128)
              kr = vb(o_kr[gt % 2], 256)
              kx = vf(o_kx, 256)
              if is_lat:
                  t = gt - 2
                  rc, rsn = ropec[gt % 2], ropes[gt % 2]
                  dma("sp", rc, ropeC[t * 128:(t + 1) * 128, :], "rope%d" % (gt % 2))
                  dma("sp", rsn, ropeS[t * 128:(t + 1) * 128, :], "rope%d" % (gt % 2))
              for h in range(2):
                  stt(kx[:, h * 128:(h + 1) * 128] if is_lat else kr[:, h * 128:(h + 1) * 128],
                      pb[:, h * 128:(h + 1) * 128], rsb[:, so + h:so + h + 1], gains_s[:, 128:256], ALU.mult, ALU.mult)
              cp("act", Vv[:, gt, :], pb[:, 256:512])
              if is_lat:
                  rope(S, tt, ap4, kx, kr, vf(o_ta, 128), vf(o_tb, 128), rc, rsn, 2)
              kb = nxt("B", [2, 3])
              pbt = bankb(kb)
              for h in range(2):
                  tr(pbt[:, h * 128:(h + 1) * 128], kr[:, h * 128:(h + 1) * 128], identB)
              for h in range(2):
                  cp("dve", KT[:, h, gt * 128:(gt + 1) * 128], pbt[:, h * 128:(h + 1) * 128])

          mark('g1_%d' % l)
          ar = Arena()
          o_tm = ar.a(18 * 512 * 2)
          tm = r3(vb(o_tm, 18 * 512), 512)
          o_pT2 = ar.a(2 * 4 * 512 * 2)
          pT2 = vb(o_pT2, 4096).rearrange("p (m g k) -> p m g k", m=2, g=4)
          wv = load_w(wsrc(w_in, l, OFF_FOUR))
          ftiles = list(range(18)) if not last else list(range(2, 18))
          for gt in ftiles:
              pb = banks[nxt("A", [0, 1])]
              for j in range(16):
                  mm(pb[:, :], hT[:, j, gt * 128:(gt + 1) * 128], wv[:, j, :], j == 0, j == 15)
              cp("act" if gt % 2 == 0 else "dve", tm[:, gt, :], pb[:, :])
          if not last:
              for mat in range(2):
                  for g in range(4):
                      pb = banks[nxt("A", [0, 1])]
                      for j in range(2):
                          mm(pb[:, 0:256], tm[:, j, g * 128:(g + 1) * 128], dft256[:, mat, j, :], j == 0, j == 1)
                      cp("act" if g % 2 == 0 else "dve", pT2[:, mat, g, 0:256], pb[:, 0:256])
              for g in range(4):
                  pb = banks[nxt("C", [4, 5])]
                  mm(pb[:, 0:256], ccs[:, 2, :], pT2[:, 0, g, 0:256], True, False)
                  mm(pb[:, 0:256], ccs[:, 3, :], pT2[:, 1, g, 0:256], False, True)
                  cp("act" if g % 2 == 0 else "dve", ReT[:, g, 0:256], pb[:, 0:256])
          for kb in range(4):
              for mat, dsrc in enumerate((dftC, dftS)):
                  dv = load_w(dsrc[:, kb * 512:(kb + 1) * 512].rearrange("(j p) k -> p j k", p=128), eng="pool")
                  for g in range(4):
                      pb = banks[nxt("A", [0, 1, 2, 3])]
                      for j in range(16):
                          mm(pb[:, :], tm[:, 2 + j, g * 128:(g + 1) * 128], dv[:, j, :], j == 0, j == 15)
                      cp("act" if g % 2 == 0 else "dve", pT2[:, mat, g, :], pb[:, :])
              for g in range(4):
                  pb = banks[nxt("C", [4, 5])]
                  mm(pb[:, :], ccs[:, 0, :], pT2[:, 0, g, :], True, False)
                  mm(pb[:, :], ccs[:, 1, :], pT2[:, 1, g, :], False, True)
                  cp("act" if g % 2 == 0 else "dve", ReT[:, g, 256 + kb * 512:256 + (kb + 1) * 512], pb[:, :])

          mark('g3_%d' % l)
          groups = ([] if last else [("ctx", 0, NCTX, 0)]) + [("lat", 256 + g * 512, 512, g * 4) for g in range(4)]
          for (which, tok0, ntok, T0) in groups:
              wi = 0 if which == "lat" else 1
              is_lat = which == "lat"
              NT = 16 if is_lat else 2
              ntile = ntok // 128
              ktiles = list(range(18)) if is_lat else [0, 1]
              src_x = xsrc[(which, l)]
              dst_x = xdst[(which, l)]
              ar = Arena()
              o_q = ar.a(6 * 1024, 1024)
              QT = r3(vb(o_q, 4 * ntok), ntok)
              qx = vf(o_q + 4096, 512)
              up_tm = r3(vb(o_q, 6 * 512), 512)
              mgT = r3(vb(ar.a(16 * ntok * 2), 16 * ntok), ntok)
              o_PT = [ar.a(ntok * 2) for _ in range(3)]
              PT = [vb(o, ntok) for o in o_PT]
              tmpAf = [vf(ar.a(2048), 512) for _ in range(2)]
              tmpA = [v[:, 0:ntok] for v in tmpAf]
              o_tO = [ar.a(2048) for _ in range(2)]
              tmpO = [vf(o, 512)[:, 0:ntok] for o in o_tO]
              o_qr = [ar.a(1024) for _ in range(2)]
              qr = [vb(o, 512) for o in o_qr]
              o_ta = ar.a(1024)
              o_tb = ar.a(1024)
              ta = vf(o_ta, 256)
              tb = vf(o_tb, 256)
              x2full = x2hole = None
              if last:
                  assert o_PT[1] == o_PT[0] + 1024 and o_qr[1] == o_qr[0] + 1024 and o_tb == o_ta + 1024
                  x2full = [vf(o_q, 512), vf(o_q + 2048, 512), vf(o_q + 4096, 512), vf(o_tO[0], 512), vf(o_tO[1], 512),
                            vf(o_qr[0], 512), vf(o_ta, 512), vf(o_PT[0], 512)]
                  x2hole = [vf(o_hT + (j * NTOK + tok0) * 2, 256) for j in range(16)]
              plT = [vb(ar.a(ntok * 2), ntok)] * 2
              xp = [vf(ar.a(2048), 512) for _ in range(2)]

              def gate_chunk(fc, wg, branch, branch_in_psum):
                  gb = banks[nxt("B", [2, 3])]
                  hcol = (fc % 4) * 128
                  for j in range(16):
                      mm(gb[:, 0:ntok], wg[:, j, hcol:hcol + 128], hT[:, j, tok0:tok0 + ntok], j == 0, j == 15)
                  th = tmpA[fc % 2]
                  act(th, gb[:, 0:ntok], AF.Tanh, scale=0.5)
                  stt(th, th, 1.0, gb[:, 0:ntok], ALU.add, ALU.mult)
                  stt(mgT[:, fc, :], th, 0.5, branch, ALU.mult, ALU.mult)

              for hb in range(2):
                  wq = load_w(wsrc(w_in, l, OFF_Q + hb * 512), ckey=(l, hb))
                  def q_mm(tti):
                      pb = banks[tti % 2]
                      c0 = tok0 + tti * 128
                      for j in range(16):
                          mm(pb[:, :], hT[:, j, c0:c0 + 128], wq[:, j, :], j == 0, j == 15)

                  q_mm(0)
                  for tti in range(ntile):
                      pb = banks[tti % 2]
                      if tti + 1 < ntile:
                          q_mm(tti + 1)
                      so = 4 + 4 * (tti % 2)
                      for h in range(4):
                          act(PT[2][:, 0:128], pb[:, h * 128:(h + 1) * 128], AF.Square, accum=ssb[:, so + h:so + h + 1])
                      mark('q1')
                      rstd_from_ss(ssb[:, so:so + 4], rsb[:, so:so + 4], 4, 1.0 / 128)
                      mark('q2')
                      qrv = qr[tti % 2]
                      if is_lat:
                          t = T0 + tti
                          rc, rsn = ropec[tti % 2], ropes[tti % 2]
                          dma("sp", rc, ropeC[t * 128:(t + 1) * 128, :], "rope%d" % (tti % 2))
                          dma("sp", rsn, ropeS[t * 128:(t + 1) * 128, :], "rope%d" % (tti % 2))
                      for h in range(4):
                          stt(qx[:, h * 128:(h + 1) * 128] if is_lat else qrv[:, h * 128:(h + 1) * 128],
                              pb[:, h * 128:(h + 1) * 128], rsb[:, so + h:so + h + 1], gains_s[:, 0:128], ALU.mult, ALU.mult)
                      if is_lat:
                          rope(S, tt, ap4, qx, qrv, ta, tb, rc, rsn, 4)
                      mark('q3')
                      pbt = bankb(nxt("B", [2, 3]))
                      for h in range(4):
                          tr(pbt[:, h * 128:(h + 1) * 128], qrv[:, h * 128:(h + 1) * 128], identB)
                      mark('q4')
                      cp("dve" if tti % 2 == 0 else "act", QT[:, :, tti * 128:(tti + 1) * 128],
                         pbt[:, 0:512].rearrange("p (h d) -> p h d", h=4))
                      mark('q5')
                  mark('qdone')
                  wg = load_w(wsrc(w_in, l, OFF_GATE + hb * 512), ckey=(l, 2 + hb))
                  for h in range(4):
                      fc = hb * 4 + h
                      Ob, Lb = banks[6], banks[7]
                      nk = len(ktiles)

                      def s_mm(ki):
                          kt = ktiles[ki]
                          sb = banks[(4, 5, 0, 1)[ki % 4]]
                          mm(sb[:, 0:ntok], KT[:, hb, kt * 128:(kt + 1) * 128], QT[:, h, :], True, True)
                          act(PT[ki % 3], sb[:, 0:ntok], AF.Exp, scale=float(128.0 ** -0.5))

                      def pv_mm(ki):
                          kt = ktiles[ki]
                          mm(Ob[:, 0:ntok], Vv[:, kt, hb * 128:(hb + 1) * 128], PT[ki % 3], ki == 0, ki == nk - 1)
                          mm(Lb[:, 0:ntok], onesB, PT[ki % 3], ki == 0, ki == nk - 1)

                      s_mm(0)
                      if nk > 1:
                          s_mm(1)
                      for ki in range(nk):
                          if ki + 2 < nk:
                              s_mm(ki + 2)
                          pv_mm(ki)
                      mark('attA')
                      tO = tmpO[fc % 2]
                      recip(tO, Lb[:, 0:ntok])
                      tt("dve", tO, Ob[:, 0:ntok], tO, ALU.mult)
                      mark('attB')
                      gate_chunk(fc, wg, tO, False)
                      mark('attC')

              mark('att_%d_%s_%d' % (l, which, T0))
              while deferred:
                  deferred.pop(0)()
              wp = load_w(wsrc(w_in, l, OFF_POOL), ckey=(l, 6))
              Tlo = max(T0 - 1, 0)
              Thi = min(T0 + ntile, NT - 1)
              seq_tok0 = 256 if is_lat else 0
              use_halo = last and T0 > 0
              for T in range(Tlo, Thi + 1):
                  if use_halo and T == T0 - 1:
                      continue
                  pb = banks[nxt("A", [0, 1])]
                  c0 = seq_tok0 + T * 128
                  for j in range(16):
                      mm(pb[:, :], hT[:, j, c0:c0 + 128], wp[:, j, :], j == 0, j == 15)
                  cp("act" if T % 2 == 0 else "dve", up_tm[:, T - Tlo, :], pb[:, :])

              def up_src(Tn, g):
                  if use_halo and Tn == T0 - 1:
                      return halo_prev[:, g * 128:(g + 1) * 128]
                  return up_tm[:, Tn - Tlo, g * 128:(g + 1) * 128]

              wg = load_w(wsrc(w_in, l, OFF_GATE + 2 * 512), ckey=(l, 4))
              for g in range(4):
                  fc = 8 + g
                  pb = banks[nxt("C", [4, 5])]
                  for tti in range(ntile):
                      T = T0 + tti
                      terms = []
                      if T > 0:
                          terms.append((T - 1, 3))
                      terms.append((T, 1 if T == 0 else (2 if T == NT - 1 else 0)))
                      if T < NT - 1:
                          terms.append((T + 1, 4))
                      for i, (Tn, kind) in enumerate(terms):
                          mm(pb[:, tti * 128:(tti + 1) * 128], up_src(Tn, g),
                             band[:, g * 5 + kind, :], i == 0, i == len(terms) - 1)
                  pl = plT[g % 2]
                  cp("act", pl, pb[:, 0:ntok])
                  yb = banks[nxt("D", [6, 7])]
                  mm(yb[:, 0:ntok], poolw[:, g, :], pl, True, True)
                  tO = tmpO[fc % 2]
                  ts("dve", tO, yb[:, 0:ntok], psc[:, l * 4 + g:l * 4 + g + 1], None, ALU.mult)
                  gate_chunk(fc, wg, tO, False)

              if last and T0 + ntile < NT:
                  cp("act", halo_prev, up_tm[:, (T0 + ntile - 1) - Tlo, :])
              wg = load_w(wsrc(w_in, l, OFF_GATE + 3 * 512), ckey=(l, 5))
              for g in range(4):
                  fc = 12 + g
                  yb = banks[nxt("D", [6, 7])]
                  mm(yb[:, 0:ntok], fourw[:, g, :], ReT[:, g, tok0:tok0 + ntok], True, True)
                  gate_chunk(fc, wg, yb[:, 0:ntok], True)

              mark('four_%d_%s_%d' % (l, which, T0))
              gate_c0 = 2 * D
              pieces = [(cb, tti) for cb in range(4) for tti in range(ntile)]

              def gp_load(cb):
                  dma("sp", gpiece[cb % 2],
                      modrow_d[l, wi:wi + 1, gate_c0 + cb * 512:gate_c0 + (cb + 1) * 512].partition_broadcast(128).rearrange("p a n -> p (a n)"),
                      "gp%d" % (cb % 2), R=[("D", "modrow", l, 8 + cb)])

              def x_load(i):
                  cb, tti = pieces[i]
                  T = T0 + tti
                  dma("sp", xp[i % 2], src_x[T * 128:(T + 1) * 128, cb * 512:(cb + 1) * 512], "xp%d" % (i % 2),
                      R=[xkey(src_x, T, cb)] if l > 0 else [])

              gp_load(0)
              x_load(0)
              wo = None
              for i, (cb, tti) in enumerate(pieces):
                  T = T0 + tti
                  if tti == 0:
                      wo = load_w(wsrc(w_out, l, cb * 512), ckey=(l, 7 + cb))
                      if cb + 1 < 4:
                          gp_load(cb + 1)
                  if i + 1 < len(pieces):
                      x_load(i + 1)
                  gp = gpiece[cb % 2]
                  xpv = xp[i % 2]
                  pb = banks[nxt("W", [0, 1, 4, 5])]
                  for fc in range(16):
                      mm(pb[:, :], mgT[:, fc, tti * 128:(tti + 1) * 128], wo[:, fc, :], fc == 0, fc == 15)
                  tmpx = tmpAf[tti % 2]
                  tt("dve", tmpx, pb[:, :], gp, ALU.mult)
                  if not last:
                      tt("dve", xpv, tmpx, xpv, ALU.add)
                      dma("sp", dst_x[T * 128:(T + 1) * 128, cb * 512:(cb + 1) * 512], xpv, "xo%d" % (i % 2),
                          W=[xkey(dst_x, T, cb)])
                  else:
                      for hf in range(2):
                          if cb < 2:
                              dest = x2full[tti * 2 + cb][:, hf * 256:(hf + 1) * 256]
                          else:
                              dest = x2hole[(tti * 2 + cb - 2) * 2 + hf]
                          tt("dve", dest, tmpx[:, hf * 256:(hf + 1) * 256], xpv[:, hf * 256:(hf + 1) * 256], ALU.add)
                          act(tmpAf[(tti + 1) % 2][:, hf * 256:(hf + 1) * 256], dest, AF.Square,
                              accum=ssq[:, tti * 8 + cb * 2 + hf:tti * 8 + cb * 2 + hf + 1])
              mark('wout_%d_%s_%d' % (l, which, T0))
              if last:
                  for tti in range(ntile):
                      S.op("dve", lambda e, o=rfin[:, tti:tti + 1], i=ssq[:, tti * 8:(tti + 1) * 8]:
                           e.tensor_reduce(o, i, mybir.AxisListType.X, ALU.add),
                           R=[ssq[:, tti * 8:(tti + 1) * 8]], W=[rfin[:, tti:tti + 1]])
                      rstd_from_ss(rfin[:, tti:tti + 1], rfin[:, tti:tti + 1], 1, 1.0 / D)
                  for cb in range(4):
                      fgv = gpiece[cb % 2]
                      dma("sp", fgv, fngb[:, cb * 512:(cb + 1) * 512], "gp%d" % (cb % 2))
                      for tti in range(ntile):
                          T = T0 + tti
                          for hf in range(2):
                              if cb < 2:
                                  srcv = x2full[tti * 2 + cb][:, hf * 256:(hf + 1) * 256]
                              else:
                                  srcv = x2hole[(tti * 2 + cb - 2) * 2 + hf]
                              stt(srcv, srcv, rfin[:, tti:tti + 1], fgv[:, hf * 256:(hf + 1) * 256], ALU.mult, ALU.mult)
                              if cb >= 2:
                                  c0 = cb * 512 + hf * 256
                                  dma("sp", out_d[T * 128:(T + 1) * 128, c0:c0 + 256], srcv, "xo%d" % hf,
                                      W=[("D", out_d.tensor.name, T, cb, hf)])
                          if cb < 2:
                              dma("sp", out_d[T * 128:(T + 1) * 128, cb * 512:(cb + 1) * 512], x2full[tti * 2 + cb],
                                  "xo%d" % (tti % 2), W=[xkey(out_d, T, cb)])
      while deferred:
          deferred.pop(0)()

    except _Stop:
        pass

    allout = [xkey(out_d, T, cb) for T in range(16) for cb in range(2)]
    allout += [("D", out_d.tensor.name, T, cb, hf) for T in range(16) for cb in (2, 3) for hf in range(2)]
    if debug:
        allout += [xkey(x1_d, T, cb) for T in range(16) for cb in range(4)]
        allout += [xkey(xc1_d, T, cb) for T in range(2) for cb in range(4)]
    S.op("sp", lambda e: e.nop(), R=allout)
    S.emit(nc, stack)
    stack.close()
    nc._wrec = wrec
    nc._marks = marks
    return nc


def rope(S, tt, ap4, xin, xout, ta, tb, rc, rsn, H):
    dims = [[128, H], [64, 2], [1, 32]]
    x1 = ap4(xin, 0, dims)
    x2 = ap4(xin, 32, dims)
    o1 = ap4(xout, 0, dims)
    o2 = ap4(xout, 32, dims)
    tdims = [[64, H], [32, 2], [1, 32]]
    a = ap4(ta, 0, tdims)
    b = ap4(tb, 0, tdims)
    cdims = [[0, H], [32, 2], [1, 32]]
    c = ap4(rc, 0, cdims)
    s = ap4(rsn, 0, cdims)
    tt("dve", a, x1, c, ALU.mult)
    tt("dve", b, x2, s, ALU.mult)
    tt("dve", o1, a, b, ALU.subtract)
    tt("dve", a, x2, c, ALU.mult)
    tt("dve", b, x1, s, ALU.mult)
    tt("dve", o2, a, b, ALU.add)


def build_two_pass(debug=False, stop_after=None):
    rec = build_program(debug=debug, stop_after=stop_after)._wrec
    return build_program(debug=debug, stop_after=stop_after, wseq=rec)


_NC_CACHE = {}


def make_in_maps(x, c, ctx, c_ctx, ada_w, ada_b, norm_g, w_in, q_norm_g, k_norm_g,
                 pool_w, pool_scale, fourier_w, w_out, final_norm_g, cores):
    f = lambda a: np.ascontiguousarray(np.asarray(a, dtype=np.float32))
    x, c, ctx, c_ctx = f(x), f(c), f(ctx), f(c_ctx)
    ada_w, ada_b, norm_g, w_in = f(ada_w), f(ada_b), f(norm_g), f(w_in)
    q_norm_g, k_norm_g, pool_w, pool_scale = f(q_norm_g), f(k_norm_g), f(pool_w), f(pool_scale)
    fourier_w, w_out, final_norm_g = f(fourier_w), f(w_out), f(final_norm_g)
    consts = get_consts()
    shared = dict(consts)
    shared["ada_w"] = ada_w
    shared["ada_b2"] = np.ascontiguousarray(np.repeat(ada_b[:, None, :], 2, axis=1))
    shared["ngcol"] = np.ascontiguousarray(norm_g.reshape(DEPTH, 16, 128).transpose(2, 0, 1).reshape(128, 32))
    shared["fngb"] = np.ascontiguousarray(np.broadcast_to(final_norm_g[None, :], (128, D)))
    shared["w_in"] = w_in
    shared["w_out"] = w_out
    g = np.concatenate([q_norm_g, k_norm_g], axis=1)
    shared["gains"] = np.ascontiguousarray(np.broadcast_to(g[:, None, :], (DEPTH, 128, 256)))
    shared["pool_w"] = pool_w
    shared["pscol"] = np.ascontiguousarray(pool_scale.reshape(DEPTH, 4, 128).transpose(2, 0, 1).reshape(128, 8))
    shared["fourier_w"] = fourier_w
    cc = c_ctx.reshape(16, 128).T
    maps = []
    for b in cores:
        m = dict(shared)
        m["x"] = x[b]
        m["ctx"] = ctx[b]
        cb = c[b].reshape(16, 128).T
        m["cvec"] = np.ascontiguousarray(np.stack([cb, cc], axis=2).reshape(128, 32))
        maps.append(m)
    return maps


def kernel(x, c, ctx, c_ctx, ada_w, ada_b, norm_g, w_in, q_norm_g, k_norm_g,
           pool_w, pool_scale, fourier_w, w_out, final_norm_g):
    if "nc" not in _NC_CACHE:
        _NC_CACHE["nc"] = build_two_pass(debug=False)
    nc = _NC_CACHE["nc"]
    maps = make_in_maps(x, c, ctx, c_ctx, ada_w, ada_b, norm_g, w_in, q_norm_g, k_norm_g,
                        pool_w, pool_scale, fourier_w, w_out, final_norm_g, list(range(8)))
    res = run_bass_kernel_spmd(nc, maps, core_ids=list(range(8)))
    out = np.stack([np.asarray(r["out"], dtype=np.float32) for r in res.results], axis=0)
    return out
```

```python
import math
from contextlib import ExitStack
import numpy as np
import ml_dtypes
import concourse.bass as bass
import concourse.mybir as mybir
from concourse.bass_utils import run_bass_kernel_spmd

F32 = mybir.dt.float32
BF16 = mybir.dt.bfloat16
AF = mybir.ActivationFunctionType
ALU = mybir.AluOpType

D = 2048
NLAT = 2048
NCTX = 256
NTOK = NLAT + NCTX
DEPTH = 2
INW = 4608
OFF_Q, OFF_K, OFF_V, OFF_POOL, OFF_FOUR, OFF_GATE = 0, 1024, 1280, 1536, 2048, 2560
EPS = 1e-6
import os
KVAR = int(os.environ.get('KVAR', '0'))
GR = 512
SB_BYTES = 207 * 1024


class Sched:
    ENGS = ("pe", "act", "dve", "pool", "sp")

    def __init__(self):
        self.streams = {e: [] for e in self.ENGS}
        self.lastw = {}
        self.readers = {}
        self.dma_slots = {}

    @staticmethod
    def keys(ap):
        if isinstance(ap, tuple):
            return [ap]
        if type(ap.tensor).__name__.startswith("DRam"):
            return []
        es = mybir.dt.size(ap.dtype)
        dims = list(ap.ap)[1:]
        lo = ap.offset * es
        ext = 1
        for st, cnt in dims:
            ext += abs(st) * (cnt - 1)
        hi = lo + ext * es
        nm = ap.tensor.name
        if nm.startswith("pb"):
            return [(nm, 0)]
        return [(nm, g) for g in range(lo // GR, (hi - 1) // GR + 1)]

    def op(self, eng, fn, R=(), W=(), dma_slot=None):
        idx = len(self.streams[eng])
        me = (eng, idx)
        deps = set()
        rk = [k for a in R for k in self.keys(a)]
        wk = [k for a in W for k in self.keys(a)]
        for k in rk:
            w = self.lastw.get(k)
            if w is not None:
                deps.add(w)
            if eng != "pe" and k[0].startswith("pb"):
                for re_, r in self.readers.get(k, {}).items():
                    if re_ != eng:
                        deps.add(r)
        for k in wk:
            w = self.lastw.get(k)
            if w is not None:
                deps.add(w)
            for r in self.readers.get(k, {}).values():
                deps.add(r)
        deps.discard(me)
        dma_need = {}
        for (de, di) in deps:
            sl = self.streams[de][di]["dma_slot"]
            if sl is not None:
                dma_need["dma_" + sl] = 16 * self.dma_slots[sl]
        ins = dict(eng=eng, idx=idx, fn=fn, deps=deps, dma_slot=dma_slot, dma_val=None, signal=False, ticket=None,
                   dma_need=dma_need)
        if dma_slot is not None:
            c = self.dma_slots.get(dma_slot, 0) + 1
            self.dma_slots[dma_slot] = c
            ins["dma_val"] = 16 * c
        self.streams[eng].append(ins)
        for k in wk:
            self.lastw[k] = me
            self.readers[k] = {}
        for k in rk:
            self.readers.setdefault(k, {})[eng if dma_slot is None else me] = me
        return me

    def emit(self, nc, stack):
        for e in self.ENGS:
            for ins in self.streams[e]:
                for (de, di) in ins["deps"]:
                    d = self.streams[de][di]
                    if d["dma_slot"] is None and (de != e or e != "pe"):
                        d["signal"] = True
        sems = {}
        for e in self.ENGS:
            sems[e] = stack.enter_context(nc.semaphore("s_" + e))
            t = 0
            for ins in self.streams[e]:
                if ins["signal"]:
                    t += 1
                    ins["ticket"] = t
        for s in self.dma_slots:
            sems["dma_" + s] = stack.enter_context(nc.semaphore("d_" + s))
        block = stack.enter_context(nc.Block())
        streams = self.streams

        def run(engname, eng):
            waited = {}
            for ins in streams[engname]:
                need = dict(ins["dma_need"])
                for (de, di) in ins["deps"]:
                    d = streams[de][di]
                    if d["dma_slot"] is not None:
                        continue
                    elif de != engname or engname != "pe":
                        key, val = de, d["ticket"]
                    else:
                        continue
                    if val > need.get(key, 0):
                        need[key] = val
                for key, val in need.items():
                    if waited.get(key, 0) < val:
                        eng.wait_ge(sems[key], val)
                        waited[key] = val
                bi = ins["fn"](eng)
                if ins["dma_slot"] is not None:
                    bi.then_inc(sems["dma_" + ins["dma_slot"]], 16)
                elif ins["signal"]:
                    bi.then_inc(sems[engname], 1)

        @block.tensor
        def _(e):
            run("pe", e)

        @block.scalar
        def _(e):
            run("act", e)

        @block.vector
        def _(e):
            run("dve", e)

        @block.gpsimd
        def _(e):
            run("pool", e)

        @block.sync
        def _(e):
            run("sp", e)


def _consts():
    bf = ml_dtypes.bfloat16
    c = {}
    n = np.arange(NLAT, dtype=np.int64)
    ang = 2.0 * np.pi * ((n[:, None] * n[None, :]) % NLAT).astype(np.float64) / NLAT
    c["dftC"] = np.cos(ang).astype(bf)
    c["dftS"] = np.sin(ang).astype(bf)
    n2 = np.arange(NCTX, dtype=np.int64)
    ang2 = 2.0 * np.pi * ((n2[:, None] * n2[None, :]) % NCTX).astype(np.float64) / NCTX
    d256 = np.stack([np.cos(ang2), np.sin(ang2)], 0)
    d256 = d256.reshape(2, 2, 128, NCTX).transpose(2, 0, 1, 3)
    c["dft256"] = np.ascontiguousarray(d256).astype(bf)
    m = np.arange(128, dtype=np.int64)
    angc = 2.0 * np.pi * ((m[:, None] * m[None, :]) % 128).astype(np.float64) / 128
    s_lat = 1.0 / math.sqrt(NLAT * 128.0)
    s_ctx = 1.0 / math.sqrt(NCTX * 128.0)
    ccs = np.stack([np.cos(angc) * s_lat, -np.sin(angc) * s_lat, np.cos(angc) * s_ctx, -np.sin(angc) * s_ctx], 1)
    c["ccs"] = np.ascontiguousarray(ccs).astype(bf)
    N3 = 384
    band = np.zeros((128, 20, 128), np.float64)
    t = np.arange(N3)
    for gi, win in enumerate((2, 4, 8, 16)):
        lo = np.clip(t - win // 2, 0, N3 - 1)
        hi = np.clip(t + (win - win // 2) - 1, 0, N3 - 1)
        cnt = (hi - lo + 1).astype(np.float64)
        B = np.zeros((N3, N3), np.float64)
        for tt in range(N3):
            B[lo[tt]:hi[tt] + 1, tt] = 1.0 / cnt[tt]
        B -= np.eye(N3)
        band[:, gi * 5 + 0, :] = B[128:256, 128:256]
        band[:, gi * 5 + 1, :] = B[0:128, 0:128]
        band[:, gi * 5 + 2, :] = B[256:384, 256:384]
        band[:, gi * 5 + 3, :] = B[0:128, 128:256]
        band[:, gi * 5 + 4, :] = B[128:256, 0:128]
    c["band"] = band.astype(bf)
    tok = np.arange(NLAT)
    row = (tok // 64).astype(np.float32)
    col = (tok % 64).astype(np.float32)
    inv = (np.float32(10000.0) ** (-np.arange(0, 64, 2, dtype=np.float32) / np.float32(64))).astype(np.float32)
    angr = np.stack([row[:, None] * inv[None, :], col[:, None] * inv[None, :]], 1).astype(np.float32)
    c["ropeC"] = np.cos(angr).reshape(NLAT, 64).astype(np.float32)
    c["ropeS"] = np.sin(angr).reshape(NLAT, 64).astype(np.float32)
    c["identF"] = np.eye(128, dtype=np.float32)
    c["identB"] = np.eye(128, dtype=np.float32).astype(bf)
    c["onesB"] = np.ones((128, 128), np.float32).astype(bf)
    return c


_CONST_CACHE = {}


def get_consts():
    if not _CONST_CACHE:
        _CONST_CACHE.update(_consts())
    return _CONST_CACHE


class _Stop(Exception):
    pass


def build_program(debug=False, stop_after=None, wseq=None):
    nc = bass.Bass("TRN2", target_bir_lowering=False)
    S = Sched()

    marks = []

    def mark(name):
        marks.append((name, len(S.streams['pe'])))
        if stop_after is not None and name == stop_after:
            raise _Stop()
    stack = ExitStack()

    def din(name, shape, dt=F32):
        return nc.dram_tensor(name, list(shape), dt, kind="ExternalInput").ap()

    x_in = din("x", [NLAT, D])
    ctx_in = din("ctx", [NCTX, D])
    cvec = din("cvec", [128, 32])
    ada_w = din("ada_w", [DEPTH, D, 3 * D])
    ada_b2 = din("ada_b2", [DEPTH, 2, 3 * D])
    ngcol = din("ngcol", [128, 32])
    fngb = din("fngb", [128, D])
    w_in = din("w_in", [DEPTH, D, INW])
    w_out = din("w_out", [DEPTH, D, D])
    gains = din("gains", [DEPTH, 128, 256])
    pool_w = din("pool_w", [DEPTH, 4, 128, 128])
    pscol = din("pscol", [128, 8])
    four_w = din("fourier_w", [DEPTH, 4, 128, 128])
    dftC = din("dftC", [NLAT, NLAT], BF16)
    dftS = din("dftS", [NLAT, NLAT], BF16)
    dft256_d = din("dft256", [128, 2, 2, NCTX], BF16)
    ccs_d = din("ccs", [128, 4, 128], BF16)
    band_d = din("band", [128, 20, 128], BF16)
    ropeC = din("ropeC", [NLAT, 64])
    ropeS = din("ropeS", [NLAT, 64])
    identF_d = din("identF", [128, 128])
    identB_d = din("identB", [128, 128], BF16)
    onesB_d = din("onesB", [128, 128], BF16)
    out_d = nc.dram_tensor("out", [NLAT, D], F32, kind="ExternalOutput").ap()
    x1_d = nc.dram_tensor("x1s", [NLAT, D], F32, kind="ExternalOutput" if debug else "Internal").ap()
    xc1_d = nc.dram_tensor("xc1s", [NCTX, D], F32, kind="ExternalOutput" if debug else "Internal").ap()
    wc_d = nc.dram_tensor("wcache", [DEPTH, 11, 128, 16 * 512], BF16).ap()
    modrow_h = nc.dram_tensor("modrow", [DEPTH, 2, 3 * D], F32)
    modrow_d = modrow_h.ap()

    SB = stack.enter_context(nc.sbuf_tensor("SB", [128, SB_BYTES // 4], F32))
    cur = [0]

    def alloc(nbytes, align=512):
        o = (cur[0] + align - 1) // align * align
        cur[0] = o + nbytes
        assert cur[0] <= SB_BYTES, ("SBUF overflow", cur[0])
        return o

    def vf(off, n):
        return SB[:, off // 4: off // 4 + n]

    def vb(off, n):
        return SB[:, off // 4: off // 4 + (n + 1) // 2].bitcast(BF16)[:, 0:n]

    def r3(v, b):
        return v.rearrange("p (a b) -> p a b", b=b)

    o_hT = alloc(16 * NTOK * 2)
    hT = r3(vb(o_hT, 16 * NTOK), NTOK)
    o_KT = alloc(2 * NTOK * 2)
    KT = r3(vb(o_KT, 2 * NTOK), NTOK)
    o_V = alloc(18 * 256 * 2)
    Vv = r3(vb(o_V, 18 * 256), 256)
    o_ReT = alloc(4 * NTOK * 2)
    ReT = r3(vb(o_ReT, 4 * NTOK), NTOK)
    o_wb = [alloc(16 * 512 * 2), alloc(16 * 512 * 2)]
    wbuf = [r3(vb(o, 16 * 512), 512) for o in o_wb]
    identF = vf(alloc(512), 128)
    identB = vb(alloc(256, 256), 128)
    onesB = vb(alloc(256, 256), 128)
    band = r3(vb(alloc(20 * 256), 20 * 128), 128)
    dft256 = vb(alloc(2048), 1024).rearrange("p (m j k) -> p m j k", m=2, j=2)
    ccs = r3(vb(alloc(1024), 512), 128)
    poolw = r3(vb(alloc(1024), 512), 128)
    fourw = r3(vb(alloc(1024), 512), 128)
    gains_s = vf(alloc(1024), 256)
    o_small = alloc(2048)
    sc3 = r3(vb(o_small, 32), 2)
    cv = vf(o_small + 128, 32)
    cvt = vf(o_small + 256, 32)
    ngc = vf(o_small + 384, 32)
    psc = vf(o_small + 512, 8)
    mcols = vf(o_small + 576, 64).rearrange("p (w k j) -> p w k j", w=2, k=2)
    mhalf = vf(o_small + 832, 16)
    ssb = vf(alloc(64, 512), 16)
    rsb = vf(alloc(64, 512), 16)
    ssq = vf(alloc(128, 512), 32)
    rfin = vf(alloc(64, 512), 4)
    o_gp = [alloc(2048), alloc(2048)]
    gpiece = [vf(o, 512) for o in o_gp]
    cols_sb = vf(alloc(256, 256), 64).rearrange("p (c w) -> p c w", w=2)
    halo_prev = vb(alloc(1024), 512)
    o_rope = alloc(1024)
    ropec = [vf(o_rope, 64), vf(o_rope + 512, 64)]
    ropes = [vf(o_rope + 256, 64), vf(o_rope + 768, 64)]
    o_R = alloc(0, 1024)
    R_BYTES = SB_BYTES - o_R

    class Arena:
        def __init__(self):
            self.c = 0

        def a(self, nbytes, align=512):
            o = (self.c + align - 1) // align * align
            self.c = o + nbytes
            assert self.c <= R_BYTES, ("scratch overflow", self.c, R_BYTES)
            return o_R + o

    banks = [stack.enter_context(nc.psum_tensor("pb%d" % i, [128, 512], F32)) for i in range(8)]

    def bankb(i):
        return banks[i][:, :].bitcast(BF16)

    rr = {}

    def nxt(cls, lst):
        i = rr.get(cls, 0)
        rr[cls] = i + 1
        return lst[i % len(lst)]

    def dma(eng, out, in_, slot, R=None, W=None):
        S.op(eng, lambda e, o=out, i=in_: e.dma_start(out=o, in_=i), R=R if R is not None else [in_],
             W=W if W is not None else [out], dma_slot=slot + "_" + eng)

    def mm(out, lhsT, rhs, start, stop, extraR=()):
        S.op("pe", lambda e, o=out, l=lhsT, r=rhs, s=start, t=stop: e.matmul(o, l, r, start=s, stop=t),
             R=[lhsT, rhs] + list(extraR), W=[out])

    def tr(out, in_, ident):
        S.op("pe", lambda e, o=out, i=in_, d=ident: e.transpose(o, i, d), R=[in_, ident], W=[out])

    def act(out, in_, func, bias=None, scale=None, accum=None, extraR=()):
        kw = {}
        Rl = [in_] + list(extraR)
        Wl = [out]
        if bias is not None:
            kw["bias"] = bias
            if not isinstance(bias, float):
                Rl.append(bias)
        if scale is not None:
            kw["scale"] = scale
            if not isinstance(scale, float):
                Rl.append(scale)
        if accum is not None:
            kw["accum_out"] = accum
            Wl.append(accum)
        S.op("act", lambda e, o=out, i=in_, f=func, k=kw: e.activation(o, i, f, **k), R=Rl, W=Wl)

    def ts(eng, out, in0, s1, s2, op0, op1=None):
        Rl = [in0] + [s for s in (s1, s2) if s is not None and not isinstance(s, float)]
        if op1 is None:
            S.op(eng, lambda e, o=out, i=in0, a=s1, p=op0: e.tensor_scalar(o, i, a, None, p), R=Rl, W=[out])
        else:
            S.op(eng, lambda e, o=out, i=in0, a=s1, b=s2, p=op0, q=op1: e.tensor_scalar(o, i, a, b, p, q), R=Rl, W=[out])

    def tt(eng, out, in0, in1, op):
        S.op(eng, lambda e, o=out, a=in0, b=in1, p=op: e.tensor_tensor(o, a, b, p), R=[in0, in1], W=[out])

    def stt(out, in0, scalar, in1, op0, op1):
        Rl = [in0, in1] + ([scalar] if not isinstance(scalar, float) else [])
        S.op("dve", lambda e, o=out, a=in0, s=scalar, b=in1, p=op0, q=op1: e.scalar_tensor_tensor(o, a, s, b, p, q),
             R=Rl, W=[out])

    def cp(eng, out, in_):
        if eng == "act":
            S.op("act", lambda e, o=out, i=in_: e.activation(o, i, AF.Copy), R=[in_], W=[out])
        else:
            S.op(eng, lambda e, o=out, i=in_: e.tensor_copy(o, i), R=[in_], W=[out])

    def recip(out, in_):
        S.op("dve", lambda e, o=out, i=in_: e.reciprocal(o, i), R=[in_], W=[out])

    def rstd_from_ss(ss, rs, n, inv_n):
        ts("pool", rs, ss, float(inv_n), float(EPS), ALU.mult, ALU.add)
        tt("pool", rs, rs, mhalf[:, 0:n], ALU.pow)

    def ap4(v, off_el, dims):
        ps = list(v.ap)[0][0]
        return bass.AP(v.tensor, v.offset + off_el, [[ps, 128]] + [list(d) for d in dims])

    wcount = [0]

    wrec = []
    wseen = set()

    wpending = []

    def issue_w(k, ent):
        src3, eng, ckey = ent
        buf = wbuf[k % 2]
        if ckey is not None and ckey in wseen:
            lyr, idx = ckey
            dma("pool", buf.rearrange("p a b -> p (a b)"), wc_d[lyr, idx], "w%d" % (k % 2), R=[("D", "wc", lyr, idx)])
            return
        dma(eng, buf[:, :, :], src3, "w%d" % (k % 2))
        if ckey is not None:
            wseen.add(ckey)
            wpending.append((k, ckey))

    def flush_w_stores(upto):
        while wpending and wpending[0][0] <= upto:
            k, (lyr, idx) = wpending.pop(0)
            dma("sp", wc_d[lyr, idx], wbuf[k % 2].rearrange("p a b -> p (a b)"), "wst", W=[("D", "wc", lyr, idx)])

    def load_w(src3, eng="pool", ckey=None):
        k = wcount[0]
        wcount[0] += 1
        wrec.append((src3, eng, ckey))
        if wseq is None:
            dma(eng, wbuf[k % 2][:, :, :], src3, "w%d" % (k % 2))
        else:
            if k == 0:
                issue_w(0, wseq[0])
            flush_w_stores(k)
            if k + 1 < len(wseq):
                issue_w(k + 1, wseq[k + 1])
        return wbuf[k % 2]

    def wsrc(wd, l, c0, ncol=512):
        return wd[l, :, c0:c0 + ncol].rearrange("(j p) n -> p j n", p=128)

    for dst, src in ((identF, identF_d), (identB, identB_d), (onesB, onesB_d),
                     (band, band_d), (dft256, dft256_d), (ccs, ccs_d),
                     (cv, cvec), (ngc, ngcol), (psc, pscol)):
        dma("sp", dst, src, "const")
    S.op("pool", lambda e: e.memset(mhalf, -0.5), W=[mhalf])
    act(cvt, cv, AF.Tanh, scale=0.5)
    stt(cvt, cvt, 1.0, cv, ALU.add, ALU.mult)
    ts("dve", sc3.rearrange("p a b -> p (a b)"), cvt, 0.5, None, ALU.mult)

    xsrc = {("lat", 0): x_in, ("ctx", 0): ctx_in, ("lat", 1): x1_d, ("ctx", 1): xc1_d}
    xdst = {("lat", 0): x1_d, ("ctx", 0): xc1_d, ("lat", 1): out_d}

    def xkey(tensor, tile, cb):
        return ("D", tensor.tensor.name, tile, cb)

    def xkeys(tensor, tile):
        return [xkey(tensor, tile, cb) for cb in range(4)]

    def ada_block(l, nb, mp, abp, par):
        wv = load_w(wsrc(ada_w, l, nb * 512))
        dma("sp", abp, ada_b2[l, :, nb * 512:(nb + 1) * 512], "adab%d" % par)
        acc = banks[0][0:2, :]
        for j in range(16):
            mm(acc, sc3[:, j, :], wv[:, j, :], j == 0, j == 15)
        tt("dve", mp, acc, abp, ALU.add)
        dma("sp", modrow_d[l, :, nb * 512:(nb + 1) * 512], mp, "modrow", W=[("D", "modrow", l, nb)])
        if nb < 8:
            cps = banks[1][:, 0:8].rearrange("p (c w) -> p c w", w=2)
            for q in range(4):
                tr(cps[:, q, :], mp[:, q * 128:(q + 1) * 128], identF[0:2, 0:2])
            cp("dve", cols_sb[:, nb * 4:(nb + 1) * 4, :], cps)

    def ada_finish(l):
        for which in range(2):
            cp("dve", mcols[:, which, 1, :], cols_sb[:, 0:16, which])
            stt(mcols[:, which, 0, :], cols_sb[:, 16:32, which], 1.0, ngc[:, l * 16:(l + 1) * 16], ALU.add, ALU.mult)

    deferred = []
    try:
      for l in range(DEPTH):
          last = l == DEPTH - 1
          dma("pool", poolw, pool_w[l].rearrange("g c d -> c g d"), "lw")
          dma("pool", fourw, four_w[l].rearrange("g c d -> c g d"), "lw")
          dma("sp", gains_s, gains[l], "lw2")

          if l == 0:
              ar = Arena()
              o_mp = [ar.a(2048), ar.a(2048)]
              o_ab = [ar.a(2048), ar.a(2048)]
              for nb in range(8):
                  ada_block(0, nb, vf(o_mp[nb % 2], 512)[0:2, :], vf(o_ab[nb % 2], 512)[0:2, :], nb % 2)
          ada_finish(l)

          mark('ada%d' % l)
          ar = Arena()
          o_xt = [ar.a(8192), ar.a(8192)]
          o_junk = ar.a(4096)
          junk = vb(o_junk, 2048)
          o_mp = [ar.a(2048), ar.a(2048)]
          o_ab = [ar.a(2048), ar.a(2048)]
          seqs = [("ctx", t) for t in range(2)] + [("lat", t) for t in range(16)]
          def p1_a(ti):
              which, t = seqs[ti]
              src = xsrc[(which, l)]
              xt = vf(o_xt[ti % 2], 2048)
              dma("sp", xt, src[t * 128:(t + 1) * 128, :], "xt%d" % (ti % 2),
                  R=xkeys(src, t) if l > 0 else [])
              ss = ssb[:, 12 + ti % 2:13 + ti % 2]
              act(junk, xt, AF.Square, accum=ss)
              rs = rsb[:, 12 + ti % 2:13 + ti % 2]
              rstd_from_ss(ss, rs, 1, 1.0 / D)
              ts("dve", xt, xt, rs, None, ALU.mult)

          def p1_b(ti):
              which, t = seqs[ti]
              wi = 0 if which == "lat" else 1
              xt = vf(o_xt[ti % 2], 2048)
              gt = ti
              pbs = [0, 1, 2, 3] if ti % 2 == 0 else [4, 5, 6, 7]
              for jb in range(4):
                  for j in range(jb * 4, jb * 4 + 4):
                      pb = banks[pbs[jb]][:, (j % 4) * 128:(j % 4 + 1) * 128]
                      tr(pb, xt[:, j * 128:(j + 1) * 128], identF)
                  for j in range(jb * 4, jb * 4 + 4):
                      pb = banks[pbs[jb]][:, (j % 4) * 128:(j % 4 + 1) * 128]
                      dst = hT[:, j, gt * 128:(gt + 1) * 128]
                      if jb % 2 == 0:
                          act(dst, pb, AF.Identity, bias=mcols[:, wi, 1, j:j + 1], scale=mcols[:, wi, 0, j:j + 1])
                      else:
                          ts("dve", dst, pb, mcols[:, wi, 0, j:j + 1], mcols[:, wi, 1, j:j + 1], ALU.mult, ALU.add)

          p1_a(0)
          for ti in range(len(seqs)):
              if ti + 1 < len(seqs):
                  p1_a(ti + 1)
              p1_b(ti)
              if ti % 4 == 3 and ti // 4 < 4:
                  nb = 8 + ti // 4
                  ada_block(l, nb, vf(o_mp[nb % 2], 512)[0:2, :], vf(o_ab[nb % 2], 512)[0:2, :], nb % 2)

          mark('p1_%d' % l)
          ar = Arena()
          o_kx = ar.a(1024)
          o_ta = ar.a(512)
          o_tb = ar.a(512)
          o_kr = [ar.a(512), ar.a(512)]
          junk = vb(ar.a(512), 256)
          wv = load_w(wsrc(w_in, l, OFF_K))
          def kv_mm(gt):
              pb = banks[gt % 2]
              for j in range(16):
                  mm(pb[:, :], hT[:, j, gt * 128:(gt + 1) * 128], wv[:, j, :], j == 0, j == 15)

          kv_mm(0)
          for gt in range(18):
              is_lat = gt >= 2
              pb = banks[gt % 2]
              if gt + 1 < 18:
                  kv_mm(gt + 1)
              so = 2 * (gt % 2)
              ss = ssb[:, so:so + 2]
              rs = rsb[:, so:so + 2]
              for h in range(2):
                  act(junk[:, 0:128], pb[:, h * 128:(h + 1) * 128], AF.Square, accum=ssb[:, so + h:so + h + 1])
              rstd_from_ss(ss, rs, 2, 1.0 / 128)
              kr = vb(o_kr[gt % 2], 256)
              kx = vf(o_kx, 256)
              if is_lat:
                  t = gt - 2
                  rc, rsn = ropec[gt % 2], ropes[gt % 2]
                  dma("sp", rc, ropeC[t * 128:(t + 1) * 128, :], "rope%d" % (gt % 2))
                  dma("sp", rsn, ropeS[t * 128:(t + 1) * 128, :], "rope%d" % (gt % 2))
              for h in range(2):
                  stt(kx[:, h * 128:(h + 1) * 128] if is_lat else kr[:, h * 128:(h + 1) * 128],
                      pb[:, h * 128:(h + 1) * 128], rsb[:, so + h:so + h + 1], gains_s[:, 128:256], ALU.mult, ALU.mult)
              cp("act", Vv[:, gt, :], pb[:, 256:512])
              if is_lat:
                  rope(S, tt, ap4, kx, kr, vf(o_ta, 128), vf(o_tb, 128), rc, rsn, 2)
              kb = nxt("B", [2, 3])
              pbt = bankb(kb)
              for h in range(2):
                  tr(pbt[:, h * 128:(h + 1) * 128], kr[:, h * 128:(h + 1) * 128], identB)
              for h in range(2):
                  cp("dve", KT[:, h, gt * 128:(gt + 1) * 128], pbt[:, h * 128:(h + 1) * 128])

          mark('g1_%d' % l)
          ar = Arena()
          o_tm = ar.a(18 * 512 * 2)
          tm = r3(vb(o_tm, 18 * 512), 512)
          o_pT2 = ar.a(2 * 4 * 512 * 2)
          pT2 = vb(o_pT2, 4096).rearrange("p (m g k) -> p m g k", m=2, g=4)
          wv = load_w(wsrc(w_in, l, OFF_FOUR))
          ftiles = list(range(18)) if not last else list(range(2, 18))
          for gt in ftiles:
              pb = banks[nxt("A", [0, 1])]
              for j in range(16):
                  mm(pb[:, :], hT[:, j, gt * 128:(gt + 1) * 128], wv[:, j, :], j == 0, j == 15)
              cp("act" if gt % 2 == 0 else "dve", tm[:, gt, :], pb[:, :])
          if not last:
              for mat in range(2):
                  for g in range(4):
                      pb = banks[nxt("A", [0, 1])]
                      for j in range(2):
                          mm(pb[:, 0:256], tm[:, j, g * 128:(g + 1) * 128], dft256[:, mat, j, :], j == 0, j == 1)
                      cp("act" if g % 2 == 0 else "dve", pT2[:, mat, g, 0:256], pb[:, 0:256])
              for g in range(4):
                  pb = banks[nxt("C", [4, 5])]
                  mm(pb[:, 0:256], ccs[:, 2, :], pT2[:, 0, g, 0:256], True, False)
                  mm(pb[:, 0:256], ccs[:, 3, :], pT2[:, 1, g, 0:256], False, True)
                  cp("act" if g % 2 == 0 else "dve", ReT[:, g, 0:256], pb[:, 0:256])
          for kb in range(4):
              for mat, dsrc in enumerate((dftC, dftS)):
                  dv = load_w(dsrc[:, kb * 512:(kb + 1) * 512].rearrange("(j p) k -> p j k", p=128), eng="pool")
                  for g in range(4):
                      pb = banks[nxt("A", [0, 1, 2, 3])]
                      for j in range(16):
                          mm(pb[:, :], tm[:, 2 + j, g * 128:(g + 1) * 128], dv[:, j, :], j == 0, j == 15)
                      cp("act" if g % 2 == 0 else "dve", pT2[:, mat, g, :], pb[:, :])
              for g in range(4):
                  pb = banks[nxt("C", [4, 5])]
                  mm(pb[:, :], ccs[:, 0, :], pT2[:, 0, g, :], True, False)
                  mm(pb[:, :], ccs[:, 1, :], pT2[:, 1, g, :], False, True)
                  cp("act" if g % 2 == 0 else "dve", ReT[:, g, 256 + kb * 512:256 + (kb + 1) * 512], pb[:, :])

          mark('g3_%d' % l)
          groups = ([] if last else [("ctx", 0, NCTX, 0)]) + [("lat", 256 + g * 512, 512, g * 4) for g in range(4)]
          for (which, tok0, ntok, T0) in groups:
              wi = 0 if which == "lat" else 1
              is_lat = which == "lat"
              NT = 16 if is_lat else 2
              ntile = ntok // 128
              ktiles = list(range(18)) if is_lat else [0, 1]
              src_x = xsrc[(which, l)]
              dst_x = xdst[(which, l)]
              ar = Arena()
              o_q = ar.a(6 * 1024, 1024)
              QT = r3(vb(o_q, 4 * ntok), ntok)
              qx = vf(o_q + 4096, 512)
              up_tm = r3(vb(o_q, 6 * 512), 512)
              mgT = r3(vb(ar.a(16 * ntok * 2), 16 * ntok), ntok)
              o_PT = [ar.a(ntok * 2) for _ in range(3)]
              PT = [vb(o, ntok) for o in o_PT]
              tmpAf = [vf(ar.a(2048), 512) for _ in range(2)]
              tmpA = [v[:, 0:ntok] for v in tmpAf]
              o_tO = [ar.a(2048) for _ in range(2)]
              tmpO = [vf(o, 512)[:, 0:ntok] for o in o_tO]
              o_qr = [ar.a(1024) for _ in range(2)]
              qr = [vb(o, 512) for o in o_qr]
              o_ta = ar.a(1024)
              o_tb = ar.a(1024)
              ta = vf(o_ta, 256)
              tb = vf(o_tb, 256)
              x2full = x2hole = None
              if last:
                  assert o_PT[1] == o_PT[0] + 1024 and o_qr[1] == o_qr[0] + 1024 and o_tb == o_ta + 1024
                  x2full = [vf(o_q, 512), vf(o_q + 2048, 512), vf(o_q + 4096, 512), vf(o_tO[0], 512), vf(o_tO[1], 512),
                            vf(o_qr[0], 512), vf(o_ta, 512), vf(o_PT[0], 512)]
                  x2hole = [vf(o_hT + (j * NTOK + tok0) * 2, 256) for j in range(16)]
              plT = [vb(ar.a(ntok * 2), ntok)] * 2
              xp = [vf(ar.a(2048), 512) for _ in range(2)]

              def gate_chunk(fc, wg, branch, branch_in_psum):
                  gb = banks[nxt("B", [2, 3])]
                  hcol = (fc % 4) * 128
                  for j in range(16):
                      mm(gb[:, 0:ntok], wg[:, j, hcol:hcol + 128], hT[:, j, tok0:tok0 + ntok], j == 0, j == 15)
                  th = tmpA[fc % 2]
                  act(th, gb[:, 0:ntok], AF.Tanh, scale=0.5)
                  stt(th, th, 1.0, gb[:, 0:ntok], ALU.add, ALU.mult)
                  stt(mgT[:, fc, :], th, 0.5, branch, ALU.mult, ALU.mult)

              for hb in range(2):
                  wq = load_w(wsrc(w_in, l, OFF_Q + hb * 512), ckey=(l, hb))
                  def q_mm(tti):
                      pb = banks[tti % 2]
                      c0 = tok0 + tti * 128
                      for j in range(16):
                          mm(pb[:, :], hT[:, j, c0:c0 + 128], wq[:, j, :], j == 0, j == 15)

                  q_mm(0)
                  for tti in range(ntile):
                      pb = banks[tti % 2]
                      if tti + 1 < ntile:
                          q_mm(tti + 1)
                      so = 4 + 4 * (tti % 2)
                      for h in range(4):
                          act(PT[2][:, 0:128], pb[:, h * 128:(h + 1) * 128], AF.Square, accum=ssb[:, so + h:so + h + 1])
                      mark('q1')
                      rstd_from_ss(ssb[:, so:so + 4], rsb[:, so:so + 4], 4, 1.0 / 128)
                      mark('q2')
                      qrv = qr[tti % 2]
                      if is_lat:
                          t = T0 + tti
                          rc, rsn = ropec[tti % 2], ropes[tti % 2]
                          dma("sp", rc, ropeC[t * 128:(t + 1) * 128, :], "rope%d" % (tti % 2))
                          dma("sp", rsn, ropeS[t * 128:(t + 1) * 128, :], "rope%d" % (tti % 2))
                      for h in range(4):
                          stt(qx[:, h * 128:(h + 1) * 128] if is_lat else qrv[:, h * 128:(h + 1) * 128],
                              pb[:, h * 128:(h + 1) * 128], rsb[:, so + h:so + h + 1], gains_s[:, 0:128], ALU.mult, ALU.mult)
                      if is_lat:
                          rope(S, tt, ap4, qx, qrv, ta, tb, rc, rsn, 4)
                      mark('q3')
                      pbt = bankb(nxt("B", [2, 3]))
                      for h in range(4):
                          tr(pbt[:, h * 128:(h + 1) * 128], qrv[:, h * 128:(h + 1) * 128], identB)
                      mark('q4')
                      cp("dve" if tti % 2 == 0 else "act", QT[:, :, tti * 128:(tti + 1) * 128],
                         pbt[:, 0:512].rearrange("p (h d) -> p h d", h=4))
                      mark('q5')
                  mark('qdone')
                  wg = load_w(wsrc(w_in, l, OFF_GATE + hb * 512), ckey=(l, 2 + hb))
                  for h in range(4):
                      fc = hb * 4 + h
                      Ob, Lb = banks[6], banks[7]
                      nk = len(ktiles)

                      def s_mm(ki):
                          kt = ktiles[ki]
                          sb = banks[(4, 5, 0, 1)[ki % 4]]
                          mm(sb[:, 0:ntok], KT[:, hb, kt * 128:(kt + 1) * 128], QT[:, h, :], True, True)
                          act(PT[ki % 3], sb[:, 0:ntok], AF.Exp, scale=float(128.0 ** -0.5))

                      def pv_mm(ki):
                          kt = ktiles[ki]
                          mm(Ob[:, 0:ntok], Vv[:, kt, hb * 128:(hb + 1) * 128], PT[ki % 3], ki == 0, ki == nk - 1)
                          mm(Lb[:, 0:ntok], onesB, PT[ki % 3], ki == 0, ki == nk - 1)

                      s_mm(0)
                      if nk > 1:
                          s_mm(1)
                      for ki in range(nk):
                          if ki + 2 < nk:
                              s_mm(ki + 2)
                          pv_mm(ki)
                      mark('attA')
                      tO = tmpO[fc % 2]
                      recip(tO, Lb[:, 0:ntok])
                      tt("dve", tO, Ob[:, 0:ntok], tO, ALU.mult)
                      mark('attB')
                      gate_chunk(fc, wg, tO, False)
                      mark('attC')

              mark('att_%d_%s_%d' % (l, which, T0))
              if l == 0 and is_lat:
                  for nb in (T0 // 2, T0 // 2 + 1):
                      ada_block(1, nb, gpiece[0][0:2, :], gpiece[1][0:2, :], nb % 2)
              while deferred:
                  deferred.pop(0)()
              wp = load_w(wsrc(w_in, l, OFF_POOL), ckey=(l, 6))
              Tlo = max(T0 - 1, 0)
              Thi = min(T0 + ntile, NT - 1)
              seq_tok0 = 256 if is_lat else 0
              use_halo = last and T0 > 0
              for T in range(Tlo, Thi + 1):
                  if use_halo and T == T0 - 1:
                      continue
                  pb = banks[nxt("A", [0, 1])]
                  c0 = seq_tok0 + T * 128
                  for j in range(16):
                      mm(pb[:, :], hT[:, j, c0:c0 + 128], wp[:, j, :], j == 0, j == 15)
                  cp("act" if T % 2 == 0 else "dve", up_tm[:, T - Tlo, :], pb[:, :])

              def up_src(Tn, g):
                  if use_halo and Tn == T0 - 1:
                      return halo_prev[:, g * 128:(g + 1) * 128]
                  return up_tm[:, Tn - Tlo, g * 128:(g + 1) * 128]

              wg = load_w(wsrc(w_in, l, OFF_GATE + 2 * 512), ckey=(l, 4))
              for g in range(4):
                  fc = 8 + g
                  pb = banks[nxt("C", [4, 5])]
                  for tti in range(ntile):
                      T = T0 + tti
                      terms = []
                      if T > 0:
                          terms.append((T - 1, 3))
                      terms.append((T, 1 if T == 0 else (2 if T == NT - 1 else 0)))
                      if T < NT - 1:
                          terms.append((T + 1, 4))
                      for i, (Tn, kind) in enumerate(terms):
                          mm(pb[:, tti * 128:(tti + 1) * 128], up_src(Tn, g),
                             band[:, g * 5 + kind, :], i == 0, i == len(terms) - 1)
                  pl = plT[g % 2]
                  cp("act", pl, pb[:, 0:ntok])
                  yb = banks[nxt("D", [6, 7])]
                  mm(yb[:, 0:ntok], poolw[:, g, :], pl, True, True)
                  tO = tmpO[fc % 2]
                  ts("dve", tO, yb[:, 0:ntok], psc[:, l * 4 + g:l * 4 + g + 1], None, ALU.mult)
                  gate_chunk(fc, wg, tO, False)

              if last and T0 + ntile < NT:
                  cp("act", halo_prev, up_tm[:, (T0 + ntile - 1) - Tlo, :])
              wg = load_w(wsrc(w_in, l, OFF_GATE + 3 * 512), ckey=(l, 5))
              for g in range(4):
                  fc = 12 + g
                  yb = banks[nxt("D", [6, 7])]
                  mm(yb[:, 0:ntok], fourw[:, g, :], ReT[:, g, tok0:tok0 + ntok], True, True)
                  gate_chunk(fc, wg, yb[:, 0:ntok], True)

              mark('four_%d_%s_%d' % (l, which, T0))
              gate_c0 = 2 * D
              pieces = [(cb, tti) for cb in range(4) for tti in range(ntile)]

              def gp_load(cb):
                  dma("sp", gpiece[cb % 2],
                      modrow_d[l, wi:wi + 1, gate_c0 + cb * 512:gate_c0 + (cb + 1) * 512].partition_broadcast(128).rearrange("p a n -> p (a n)"),
                      "gp%d" % (cb % 2), R=[("D", "modrow", l, 8 + cb)])

              def x_load(i):
                  cb, tti = pieces[i]
                  T = T0 + tti
                  dma("sp", xp[i % 2], src_x[T * 128:(T + 1) * 128, cb * 512:(cb + 1) * 512], "xp%d" % (i % 2),
                      R=[xkey(src_x, T, cb)] if l > 0 else [])

              gp_load(0)
              x_load(0)
              wo = None
              for i, (cb, tti) in enumerate(pieces):
                  T = T0 + tti
                  if tti == 0:
                      wo = load_w(wsrc(w_out, l, cb * 512), ckey=(l, 7 + cb))
                      if cb + 1 < 4:
                          gp_load(cb + 1)
                  if i + 1 < len(pieces):
                      x_load(i + 1)
                  gp = gpiece[cb % 2]
                  xpv = xp[i % 2]
                  pb = banks[nxt("W", [0, 1, 4, 5])]
                  for fc in range(16):
                      mm(pb[:, :], mgT[:, fc, tti * 128:(tti + 1) * 128], wo[:, fc, :], fc == 0, fc == 15)
                  tmpx = tmpAf[tti % 2]
                  tt("dve", tmpx, pb[:, :], gp, ALU.mult)
                  if not last:
                      tt("dve", xpv, tmpx, xpv, ALU.add)
                      dma("sp", dst_x[T * 128:(T + 1) * 128, cb * 512:(cb + 1) * 512], xpv, "xo%d" % (i % 2),
                          W=[xkey(dst_x, T, cb)])
                  else:
                      for hf in range(2):
                          if cb < 2:
                              dest = x2full[tti * 2 + cb][:, hf * 256:(hf + 1) * 256]
                          else:
                              dest = x2hole[(tti * 2 + cb - 2) * 2 + hf]
                          tt("dve", dest, tmpx[:, hf * 256:(hf + 1) * 256], xpv[:, hf * 256:(hf + 1) * 256], ALU.add)
                          act(tmpAf[(tti + 1) % 2][:, hf * 256:(hf + 1) * 256], dest, AF.Square,
                              accum=ssq[:, tti * 8 + cb * 2 + hf:tti * 8 + cb * 2 + hf + 1])
              mark('wout_%d_%s_%d' % (l, which, T0))
              if last:
                  for tti in range(ntile):
                      S.op("dve", lambda e, o=rfin[:, tti:tti + 1], i=ssq[:, tti * 8:(tti + 1) * 8]:
                           e.tensor_reduce(o, i, mybir.AxisListType.X, ALU.add),
                           R=[ssq[:, tti * 8:(tti + 1) * 8]], W=[rfin[:, tti:tti + 1]])
                      rstd_from_ss(rfin[:, tti:tti + 1], rfin[:, tti:tti + 1], 1, 1.0 / D)
                  for cb in range(4):
                      fgv = gpiece[cb % 2]
                      dma("sp", fgv, fngb[:, cb * 512:(cb + 1) * 512], "gp%d" % (cb % 2))
                      for tti in range(ntile):
                          T = T0 + tti
                          for hf in range(2):
                              if cb < 2:
                                  srcv = x2full[tti * 2 + cb][:, hf * 256:(hf + 1) * 256]
                              else:
                                  srcv = x2hole[(tti * 2 + cb - 2) * 2 + hf]
                              stt(srcv, srcv, rfin[:, tti:tti + 1], fgv[:, hf * 256:(hf + 1) * 256], ALU.mult, ALU.mult)
                              if cb >= 2:
                                  c0 = cb * 512 + hf * 256
                                  dma("sp", out_d[T * 128:(T + 1) * 128, c0:c0 + 256], srcv, "xo%d" % hf,
                                      W=[("D", out_d.tensor.name, T, cb, hf)])
                          if cb < 2:
                              dma("sp", out_d[T * 128:(T + 1) * 128, cb * 512:(cb + 1) * 512], x2full[tti * 2 + cb],
                                  "xo%d" % (tti % 2), W=[xkey(out_d, T, cb)])
      while deferred:
          deferred.pop(0)()

    except _Stop:
        pass

    allout = [xkey(out_d, T, cb) for T in range(16) for cb in range(2)]
    allout += [("D", out_d.tensor.name, T, cb, hf) for T in range(16) for cb in (2, 3) for hf in range(2)]
    if debug:
        allout += [xkey(x1_d, T, cb) for T in range(16) for cb in range(4)]
        allout += [xkey(xc1_d, T, cb) for T in range(2) for cb in range(4)]
    S.op("sp", lambda e: e.nop(), R=allout)
    S.emit(nc, stack)
    stack.close()
    nc._wrec = wrec
    nc._marks = marks
    return nc


def rope(S, tt, ap4, xin, xout, ta, tb, rc, rsn, H):
    dims = [[128, H], [64, 2], [1, 32]]
    x1 = ap4(xin, 0, dims)
    x2 = ap4(xin, 32, dims)
    o1 = ap4(xout, 0, dims)
    o2 = ap4(xout, 32, dims)
    tdims = [[64, H], [32, 2], [1, 32]]
    a = ap4(ta, 0, tdims)
    b = ap4(tb, 0, tdims)
    cdims = [[0, H], [32, 2], [1, 32]]
    c = ap4(rc, 0, cdims)
    s = ap4(rsn, 0, cdims)
    tt("dve", a, x1, c, ALU.mult)
    tt("dve", b, x2, s, ALU.mult)
    tt("dve", o1, a, b, ALU.subtract)
    tt("dve", a, x2, c, ALU.mult)
    tt("dve", b, x1, s, ALU.mult)
    tt("dve", o2, a, b, ALU.add)


def build_two_pass(debug=False, stop_after=None):
    rec = build_program(debug=debug, stop_after=stop_after)._wrec
    return build_program(debug=debug, stop_after=stop_after, wseq=rec)


_NC_CACHE = {}


def make_in_maps(x, c, ctx, c_ctx, ada_w, ada_b, norm_g, w_in, q_norm_g, k_norm_g,
                 pool_w, pool_scale, fourier_w, w_out, final_norm_g, cores):
    f = lambda a: np.ascontiguousarray(np.asarray(a, dtype=np.float32))
    x, c, ctx, c_ctx = f(x), f(c), f(ctx), f(c_ctx)
    ada_w, ada_b, norm_g, w_in = f(ada_w), f(ada_b), f(norm_g), f(w_in)
    q_norm_g, k_norm_g, pool_w, pool_scale = f(q_norm_g), f(k_norm_g), f(pool_w), f(pool_scale)
    fourier_w, w_out, final_norm_g = f(fourier_w), f(w_out), f(final_norm_g)
    consts = get_consts()
    shared = dict(consts)
    shared["ada_w"] = ada_w
    shared["ada_b2"] = np.ascontiguousarray(np.repeat(ada_b[:, None, :], 2, axis=1))
    shared["ngcol"] = np.ascontiguousarray(norm_g.reshape(DEPTH, 16, 128).transpose(2, 0, 1).reshape(128, 32))
    shared["fngb"] = np.ascontiguousarray(np.broadcast_to(final_norm_g[None, :], (128, D)))
    shared["w_in"] = w_in
    shared["w_out"] = w_out
    g = np.concatenate([q_norm_g, k_norm_g], axis=1)
    shared["gains"] = np.ascontiguousarray(np.broadcast_to(g[:, None, :], (DEPTH, 128, 256)))
    shared["pool_w"] = pool_w
    shared["pscol"] = np.ascontiguousarray(pool_scale.reshape(DEPTH, 4, 128).transpose(2, 0, 1).reshape(128, 8))
    shared["fourier_w"] = fourier_w
    cc = c_ctx.reshape(16, 128).T
    maps = []
    for b in cores:
        m = dict(shared)
        m["x"] = x[b]
        m["ctx"] = ctx[b]
        cb = c[b].reshape(16, 128).T
        m["cvec"] = np.ascontiguousarray(np.stack([cb, cc], axis=2).reshape(128, 32))
        maps.append(m)
    return maps


def kernel(x, c, ctx, c_ctx, ada_w, ada_b, norm_g, w_in, q_norm_g, k_norm_g,
           pool_w, pool_scale, fourier_w, w_out, final_norm_g):
    if "nc" not in _NC_CACHE:
        _NC_CACHE["nc"] = build_two_pass(debug=False)
    nc = _NC_CACHE["nc"]
    maps = make_in_maps(x, c, ctx, c_ctx, ada_w, ada_b, norm_g, w_in, q_norm_g, k_norm_g,
                        pool_w, pool_scale, fourier_w, w_out, final_norm_g, list(range(8)))
    res = run_bass_kernel_spmd(nc, maps, core_ids=list(range(8)))
    out = np.stack([np.asarray(r["out"], dtype=np.float32) for r in res.results], axis=0)
    return out
```

```python
import math
from contextlib import ExitStack
import numpy as np
import ml_dtypes
import concourse.bass as bass
import concourse.mybir as mybir
from concourse.bass_utils import run_bass_kernel_spmd

F32 = mybir.dt.float32
BF16 = mybir.dt.bfloat16
AF = mybir.ActivationFunctionType
ALU = mybir.AluOpType

D = 2048
NLAT = 2048
NCTX = 256
NTOK = NLAT + NCTX
DEPTH = 2
INW = 4608
OFF_Q, OFF_K, OFF_V, OFF_POOL, OFF_FOUR, OFF_GATE = 0, 1024, 1280, 1536, 2048, 2560
EPS = 1e-6
import os
KVAR = int(os.environ.get('KVAR', '0'))
GR = 512
SB_BYTES = 207 * 1024


class Sched:
    ENGS = ("pe", "act", "dve", "pool", "sp")

    def __init__(self):
        self.streams = {e: [] for e in self.ENGS}
        self.lastw = {}
        self.readers = {}
        self.dma_slots = {}

    @staticmethod
    def keys(ap):
        if isinstance(ap, tuple):
            return [ap]
        if type(ap.tensor).__name__.startswith("DRam"):
            return []
        es = mybir.dt.size(ap.dtype)
        dims = list(ap.ap)[1:]
        lo = ap.offset * es
        ext = 1
        for st, cnt in dims:
            ext += abs(st) * (cnt - 1)
        hi = lo + ext * es
        nm = ap.tensor.name
        if nm.startswith("pb"):
            return [(nm, 0)]
        return [(nm, g) for g in range(lo // GR, (hi - 1) // GR + 1)]

    def op(self, eng, fn, R=(), W=(), dma_slot=None):
        idx = len(self.streams[eng])
        me = (eng, idx)
        deps = set()
        rk = [k for a in R for k in self.keys(a)]
        wk = [k for a in W for k in self.keys(a)]
        for k in rk:
            w = self.lastw.get(k)
            if w is not None:
                deps.add(w)
            if eng != "pe" and k[0].startswith("pb"):
                for re_, r in self.readers.get(k, {}).items():
                    if re_ != eng:
                        deps.add(r)
        for k in wk:
            w = self.lastw.get(k)
            if w is not None:
                deps.add(w)
            for r in self.readers.get(k, {}).values():
                deps.add(r)
        deps.discard(me)
        dma_need = {}
        for (de, di) in deps:
            sl = self.streams[de][di]["dma_slot"]
            if sl is not None:
                dma_need["dma_" + sl] = 16 * self.dma_slots[sl]
        ins = dict(eng=eng, idx=idx, fn=fn, deps=deps, dma_slot=dma_slot, dma_val=None, signal=False, ticket=None,
                   dma_need=dma_need)
        if dma_slot is not None:
            c = self.dma_slots.get(dma_slot, 0) + 1
            self.dma_slots[dma_slot] = c
            ins["dma_val"] = 16 * c
        self.streams[eng].append(ins)
        for k in wk:
            self.lastw[k] = me
            self.readers[k] = {}
        for k in rk:
            self.readers.setdefault(k, {})[eng if dma_slot is None else me] = me
        return me

    def emit(self, nc, stack):
        for e in self.ENGS:
            for ins in self.streams[e]:
                for (de, di) in ins["deps"]:
                    d = self.streams[de][di]
                    if d["dma_slot"] is None and (de != e or e != "pe"):
                        d["signal"] = True
        sems = {}
        for e in self.ENGS:
            sems[e] = stack.enter_context(nc.semaphore("s_" + e))
            t = 0
            for ins in self.streams[e]:
                if ins["signal"]:
                    t += 1
                    ins["ticket"] = t
        for s in self.dma_slots:
            sems["dma_" + s] = stack.enter_context(nc.semaphore("d_" + s))
        block = stack.enter_context(nc.Block())
        streams = self.streams

        def run(engname, eng):
            waited = {}
            for ins in streams[engname]:
                need = dict(ins["dma_need"])
                for (de, di) in ins["deps"]:
                    d = streams[de][di]
                    if d["dma_slot"] is not None:
                        continue
                    elif de != engname or engname != "pe":
                        key, val = de, d["ticket"]
                    else:
                        continue
                    if val > need.get(key, 0):
                        need[key] = val
                for key, val in need.items():
                    if waited.get(key, 0) < val:
                        eng.wait_ge(sems[key], val)
                        waited[key] = val
                bi = ins["fn"](eng)
                if ins["dma_slot"] is not None:
                    bi.then_inc(sems["dma_" + ins["dma_slot"]], 16)
                elif ins["signal"]:
                    bi.then_inc(sems[engname], 1)

        @block.tensor
        def _(e):
            run("pe", e)

        @block.scalar
        def _(e):
            run("act", e)

        @block.vector
        def _(e):
            run("dve", e)

        @block.gpsimd
        def _(e):
            run("pool", e)

        @block.sync
        def _(e):
            run("sp", e)


def _consts():
    bf = ml_dtypes.bfloat16
    c = {}
    n = np.arange(NLAT, dtype=np.int64)
    ang = 2.0 * np.pi * ((n[:, None] * n[None, :]) % NLAT).astype(np.float64) / NLAT
    c["dftC"] = np.cos(ang).astype(bf)
    c["dftS"] = np.sin(ang).astype(bf)
    n2 = np.arange(NCTX, dtype=np.int64)
    ang2 = 2.0 * np.pi * ((n2[:, None] * n2[None, :]) % NCTX).astype(np.float64) / NCTX
    d256 = np.stack([np.cos(ang2), np.sin(ang2)], 0)
    d256 = d256.reshape(2, 2, 128, NCTX).transpose(2, 0, 1, 3)
    c["dft256"] = np.ascontiguousarray(d256).astype(bf)
    m = np.arange(128, dtype=np.int64)
    angc = 2.0 * np.pi * ((m[:, None] * m[None, :]) % 128).astype(np.float64) / 128
    s_lat = 1.0 / math.sqrt(NLAT * 128.0)
    s_ctx = 1.0 / math.sqrt(NCTX * 128.0)
    ccs = np.stack([np.cos(angc) * s_lat, -np.sin(angc) * s_lat, np.cos(angc) * s_ctx, -np.sin(angc) * s_ctx], 1)
    c["ccs"] = np.ascontiguousarray(ccs).astype(bf)
    N3 = 384
    band = np.zeros((128, 20, 128), np.float64)
    t = np.arange(N3)
    for gi, win in enumerate((2, 4, 8, 16)):
        lo = np.clip(t - win // 2, 0, N3 - 1)
        hi = np.clip(t + (win - win // 2) - 1, 0, N3 - 1)
        cnt = (hi - lo + 1).astype(np.float64)
        B = np.zeros((N3, N3), np.float64)
        for tt in range(N3):
            B[lo[tt]:hi[tt] + 1, tt] = 1.0 / cnt[tt]
        B -= np.eye(N3)
        band[:, gi * 5 + 0, :] = B[128:256, 128:256]
        band[:, gi * 5 + 1, :] = B[0:128, 0:128]
        band[:, gi * 5 + 2, :] = B[256:384, 256:384]
        band[:, gi * 5 + 3, :] = B[0:128, 128:256]
        band[:, gi * 5 + 4, :] = B[128:256, 0:128]
    c["band"] = band.astype(bf)
    tok = np.arange(NLAT)
    row = (tok // 64).astype(np.float32)
    col = (tok % 64).astype(np.float32)
    inv = (np.float32(10000.0) ** (-np.arange(0, 64, 2, dtype=np.float32) / np.float32(64))).astype(np.float32)
    angr = np.stack([row[:, None] * inv[None, :], col[:, None] * inv[None, :]], 1).astype(np.float32)
    c["ropeC"] = np.cos(angr).reshape(NLAT, 64).astype(np.float32)
    c["ropeS"] = np.sin(angr).reshape(NLAT, 64).astype(np.float32)
    c["identF"] = np.eye(128, dtype=np.float32)
    c["identB"] = np.eye(128, dtype=np.float32).astype(bf)
    c["onesB"] = np.ones((128, 128), np.float32).astype(bf)
    return c


_CONST_CACHE = {}


def get_consts():
    if not _CONST_CACHE:
        _CONST_CACHE.update(_consts())
    return _CONST_CACHE


class _Stop(Exception):
    pass


def build_program(debug=False, stop_after=None, wseq=None):
    nc = bass.Bass("TRN2", target_bir_lowering=False)
    S = Sched()

    marks = []

    def mark(name):
        marks.append((name, len(S.streams['pe'])))
        if stop_after is not None and name == stop_after:
            raise _Stop()
    stack = ExitStack()

    def din(name, shape, dt=F32):
        return nc.dram_tensor(name, list(shape), dt, kind="ExternalInput").ap()

    x_in = din("x", [NLAT, D])
    ctx_in = din("ctx", [NCTX, D])
    cvec = din("cvec", [128, 32])
    ada_w = din("ada_w", [DEPTH, D, 3 * D])
    ada_b2 = din("ada_b2", [DEPTH, 2, 3 * D])
    ngcol = din("ngcol", [128, 32])
    fngb = din("fngb", [128, D])
    w_in = din("w_in", [DEPTH, D, INW])
    w_out = din("w_out", [DEPTH, D, D])
    gains = din("gains", [DEPTH, 128, 256])
    pool_w = din("pool_w", [DEPTH, 4, 128, 128])
    pscol = din("pscol", [128, 8])
    four_w = din("fourier_w", [DEPTH, 4, 128, 128])
    dftC = din("dftC", [NLAT, NLAT], BF16)
    dftS = din("dftS", [NLAT, NLAT], BF16)
    dft256_d = din("dft256", [128, 2, 2, NCTX], BF16)
    ccs_d = din("ccs", [128, 4, 128], BF16)
    band_d = din("band", [128, 20, 128], BF16)
    ropeC = din("ropeC", [NLAT, 64])
    ropeS = din("ropeS", [NLAT, 64])
    identF_d = din("identF", [128, 128])
    identB_d = din("identB", [128, 128], BF16)
    onesB_d = din("onesB", [128, 128], BF16)
    out_d = nc.dram_tensor("out", [NLAT, D], F32, kind="ExternalOutput").ap()
    x1_d = nc.dram_tensor("x1s", [NLAT, D], F32, kind="ExternalOutput" if debug else "Internal").ap()
    xc1_d = nc.dram_tensor("xc1s", [NCTX, D], F32, kind="ExternalOutput" if debug else "Internal").ap()
    wc_d = nc.dram_tensor("wcache", [DEPTH, 11, 128, 16 * 512], BF16).ap()
    modrow_h = nc.dram_tensor("modrow", [DEPTH, 2, 3 * D], F32)
    modrow_d = modrow_h.ap()

    SB = stack.enter_context(nc.sbuf_tensor("SB", [128, SB_BYTES // 4], F32))
    cur = [0]

    def alloc(nbytes, align=512):
        o = (cur[0] + align - 1) // align * align
        cur[0] = o + nbytes
        assert cur[0] <= SB_BYTES, ("SBUF overflow", cur[0])
        return o

    def vf(off, n):
        return SB[:, off // 4: off // 4 + n]

    def vb(off, n):
        return SB[:, off // 4: off // 4 + (n + 1) // 2].bitcast(BF16)[:, 0:n]

    def r3(v, b):
        return v.rearrange("p (a b) -> p a b", b=b)

    o_hT = alloc(16 * NTOK * 2)
    hT = r3(vb(o_hT, 16 * NTOK), NTOK)
    o_KT = alloc(2 * NTOK * 2)
    KT = r3(vb(o_KT, 2 * NTOK), NTOK)
    o_V = alloc(18 * 256 * 2)
    Vv = r3(vb(o_V, 18 * 256), 256)
    o_ReT = alloc(4 * NTOK * 2)
    ReT = r3(vb(o_ReT, 4 * NTOK), NTOK)
    o_wb = [alloc(16 * 512 * 2), alloc(16 * 512 * 2)]
    wbuf = [r3(vb(o, 16 * 512), 512) for o in o_wb]
    identF = vf(alloc(512), 128)
    identB = vb(alloc(256, 256), 128)
    onesB = vb(alloc(256, 256), 128)
    band = r3(vb(alloc(20 * 256), 20 * 128), 128)
    dft256 = vb(alloc(2048), 1024).rearrange("p (m j k) -> p m j k", m=2, j=2)
    ccs = r3(vb(alloc(1024), 512), 128)
    poolw = r3(vb(alloc(1024), 512), 128)
    fourw = r3(vb(alloc(1024), 512), 128)
    gains_s = vf(alloc(1024), 256)
    o_small = alloc(2048)
    sc3 = r3(vb(o_small, 32), 2)
    cv = vf(o_small + 128, 32)
    cvt = vf(o_small + 256, 32)
    ngc = vf(o_small + 384, 32)
    psc = vf(o_small + 512, 8)
    mcols = vf(o_small + 576, 64).rearrange("p (w k j) -> p w k j", w=2, k=2)
    mhalf = vf(o_small + 832, 16)
    ssb = vf(alloc(64, 512), 16)
    rsb = vf(alloc(64, 512), 16)
    ssq = vf(alloc(128, 512), 32)
    rfin = vf(alloc(64, 512), 4)
    o_gp = [alloc(2048), alloc(2048)]
    gpiece = [vf(o, 512) for o in o_gp]
    cols_sb = vf(alloc(256, 256), 64).rearrange("p (c w) -> p c w", w=2)
    halo_prev = vb(alloc(1024), 512)
    o_rope = alloc(1024)
    ropec = [vf(o_rope, 64), vf(o_rope + 512, 64)]
    ropes = [vf(o_rope + 256, 64), vf(o_rope + 768, 64)]
    o_R = alloc(0, 1024)
    R_BYTES = SB_BYTES - o_R

    class Arena:
        def __init__(self):
            self.c = 0

        def a(self, nbytes, align=512):
            o = (self.c + align - 1) // align * align
            self.c = o + nbytes
            assert self.c <= R_BYTES, ("scratch overflow", self.c, R_BYTES)
            return o_R + o

    banks = [stack.enter_context(nc.psum_tensor("pb%d" % i, [128, 512], F32)) for i in range(8)]

    def bankb(i):
        return banks[i][:, :].bitcast(BF16)

    rr = {}

    def nxt(cls, lst):
        i = rr.get(cls, 0)
        rr[cls] = i + 1
        return lst[i % len(lst)]

    def dma(eng, out, in_, slot, R=None, W=None):
        S.op(eng, lambda e, o=out, i=in_: e.dma_start(out=o, in_=i), R=R if R is not None else [in_],
             W=W if W is not None else [out], dma_slot=slot + "_" + eng)

    def mm(out, lhsT, rhs, start, stop, extraR=()):
        S.op("pe", lambda e, o=out, l=lhsT, r=rhs, s=start, t=stop: e.matmul(o, l, r, start=s, stop=t),
             R=[lhsT, rhs] + list(extraR), W=[out])

    def tr(out, in_, ident):
        S.op("pe", lambda e, o=out, i=in_, d=ident: e.transpose(o, i, d), R=[in_, ident], W=[out])

    def act(out, in_, func, bias=None, scale=None, accum=None, extraR=()):
        kw = {}
        Rl = [in_] + list(extraR)
        Wl = [out]
        if bias is not None:
            kw["bias"] = bias
            if not isinstance(bias, float):
                Rl.append(bias)
        if scale is not None:
            kw["scale"] = scale
            if not isinstance(scale, float):
                Rl.append(scale)
        if accum is not None:
            kw["accum_out"] = accum
            Wl.append(accum)
        S.op("act", lambda e, o=out, i=in_, f=func, k=kw: e.activation(o, i, f, **k), R=Rl, W=Wl)

    def ts(eng, out, in0, s1, s2, op0, op1=None):
        Rl = [in0] + [s for s in (s1, s2) if s is not None and not isinstance(s, float)]
        if op1 is None:
            S.op(eng, lambda e, o=out, i=in0, a=s1, p=op0: e.tensor_scalar(o, i, a, None, p), R=Rl, W=[out])
        else:
            S.op(eng, lambda e, o=out, i=in0, a=s1, b=s2, p=op0, q=op1: e.tensor_scalar(o, i, a, b, p, q), R=Rl, W=[out])

    def tt(eng, out, in0, in1, op):
        S.op(eng, lambda e, o=out, a=in0, b=in1, p=op: e.tensor_tensor(o, a, b, p), R=[in0, in1], W=[out])

    def stt(out, in0, scalar, in1, op0, op1):
        Rl = [in0, in1] + ([scalar] if not isinstance(scalar, float) else [])
        S.op("dve", lambda e, o=out, a=in0, s=scalar, b=in1, p=op0, q=op1: e.scalar_tensor_tensor(o, a, s, b, p, q),
             R=Rl, W=[out])

    def cp(eng, out, in_):
        if eng == "act":
            S.op("act", lambda e, o=out, i=in_: e.activation(o, i, AF.Copy), R=[in_], W=[out])
        else:
            S.op(eng, lambda e, o=out, i=in_: e.tensor_copy(o, i), R=[in_], W=[out])

    def recip(out, in_):
        S.op("dve", lambda e, o=out, i=in_: e.reciprocal(o, i), R=[in_], W=[out])

    def rstd_from_ss(ss, rs, n, inv_n):
        ts("pool", rs, ss, float(inv_n), float(EPS), ALU.mult, ALU.add)
        tt("pool", rs, rs, mhalf[:, 0:n], ALU.pow)

    def ap4(v, off_el, dims):
        ps = list(v.ap)[0][0]
        return bass.AP(v.tensor, v.offset + off_el, [[ps, 128]] + [list(d) for d in dims])

    wcount = [0]

    wrec = []
    wseen = set()

    wpending = []

    def issue_w(k, ent):
        src3, eng, ckey = ent
        buf = wbuf[k % 2]
        if ckey is not None and ckey in wseen:
            lyr, idx = ckey
            dma("pool", buf.rearrange("p a b -> p (a b)"), wc_d[lyr, idx], "w%d" % (k % 2), R=[("D", "wc", lyr, idx)])
            return
        dma(eng, buf[:, :, :], src3, "w%d" % (k % 2))
        if ckey is not None:
            wseen.add(ckey)
            wpending.append((k, ckey))

    def flush_w_stores(upto):
        while wpending and wpending[0][0] <= upto:
            k, (lyr, idx) = wpending.pop(0)
            dma("sp", wc_d[lyr, idx], wbuf[k % 2].rearrange("p a b -> p (a b)"), "wst", W=[("D", "wc", lyr, idx)])

    def load_w(src3, eng="pool", ckey=None):
        k = wcount[0]
        wcount[0] += 1
        wrec.append((src3, eng, ckey))
        if wseq is None:
            dma(eng, wbuf[k % 2][:, :, :], src3, "w%d" % (k % 2))
        else:
            if k == 0:
                issue_w(0, wseq[0])
            flush_w_stores(k)
            if k + 1 < len(wseq):
                issue_w(k + 1, wseq[k + 1])
        return wbuf[k % 2]

    def wsrc(wd, l, c0, ncol=512):
        return wd[l, :, c0:c0 + ncol].rearrange("(j p) n -> p j n", p=128)

    for dst, src in ((identF, identF_d), (identB, identB_d), (onesB, onesB_d),
                     (band, band_d), (dft256, dft256_d), (ccs, ccs_d),
                     (cv, cvec), (ngc, ngcol), (psc, pscol)):
        dma("sp", dst, src, "const")
    S.op("pool", lambda e: e.memset(mhalf, -0.5), W=[mhalf])
    act(cvt, cv, AF.Tanh, scale=0.5)
    stt(cvt, cvt, 1.0, cv, ALU.add, ALU.mult)
    ts("dve", sc3.rearrange("p a b -> p (a b)"), cvt, 0.5, None, ALU.mult)

    xsrc = {("lat", 0): x_in, ("ctx", 0): ctx_in, ("lat", 1): x1_d, ("ctx", 1): xc1_d}
    xdst = {("lat", 0): x1_d, ("ctx", 0): xc1_d, ("lat", 1): out_d}

    def xkey(tensor, tile, cb):
        return ("D", tensor.tensor.name, tile, cb)

    def xkeys(tensor, tile):
        return [xkey(tensor, tile, cb) for cb in range(4)]

    def ada_block(l, nb, mp, abp, par):
        wv = load_w(wsrc(ada_w, l, nb * 512))
        dma("sp", abp, ada_b2[l, :, nb * 512:(nb + 1) * 512], "adab%d" % par)
        acc = banks[0][0:2, :]
        for j in range(16):
            mm(acc, sc3[:, j, :], wv[:, j, :], j == 0, j == 15)
        tt("dve", mp, acc, abp, ALU.add)
        dma("sp", modrow_d[l, :, nb * 512:(nb + 1) * 512], mp, "modrow", W=[("D", "modrow", l, nb)])
        if nb < 8:
            cps = banks[1][:, 0:8].rearrange("p (c w) -> p c w", w=2)
            for q in range(4):
                tr(cps[:, q, :], mp[:, q * 128:(q + 1) * 128], identF[0:2, 0:2])
            cp("dve", cols_sb[:, nb * 4:(nb + 1) * 4, :], cps)

    def ada_finish(l):
        for which in range(2):
            cp("dve", mcols[:, which, 1, :], cols_sb[:, 0:16, which])
            stt(mcols[:, which, 0, :], cols_sb[:, 16:32, which], 1.0, ngc[:, l * 16:(l + 1) * 16], ALU.add, ALU.mult)

    deferred = []
    try:
      for l in range(DEPTH):
          last = l == DEPTH - 1
          dma("pool", poolw, pool_w[l].rearrange("g c d -> c g d"), "lw")
          dma("pool", fourw, four_w[l].rearrange("g c d -> c g d"), "lw")
          dma("sp", gains_s, gains[l], "lw2")

          if l == 0:
              ar = Arena()
              o_mp = [ar.a(2048), ar.a(2048)]
              o_ab = [ar.a(2048), ar.a(2048)]
              for nb in range(8):
                  ada_block(0, nb, vf(o_mp[nb % 2], 512)[0:2, :], vf(o_ab[nb % 2], 512)[0:2, :], nb % 2)
          ada_finish(l)

          mark('ada%d' % l)
          ar = Arena()
          o_xt = [ar.a(8192), ar.a(8192)]
          o_junk = ar.a(4096)
          junk = vb(o_junk, 2048)
          o_mp = [ar.a(2048), ar.a(2048)]
          o_ab = [ar.a(2048), ar.a(2048)]
          seqs = [("ctx", t) for t in range(2)] + [("lat", t) for t in range(16)]
          def p1_a(ti):
              which, t = seqs[ti]
              src = xsrc[(which, l)]
              xt = vf(o_xt[ti % 2], 2048)
              dma("sp", xt, src[t * 128:(t + 1) * 128, :], "xt%d" % (ti % 2),
                  R=xkeys(src, t) if l > 0 else [])
              ss = ssb[:, 12 + ti % 2:13 + ti % 2]
              act(junk, xt, AF.Square, accum=ss)
              rs = rsb[:, 12 + ti % 2:13 + ti % 2]
              rstd_from_ss(ss, rs, 1, 1.0 / D)
              ts("dve", xt, xt, rs, None, ALU.mult)

          def p1_b(ti):
              which, t = seqs[ti]
              wi = 0 if which == "lat" else 1
              xt = vf(o_xt[ti % 2], 2048)
              gt = ti
              pbs = [0, 1, 2, 3] if ti % 2 == 0 else [4, 5, 6, 7]
              for jb in range(4):
                  for j in range(jb * 4, jb * 4 + 4):
                      pb = banks[pbs[jb]][:, (j % 4) * 128:(j % 4 + 1) * 128]
                      tr(pb, xt[:, j * 128:(j + 1) * 128], identF)
                  for j in range(jb * 4, jb * 4 + 4):
                      pb = banks[pbs[jb]][:, (j % 4) * 128:(j % 4 + 1) * 128]
                      dst = hT[:, j, gt * 128:(gt + 1) * 128]
                      if jb % 2 == 0:
                          act(dst, pb, AF.Identity, bias=mcols[:, wi, 1, j:j + 1], scale=mcols[:, wi, 0, j:j + 1])
                      else:
                          ts("dve", dst, pb, mcols[:, wi, 0, j:j + 1], mcols[:, wi, 1, j:j + 1], ALU.mult, ALU.add)

          p1_a(0)
          for ti in range(len(seqs)):
              if ti + 1 < len(seqs):
                  p1_a(ti + 1)
              p1_b(ti)
              if ti % 4 == 3 and ti // 4 < 4:
                  nb = 8 + ti // 4
                  ada_block(l, nb, vf(o_mp[nb % 2], 512)[0:2, :], vf(o_ab[nb % 2], 512)[0:2, :], nb % 2)

          mark('p1_%d' % l)
          ar = Arena()
          o_kx = ar.a(1024)
          o_ta = ar.a(512)
          o_tb = ar.a(512)
          o_kr = [ar.a(512), ar.a(512)]
          junk = vb(ar.a(512), 256)
          wv = load_w(wsrc(w_in, l, OFF_K))
          def kv_mm(gt):
              pb = banks[gt % 2]
              for j in range(16):
                  mm(pb[:, :], hT[:, j, gt * 128:(gt + 1) * 128], wv[:, j, :], j == 0, j == 15)

          kv_mm(0)
          for gt in range(18):
              is_lat = gt >= 2
              pb = banks[gt % 2]
              if gt + 1 < 18:
                  kv_mm(gt + 1)
              so = 2 * (gt % 2)
              ss = ssb[:, so:so + 2]
              rs = rsb[:, so:so + 2]
              for h in range(2):
                  act(junk[:, 0:128], pb[:, h * 128:(h + 1) * 128], AF.Square, accum=ssb[:, so + h:so + h + 1])
              rstd_from_ss(ss, rs, 2, 1.0 / 128)
              kr = vb(o_kr[gt % 2], 256)
              kx = vf(o_kx, 256)
              if is_lat:
                  t = gt - 2
                  rc, rsn = ropec[gt % 2], ropes[gt % 2]
                  dma("sp", rc, ropeC[t * 128:(t + 1) * 128, :], "rope%d" % (gt % 2))
                  dma("sp", rsn, ropeS[t * 128:(t + 1) * 128, :], "rope%d" % (gt % 2))
              for h in range(2):
                  stt(kx[:, h * 128:(h + 1) * 128] if is_lat else kr[:, h * 128:(h + 1) * 128],
                      pb[:, h * 128:(h + 1) * 128], rsb[:, so + h:so + h + 1], gains_s[:, 128:256], ALU.mult, ALU.mult)
              cp("act", Vv[:, gt, :], pb[:, 256:512])
              if is_lat:
                  rope(S, tt, ap4, kx, kr, vf(o_ta, 128), vf(o_tb, 128), rc, rsn, 2)
              kb = nxt("B", [2, 3])
              pbt = bankb(kb)
              for h in range(2):
                  tr(pbt[:, h * 128:(h + 1) * 128], kr[:, h * 128:(h + 1) * 128], identB)
              for h in range(2):
                  cp("dve", KT[:, h, gt * 128:(gt + 1) * 128], pbt[:, h * 128:(h + 1) * 128])

          mark('g1_%d' % l)
          ar = Arena()
          o_tm = ar.a(18 * 512 * 2)
          tm = r3(vb(o_tm, 18 * 512), 512)
          o_pT2 = ar.a(2 * 4 * 512 * 2)
          pT2 = vb(o_pT2, 4096).rearrange("p (m g k) -> p m g k", m=2, g=4)
          wv = load_w(wsrc(w_in, l, OFF_FOUR))
          ftiles = list(range(18)) if not last else list(range(2, 18))
          for gt in ftiles:
              pb = banks[nxt("A", [0, 1])]
              for j in range(16):
                  mm(pb[:, :], hT[:, j, gt * 128:(gt + 1) * 128], wv[:, j, :], j == 0, j == 15)
              cp("act" if gt % 2 == 0 else "dve", tm[:, gt, :], pb[:, :])
          if not last:
              for mat in range(2):
                  for g in range(4):
                      pb = banks[nxt("A", [0, 1])]
                      for j in range(2):
                          mm(pb[:, 0:256], tm[:, j, g * 128:(g + 1) * 128], dft256[:, mat, j, :], j == 0, j == 1)
                      cp("act" if g % 2 == 0 else "dve", pT2[:, mat, g, 0:256], pb[:, 0:256])
              for g in range(4):
                  pb = banks[nxt("C", [4, 5])]
                  mm(pb[:, 0:256], ccs[:, 2, :], pT2[:, 0, g, 0:256], True, False)
                  mm(pb[:, 0:256], ccs[:, 3, :], pT2[:, 1, g, 0:256], False, True)
                  cp("act" if g % 2 == 0 else "dve", ReT[:, g, 0:256], pb[:, 0:256])
          for kb in range(4):
              for mat, dsrc in enumerate((dftC, dftS)):
                  dv = load_w(dsrc[:, kb * 512:(kb + 1) * 512].rearrange("(j p) k -> p j k", p=128), eng="pool")
                  for g in range(4):
                      pb = banks[nxt("A", [0, 1, 2, 3])]
                      for j in range(16):
                          mm(pb[:, :], tm[:, 2 + j, g * 128:(g + 1) * 128], dv[:, j, :], j == 0, j == 15)
                      cp("act" if g % 2 == 0 else "dve", pT2[:, mat, g, :], pb[:, :])
              for g in range(4):
                  pb = banks[nxt("C", [4, 5])]
                  mm(pb[:, :], ccs[:, 0, :], pT2[:, 0, g, :], True, False)
                  mm(pb[:, :], ccs[:, 1, :], pT2[:, 1, g, :], False, True)
                  cp("act" if g % 2 == 0 else "dve", ReT[:, g, 256 + kb * 512:256 + (kb + 1) * 512], pb[:, :])

          mark('g3_%d' % l)
          groups = ([] if last else [("ctx", 0, NCTX, 0)]) + [("lat", 256 + g * 512, 512, g * 4) for g in range(4)]
          for (which, tok0, ntok, T0) in groups:
              wi = 0 if which == "lat" else 1
              is_lat = which == "lat"
              NT = 16 if is_lat else 2
              ntile = ntok // 128
              ktiles = list(range(18)) if is_lat else [0, 1]
              src_x = xsrc[(which, l)]
              dst_x = xdst[(which, l)]
              ar = Arena()
              o_q = ar.a(6 * 1024, 1024)
              QT = r3(vb(o_q, 4 * ntok), ntok)
              qx = vf(o_q + 4096, 512)
              up_tm = r3(vb(o_q, 6 * 512), 512)
              mgT = r3(vb(ar.a(16 * ntok * 2), 16 * ntok), ntok)
              o_PT = [ar.a(ntok * 2) for _ in range(3)]
              PT = [vb(o, ntok) for o in o_PT]
              tmpAf = [vf(ar.a(2048), 512) for _ in range(2)]
              tmpA = [v[:, 0:ntok] for v in tmpAf]
              o_tO = [ar.a(2048) for _ in range(2)]
              tmpO = [vf(o, 512)[:, 0:ntok] for o in o_tO]
              o_qr = [ar.a(1024) for _ in range(2)]
              qr = [vb(o, 512) for o in o_qr]
              o_ta = ar.a(1024)
              o_tb = ar.a(1024)
              ta = vf(o_ta, 256)
              tb = vf(o_tb, 256)
              x2full = x2hole = None
              if last:
                  assert o_PT[1] == o_PT[0] + 1024 and o_qr[1] == o_qr[0] + 1024 and o_tb == o_ta + 1024
                  x2full = [vf(o_q, 512), vf(o_q + 2048, 512), vf(o_q + 4096, 512), vf(o_tO[0], 512), vf(o_tO[1], 512),
                            vf(o_qr[0], 512), vf(o_ta, 512), vf(o_PT[0], 512)]
                  x2hole = [vf(o_hT + (j * NTOK + tok0) * 2, 256) for j in range(16)]
              plT = [vb(ar.a(ntok * 2), ntok)] * 2
              xp = [vf(ar.a(2048), 512) for _ in range(2)]

              def gate_chunk(fc, wg, branch, branch_in_psum):
                  gb = banks[nxt("B", [2, 3])]
                  hcol = (fc % 4) * 128
                  for j in range(16):
                      mm(gb[:, 0:ntok], wg[:, j, hcol:hcol + 128], hT[:, j, tok0:tok0 + ntok], j == 0, j == 15)
                  th = tmpA[fc % 2]
                  act(th, gb[:, 0:ntok], AF.Tanh, scale=0.5)
                  stt(th, th, 1.0, gb[:, 0:ntok], ALU.add, ALU.mult)
                  stt(mgT[:, fc, :], th, 0.5, branch, ALU.mult, ALU.mult)

              for hb in range(2):
                  wq = load_w(wsrc(w_in, l, OFF_Q + hb * 512), ckey=(l, hb))
                  def q_mm(tti):
                      pb = banks[tti % 2]
                      c0 = tok0 + tti * 128
                      for j in range(16):
                          mm(pb[:, :], hT[:, j, c0:c0 + 128], wq[:, j, :], j == 0, j == 15)

                  q_mm(0)
                  for tti in range(ntile):
                      pb = banks[tti % 2]
                      if tti + 1 < ntile:
                          q_mm(tti + 1)
                      so = 4 + 4 * (tti % 2)
                      for h in range(4):
                          act(PT[2][:, 0:128], pb[:, h * 128:(h + 1) * 128], AF.Square, accum=ssb[:, so + h:so + h + 1])
                      mark('q1')
                      rstd_from_ss(ssb[:, so:so + 4], rsb[:, so:so + 4], 4, 1.0 / 128)
                      mark('q2')
                      qrv = qr[tti % 2]
                      if is_lat:
                          t = T0 + tti
                          rc, rsn = ropec[tti % 2], ropes[tti % 2]
                          dma("sp", rc, ropeC[t * 128:(t + 1) * 128, :], "rope%d" % (tti % 2))
                          dma("sp", rsn, ropeS[t * 128:(t + 1) * 128, :], "rope%d" % (tti % 2))
                      for h in range(4):
                          stt(qx[:, h * 128:(h + 1) * 128] if is_lat else qrv[:, h * 128:(h + 1) * 128],
                              pb[:, h * 128:(h + 1) * 128], rsb[:, so + h:so + h + 1], gains_s[:, 0:128], ALU.mult, ALU.mult)
                      if is_lat:
                          rope(S, tt, ap4, qx, qrv, ta, tb, rc, rsn, 4)
                      mark('q3')
                      pbt = bankb(nxt("B", [2, 3]))
                      for h in range(4):
                          tr(pbt[:, h * 128:(h + 1) * 128], qrv[:, h * 128:(h + 1) * 128], identB)
                      mark('q4')
                      cp("dve" if tti % 2 == 0 else "act", QT[:, :, tti * 128:(tti + 1) * 128],
                         pbt[:, 0:512].rearrange("p (h d) -> p h d", h=4))
                      mark('q5')
                  mark('qdone')
                  wg = load_w(wsrc(w_in, l, OFF_GATE + hb * 512), ckey=(l, 2 + hb))
                  for h in range(4):
                      fc = hb * 4 + h
                      Ob, Lb = banks[6], banks[7]
                      nk = len(ktiles)

                      def s_mm(ki):
                          kt = ktiles[ki]
                          sb = banks[(4, 5, 0, 1)[ki % 4]]
                          mm(sb[:, 0:ntok], KT[:, hb, kt * 128:(kt + 1) * 128], QT[:, h, :], True, True)
                          act(PT[ki % 3], sb[:, 0:ntok], AF.Exp, scale=float(128.0 ** -0.5))

                      def pv_mm(ki):
                          kt = ktiles[ki]
                          mm(Ob[:, 0:ntok], Vv[:, kt, hb * 128:(hb + 1) * 128], PT[ki % 3], ki == 0, ki == nk - 1)
                          mm(Lb[:, 0:ntok], onesB, PT[ki % 3], ki == 0, ki == nk - 1)

                      s_mm(0)
                      if nk > 1:
                          s_mm(1)
                      for ki in range(nk):
                          if ki + 2 < nk:
                              s_mm(ki + 2)
                          pv_mm(ki)
                      mark('attA')
                      tO = tmpO[fc % 2]
                      recip(tO, Lb[:, 0:ntok])
                      tt("dve", tO, Ob[:, 0:ntok], tO, ALU.mult)
                      mark('attB')
                      gate_chunk(fc, wg, tO, False)
                      mark('attC')
                  if l == 0 and is_lat:
                      nb = T0 // 2 + hb
                      ada_block(1, nb, gpiece[0][0:2, :], gpiece[1][0:2, :], nb % 2)

              mark('att_%d_%s_%d' % (l, which, T0))
              while deferred:
                  deferred.pop(0)()
              wp = load_w(wsrc(w_in, l, OFF_POOL), ckey=(l, 6))
              Tlo = max(T0 - 1, 0)
              Thi = min(T0 + ntile, NT - 1)
              seq_tok0 = 256 if is_lat else 0
              use_halo = last and T0 > 0
              for T in range(Tlo, Thi + 1):
                  if use_halo and T == T0 - 1:
                      continue
                  pb = banks[nxt("A", [0, 1])]
                  c0 = seq_tok0 + T * 128
                  for j in range(16):
                      mm(pb[:, :], hT[:, j, c0:c0 + 128], wp[:, j, :], j == 0, j == 15)
                  cp("act" if T % 2 == 0 else "dve", up_tm[:, T - Tlo, :], pb[:, :])

              def up_src(Tn, g):
                  if use_halo and Tn == T0 - 1:
                      return halo_prev[:, g * 128:(g + 1) * 128]
                  return up_tm[:, Tn - Tlo, g * 128:(g + 1) * 128]

              wg = load_w(wsrc(w_in, l, OFF_GATE + 2 * 512), ckey=(l, 4))
              for g in range(4):
                  fc = 8 + g
                  pb = banks[nxt("C", [4, 5])]
                  for tti in range(ntile):
                      T = T0 + tti
                      terms = []
                      if T > 0:
                          terms.append((T - 1, 3))
                      terms.append((T, 1 if T == 0 else (2 if T == NT - 1 else 0)))
                      if T < NT - 1:
                          terms.append((T + 1, 4))
                      for i, (Tn, kind) in enumerate(terms):
                          mm(pb[:, tti * 128:(tti + 1) * 128], up_src(Tn, g),
                             band[:, g * 5 + kind, :], i == 0, i == len(terms) - 1)
                  pl = plT[g % 2]
                  cp("act", pl, pb[:, 0:ntok])
                  yb = banks[nxt("D", [6, 7])]
                  mm(yb[:, 0:ntok], poolw[:, g, :], pl, True, True)
                  tO = tmpO[fc % 2]
                  ts("dve", tO, yb[:, 0:ntok], psc[:, l * 4 + g:l * 4 + g + 1], None, ALU.mult)
                  gate_chunk(fc, wg, tO, False)

              if last and T0 + ntile < NT:
                  cp("act", halo_prev, up_tm[:, (T0 + ntile - 1) - Tlo, :])
              wg = load_w(wsrc(w_in, l, OFF_GATE + 3 * 512), ckey=(l, 5))
              for g in range(4):
                  fc = 12 + g
                  yb = banks[nxt("D", [6, 7])]
                  mm(yb[:, 0:ntok], fourw[:, g, :], ReT[:, g, tok0:tok0 + ntok], True, True)
                  gate_chunk(fc, wg, yb[:, 0:ntok], True)

              mark('four_%d_%s_%d' % (l, which, T0))
              gate_c0 = 2 * D
              pieces = [(cb, tti) for cb in range(4) for tti in range(ntile)]

              def gp_load(cb):
                  dma("sp", gpiece[cb % 2],
                      modrow_d[l, wi:wi + 1, gate_c0 + cb * 512:gate_c0 + (cb + 1) * 512].partition_broadcast(128).rearrange("p a n -> p (a n)"),
                      "gp%d" % (cb % 2), R=[("D", "modrow", l, 8 + cb)])

              def x_load(i):
                  cb, tti = pieces[i]
                  T = T0 + tti
                  dma("sp", xp[i % 2], src_x[T * 128:(T + 1) * 128, cb * 512:(cb + 1) * 512], "xp%d" % (i % 2),
                      R=[xkey(src_x, T, cb)] if l > 0 else [])

              gp_load(0)
              x_load(0)
              wo = None
              for i, (cb, tti) in enumerate(pieces):
                  T = T0 + tti
                  if tti == 0:
                      wo = load_w(wsrc(w_out, l, cb * 512), ckey=(l, 7 + cb))
                      if cb + 1 < 4:
                          gp_load(cb + 1)
                  if i + 1 < len(pieces):
                      x_load(i + 1)
                  gp = gpiece[cb % 2]
                  xpv = xp[i % 2]
                  pb = banks[nxt("W", [0, 1, 4, 5])]
                  for fc in range(16):
                      mm(pb[:, :], mgT[:, fc, tti * 128:(tti + 1) * 128], wo[:, fc, :], fc == 0, fc == 15)
                  tmpx = tmpAf[tti % 2]
                  tt("dve", tmpx, pb[:, :], gp, ALU.mult)
                  if not last:
                      tt("dve", xpv, tmpx, xpv, ALU.add)
                      dma("sp", dst_x[T * 128:(T + 1) * 128, cb * 512:(cb + 1) * 512], xpv, "xo%d" % (i % 2),
                          W=[xkey(dst_x, T, cb)])
                  else:
                      for hf in range(2):
                          if cb < 2:
                              dest = x2full[tti * 2 + cb][:, hf * 256:(hf + 1) * 256]
                          else:
                              dest = x2hole[(tti * 2 + cb - 2) * 2 + hf]
                          tt("dve", dest, tmpx[:, hf * 256:(hf + 1) * 256], xpv[:, hf * 256:(hf + 1) * 256], ALU.add)
                          act(tmpAf[(tti + 1) % 2][:, hf * 256:(hf + 1) * 256], dest, AF.Square,
                              accum=ssq[:, tti * 8 + cb * 2 + hf:tti * 8 + cb * 2 + hf + 1])
              mark('wout_%d_%s_%d' % (l, which, T0))
              if last:
                  for tti in range(ntile):
                      S.op("dve", lambda e, o=rfin[:, tti:tti + 1], i=ssq[:, tti * 8:(tti + 1) * 8]:
                           e.tensor_reduce(o, i, mybir.AxisListType.X, ALU.add),
                           R=[ssq[:, tti * 8:(tti + 1) * 8]], W=[rfin[:, tti:tti + 1]])
                      rstd_from_ss(rfin[:, tti:tti + 1], rfin[:, tti:tti + 1], 1, 1.0 / D)
                  for cb in range(4):
                      fgv = gpiece[cb % 2]
                      dma("sp", fgv, fngb[:, cb * 512:(cb + 1) * 512], "gp%d" % (cb % 2))
                      for tti in range(ntile):
                          T = T0 + tti
                          for hf in range(2):
                              if cb < 2:
                                  srcv = x2full[tti * 2 + cb][:, hf * 256:(hf + 1) * 256]
                              else:
                                  srcv = x2hole[(tti * 2 + cb - 2) * 2 + hf]
                              stt(srcv, srcv, rfin[:, tti:tti + 1], fgv[:, hf * 256:(hf + 1) * 256], ALU.mult, ALU.mult)
                              if cb >= 2:
                                  c0 = cb * 512 + hf * 256
                                  dma("sp", out_d[T * 128:(T + 1) * 128, c0:c0 + 256], srcv, "xo%d" % hf,
                                      W=[("D", out_d.tensor.name, T, cb, hf)])
                          if cb < 2:
                              dma("sp", out_d[T * 128:(T + 1) * 128, cb * 512:(cb + 1) * 512], x2full[tti * 2 + cb],
                                  "xo%d" % (tti % 2), W=[xkey(out_d, T, cb)])
      while deferred:
          deferred.pop(0)()

    except _Stop:
        pass

    allout = [xkey(out_d, T, cb) for T in range(16) for cb in range(2)]
    allout += [("D", out_d.tensor.name, T, cb, hf) for T in range(16) for cb in (2, 3) for hf in range(2)]
    if debug:
        allout += [xkey(x1_d, T, cb) for T in range(16) for cb in range(4)]
        allout += [xkey(xc1_d, T, cb) for T in range(2) for cb in range(4)]
    S.op("sp", lambda e: e.nop(), R=allout)
    S.emit(nc, stack)
    stack.close()
    nc._wrec = wrec
    nc._marks = marks
    return nc


def rope(S, tt, ap4, xin, xout, ta, tb, rc, rsn, H):
    dims = [[128, H], [64, 2], [1, 32]]
    x1 = ap4(xin, 0, dims)
    x2 = ap4(xin, 32, dims)
    o1 = ap4(xout, 0, dims)
    o2 = ap4(xout, 32, dims)
    tdims = [[64, H], [32, 2], [1, 32]]
    a = ap4(ta, 0, tdims)
    b = ap4(tb, 0, tdims)
    cdims = [[0, H], [32, 2], [1, 32]]
    c = ap4(rc, 0, cdims)
    s = ap4(rsn, 0, cdims)
    tt("dve", a, x1, c, ALU.mult)
    tt("dve", b, x2, s, ALU.mult)
    tt("dve", o1, a, b, ALU.subtract)
    tt("dve", a, x2, c, ALU.mult)
    tt("dve", b, x1, s, ALU.mult)
    tt("dve", o2, a, b, ALU.add)


def build_two_pass(debug=False, stop_after=None):
    rec = build_program(debug=debug, stop_after=stop_after)._wrec
    return build_program(debug=debug, stop_after=stop_after, wseq=rec)


_NC_CACHE = {}


def make_in_maps(x, c, ctx, c_ctx, ada_w, ada_b, norm_g, w_in, q_norm_g, k_norm_g,
                 pool_w, pool_scale, fourier_w, w_out, final_norm_g, cores):
    f = lambda a: np.ascontiguousarray(np.asarray(a, dtype=np.float32))
    x, c, ctx, c_ctx = f(x), f(c), f(ctx), f(c_ctx)
    ada_w, ada_b, norm_g, w_in = f(ada_w), f(ada_b), f(norm_g), f(w_in)
    q_norm_g, k_norm_g, pool_w, pool_scale = f(q_norm_g), f(k_norm_g), f(pool_w), f(pool_scale)
    fourier_w, w_out, final_norm_g = f(fourier_w), f(w_out), f(final_norm_g)
    consts = get_consts()
    shared = dict(consts)
    shared["ada_w"] = ada_w
    shared["ada_b2"] = np.ascontiguousarray(np.repeat(ada_b[:, None, :], 2, axis=1))
    shared["ngcol"] = np.ascontiguousarray(norm_g.reshape(DEPTH, 16, 128).transpose(2, 0, 1).reshape(128, 32))
    shared["fngb"] = np.ascontiguousarray(np.broadcast_to(final_norm_g[None, :], (128, D)))
    shared["w_in"] = w_in
    shared["w_out"] = w_out
    g = np.concatenate([q_norm_g, k_norm_g], axis=1)
    shared["gains"] = np.ascontiguousarray(np.broadcast_to(g[:, None, :], (DEPTH, 128, 256)))
    shared["pool_w"] = pool_w
    shared["pscol"] = np.ascontiguousarray(pool_scale.reshape(DEPTH, 4, 128).transpose(2, 0, 1).reshape(128, 8))
    shared["fourier_w"] = fourier_w
    cc = c_ctx.reshape(16, 128).T
    maps = []
    for b in cores:
        m = dict(shared)
        m["x"] = x[b]
        m["ctx"] = ctx[b]
        cb = c[b].reshape(16, 128).T
        m["cvec"] = np.ascontiguousarray(np.stack([cb, cc], axis=2).reshape(128, 32))
        maps.append(m)
    return maps


def kernel(x, c, ctx, c_ctx, ada_w, ada_b, norm_g, w_in, q_norm_g, k_norm_g,
           pool_w, pool_scale, fourier_w, w_out, final_norm_g):
    if "nc" not in _NC_CACHE:
        _NC_CACHE["nc"] = build_two_pass(debug=False)
    nc = _NC_CACHE["nc"]
    maps = make_in_maps(x, c, ctx, c_ctx, ada_w, ada_b, norm_g, w_in, q_norm_g, k_norm_g,
                        pool_w, pool_scale, fourier_w, w_out, final_norm_g, list(range(8)))
    res = run_bass_kernel_spmd(nc, maps, core_ids=list(range(8)))
    out = np.stack([np.asarray(r["out"], dtype=np.float32) for r in res.results], axis=0)
    return out
```
